# Optimizing a Trainium2 kernel written in Bass

```python
import jax, jax.numpy as jnp
from jax import lax
import numpy as np

D_MODEL = 1024
BATCH = 4
SEQ = 4096
DEPTH = 2

GRID_W = 64
CTX_LEN = 256

N_MIXERS = 2
N_A = (DEPTH + N_MIXERS - 1) // N_MIXERS
N_B = DEPTH // N_MIXERS
LAST_CTX_READER = (DEPTH - 1) - ((DEPTH - 1) % N_MIXERS)

ALPHA = (2 * DEPTH) ** 0.25
BETA = (8 * DEPTH) ** -0.25

GLA_HEADS = 4
GLA_DK = D_MODEL // 2
GLA_DV = D_MODEL
GLA_DK_HEAD = GLA_DK // GLA_HEADS
GLA_DV_HEAD = GLA_DV // GLA_HEADS
GLA_RANK = 16
GLA_GATE_NORM = 16.0
GLA_CHUNK = GRID_W
GLA_Q_SCALE = GLA_DK_HEAD ** -0.5
GLA_CTX_COLS = GLA_DK + GLA_DV + 2 * GLA_RANK
GLA_IN = GLA_CTX_COLS + GLA_DK + GLA_DV

GM_WIDTH = 3 * D_MODEL
GM_GROUPS = 4
GM_CHUNK = 128
GM_ROWS_PER_CHUNK = GM_CHUNK // GRID_W

FFN_HIDDEN = -(-8 * D_MODEL // (3 * 256)) * 256

kernel_name = 'hybrid_gla_gmlp_prefix_dit'


def layer_norm(x, g, b, eps=1e-5):
    xf = x.astype(jnp.float32)
    mu = xf.mean(-1, keepdims=True)
    var = jnp.square(xf - mu).mean(-1, keepdims=True)
    return ((xf - mu) * lax.rsqrt(var + eps) * g.astype(jnp.float32) + b.astype(jnp.float32)).astype(x.dtype)


def rms_norm_f32(x, g, eps=1e-6):
    return x * lax.rsqrt(jnp.mean(jnp.square(x), -1, keepdims=True) + eps) * g.astype(jnp.float32)


def adaln(cond, w, b):
    m = (jax.nn.silu(cond) @ w + b)[..., None, :]
    return jnp.split(m, 6, axis=-1)


def modulate(h, shift, scale):
    return h * (1.0 + scale) + shift


def flip(t):
    return jnp.flip(t, axis=1)


def gla_log_decay(a_lr, w_dec, b_dec):
    z = (a_lr @ w_dec + b_dec).astype(jnp.float32)
    g = jax.nn.log_sigmoid(z) / GLA_GATE_NORM
    return g.reshape(*g.shape[:-1], GLA_HEADS, GLA_DK_HEAD)


def gla_split_kva(p, w_dec, b_dec):
    B, L = p.shape[:2]
    o_v = GLA_DK
    o_a = GLA_DK + GLA_DV
    k = p[..., :o_v].astype(jnp.float32).reshape(B, L, GLA_HEADS, GLA_DK_HEAD)
    v = p[..., o_v:o_a].astype(jnp.float32).reshape(B, L, GLA_HEADS, GLA_DV_HEAD)
    g_f = gla_log_decay(p[..., o_a:o_a + GLA_RANK], w_dec[0], b_dec[0])
    g_b = gla_log_decay(p[..., o_a + GLA_RANK:GLA_CTX_COLS], w_dec[1], b_dec[1])
    return k, v, g_f, g_b


def gla_split_qr(p):
    B, L = p.shape[:2]
    q = p[..., GLA_CTX_COLS:GLA_CTX_COLS + GLA_DK].astype(jnp.float32)
    q = q.reshape(B, L, GLA_HEADS, GLA_DK_HEAD) * GLA_Q_SCALE
    r = p[..., GLA_CTX_COLS + GLA_DK:]
    return q, r


def to_chunks(t):
    B, L, H, d = t.shape
    return t.reshape(B, L // GLA_CHUNK, GLA_CHUNK, H, d).transpose(1, 0, 3, 2, 4)


def from_chunks(t):
    n, B, H, C, d = t.shape
    return t.transpose(1, 0, 3, 2, 4).reshape(B, n * C, H, d)


def gla_chunked(q, k, v, g, s0):
    qc, kc, vc, gc = to_chunks(q), to_chunks(k), to_chunks(v), to_chunks(g)
    b = jnp.cumsum(gc, axis=3)
    b_last = b[..., -1:, :]
    q_in = qc * jnp.exp(b)
    k_in = kc * jnp.exp(-b)
    k_st = kc * jnp.exp(b_last - b)
    mask = jnp.tril(jnp.ones((GLA_CHUNK, GLA_CHUNK), dtype=bool))
    att = jnp.where(mask, jnp.einsum('nbhik,nbhjk->nbhij', q_in, k_in), 0.0)
    o_intra = jnp.einsum('nbhij,nbhjv->nbhiv', att, vc)

    def step(S, inp):
        qi, ki, vi, dl = inp
        o = jnp.einsum('bhik,bhkv->bhiv', qi, S)
        S = S * jnp.exp(dl)[..., 0, :, None] + jnp.einsum('bhik,bhiv->bhkv', ki, vi)
        return S, o

    s_fin, o_inter = lax.scan(step, s0, (q_in, k_st, vc, b_last))
    return from_chunks(o_intra + o_inter), s_fin


def gla_final_state(k, v, g):
    b = jnp.cumsum(g, axis=1)
    return jnp.einsum('blhk,blhv->bhkv', k * jnp.exp(b[:, -1:] - b), v)


def gla_output(o, r, norm_g, w_out):
    B, L = o.shape[:2]
    o = rms_norm_f32(o, norm_g).reshape(B, L, GLA_DV)
    return (o * jax.nn.silu(r.astype(jnp.float32))).astype(r.dtype) @ w_out


def gla_mixer(a, ac, ctx_out, w_in, w_dec, b_dec, norm_g, w_out):
    B = a.shape[0]
    zeros = jnp.zeros((B, GLA_HEADS, GLA_DK_HEAD, GLA_DV_HEAD), jnp.float32)
    if ctx_out:
        pc = ac @ w_in
        kc, vc, gcf, gcb = gla_split_kva(pc, w_dec, b_dec)
        qc, rc = gla_split_qr(pc)
        o_cf, s_cf = gla_chunked(qc, kc, vc, gcf, zeros)
        o_cb, s_cb = gla_chunked(flip(qc), flip(kc), flip(vc), flip(gcb), zeros)
        yc = gla_output(o_cf + flip(o_cb), rc, norm_g, w_out)
    else:
        pc = ac @ w_in[:, :GLA_CTX_COLS]
        kc, vc, gcf, gcb = gla_split_kva(pc, w_dec, b_dec)
        s_cf = gla_final_state(kc, vc, gcf)
        s_cb = gla_final_state(flip(kc), flip(vc), flip(gcb))
        yc = None
    p = a @ w_in
    k, v, g_f, g_b = gla_split_kva(p, w_dec, b_dec)
    q, r = gla_split_qr(p)
    o_f, _ = gla_chunked(q, k, v, g_f, s_cf)
    o_b, _ = gla_chunked(flip(q), flip(k), flip(v), flip(g_b), s_cb)
    y = gla_output(o_f + flip(o_b), r, norm_g, w_out)
    return y, yc


def gmlp_chunk_mixer(h, n_chunks, w_in, ln_g, ln_b, w_s, b_s, w_out):
    B = h.shape[0]
    z = jax.nn.gelu(h @ w_in, approximate=False)
    u, v = jnp.split(z, 2, axis=-1)
    v = layer_norm(v, ln_g, ln_b)
    v = v.reshape(B, n_chunks, GM_CHUNK, GM_GROUPS, GM_WIDTH // GM_GROUPS)
    s = jnp.einsum('gpq,bnqgc->bnpgc', w_s, v) + b_s[:, :, None]
    return (u * s.reshape(u.shape)) @ w_out


def swiglu(h, w_gate, w_up, w_down):
    return (jax.nn.silu(h @ w_gate) * (h @ w_up)) @ w_down


def setup_inputs(seed: int = 0) -> dict:
    key = jax.random.key(seed)
    ks = jax.random.split(key, 24)
    D = D_MODEL

    def nrm(k, shape, s):
        return jax.random.normal(k, shape, jnp.float32) * s

    return {
        'x': nrm(ks[0], (BATCH, SEQ, D), 1.0),
        'c': nrm(ks[1], (BATCH, D), 1.0),
        'ctx': nrm(ks[2], (BATCH, CTX_LEN, D), 1.0),
        'c_ctx': nrm(ks[3], (D,), 1.0),
        'mod_w': nrm(ks[4], (DEPTH, D, 6 * D), 0.5 * D ** -0.5),
        'mod_b': nrm(ks[5], (DEPTH, 6 * D), 0.02),
        'ln_g': 1.0 + nrm(ks[6], (DEPTH, 2, D), 0.02),
        'ln_b': nrm(ks[7], (DEPTH, 2, D), 0.02),
        'gla_w_in': nrm(ks[8], (N_A, D, GLA_IN), D ** -0.5),
        'gla_w_decay': nrm(ks[9], (N_A, 2, GLA_RANK, GLA_DK), GLA_RANK ** -0.5),
        'gla_b_decay': 1.0 + nrm(ks[10], (N_A, 2, GLA_DK), 0.1),
        'gla_norm_g': 1.0 + nrm(ks[11], (N_A, GLA_DV_HEAD), 0.02),
        'gla_w_out': nrm(ks[12], (N_A, GLA_DV, D), BETA * GLA_DV ** -0.5),
        'gm_w_in': nrm(ks[13], (N_B, D, 2 * GM_WIDTH), D ** -0.5),
        'gm_ln_g': 1.0 + nrm(ks[14], (N_B, GM_WIDTH), 0.02),
        'gm_ln_b': nrm(ks[15], (N_B, GM_WIDTH), 0.02),
        'gm_w_s': nrm(ks[16], (N_B, GM_GROUPS, GM_CHUNK, GM_CHUNK), GM_CHUNK ** -0.5),
        'gm_b_s': 1.0 + nrm(ks[17], (N_B, GM_CHUNK, GM_GROUPS), 0.02),
        'gm_w_out': nrm(ks[18], (N_B, GM_WIDTH, D), BETA * GM_WIDTH ** -0.5),
        'ffn_w_gate': nrm(ks[19], (DEPTH, D, FFN_HIDDEN), D ** -0.5),
        'ffn_w_up': nrm(ks[20], (DEPTH, D, FFN_HIDDEN), D ** -0.5),
        'ffn_w_down': nrm(ks[21], (DEPTH, FFN_HIDDEN, D), BETA * FFN_HIDDEN ** -0.5),
    }


def reference(x, c, ctx, c_ctx, mod_w, mod_b, ln_g, ln_b, gla_w_in, gla_w_decay, gla_b_decay,
              gla_norm_g, gla_w_out, gm_w_in, gm_ln_g, gm_ln_b, gm_w_s, gm_b_s, gm_w_out,
              ffn_w_gate, ffn_w_up, ffn_w_down):
    rows = x.shape[1] // GRID_W
    n_lat_chunks = rows // GM_ROWS_PER_CHUNK
    h, hc = x, ctx
    for i in range(DEPTH):
        ctx_live = i <= LAST_CTX_READER
        ctx_out = i < LAST_CTX_READER
        j = i // N_MIXERS
        sh1, sc1, g1, sh2, sc2, g2 = adaln(c, mod_w[i], mod_b[i])
        if ctx_live:
            csh1, csc1, cg1, csh2, csc2, cg2 = adaln(c_ctx, mod_w[i], mod_b[i])
            ac = modulate(hc, csh1, csc1)
        a = modulate(h, sh1, sc1)
        if i % N_MIXERS == 0:
            y, yc = gla_mixer(a, ac, ctx_out, gla_w_in[j], gla_w_decay[j], gla_b_decay[j],
                              gla_norm_g[j], gla_w_out[j])
        else:
            y = gmlp_chunk_mixer(a, n_lat_chunks, gm_w_in[j], gm_ln_g[j], gm_ln_b[j],
                                 gm_w_s[j], gm_b_s[j], gm_w_out[j])
            if ctx_out:
                yc = gmlp_chunk_mixer(ac, hc.shape[1] // GM_CHUNK, gm_w_in[j], gm_ln_g[j], gm_ln_b[j],
                                      gm_w_s[j], gm_b_s[j], gm_w_out[j])
        h = layer_norm(ALPHA * h + g1 * y, ln_g[i, 0], ln_b[i, 0])
        if ctx_out:
            hc = layer_norm(ALPHA * hc + cg1 * yc, ln_g[i, 0], ln_b[i, 0])
        f = swiglu(modulate(h, sh2, sc2), ffn_w_gate[i], ffn_w_up[i], ffn_w_down[i])
        h = layer_norm(ALPHA * h + g2 * f, ln_g[i, 1], ln_b[i, 1])
        if ctx_out:
            fc = swiglu(modulate(hc, csh2, csc2), ffn_w_gate[i], ffn_w_up[i], ffn_w_down[i])
            hc = layer_norm(ALPHA * hc + cg2 * fc, ln_g[i, 1], ln_b[i, 1])
    return h
```

```python
import numpy as np
import concourse.bass as bass
import concourse.mybir as mybir
from contextlib import ExitStack
from concourse.bass_utils import run_bass_kernel_spmd

F32 = mybir.dt.float32
BF16 = mybir.dt.bfloat16
AF = mybir.ActivationFunctionType
ALU = mybir.AluOpType

ENGINES = ("pe", "act", "dve", "pool", "sp")


class Tile:
    __slots__ = ("name", "ap", "space", "lo", "hi", "writers", "readers", "overlaps")

    def __init__(self, name, ap, space, lo, hi):
        self.name = name
        self.ap = ap
        self.space = space
        self.lo = lo
        self.hi = hi
        self.writers = {}
        self.readers = {}
        self.overlaps = []

    def __getitem__(self, k):
        return self.ap[k]


class Op:
    __slots__ = ("eng", "fn", "reads", "writes", "dom", "deps", "signal", "idx", "inc", "waits", "vc", "label")

    def __init__(self, eng, fn, reads, writes, dom, inc, label):
        self.eng = eng
        self.fn = fn
        self.reads = reads
        self.writes = writes
        self.dom = dom
        self.deps = []
        self.signal = False
        self.idx = 0
        self.inc = inc
        self.waits = []
        self.vc = None
        self.label = label


class Sched:
    def __init__(self, nc, stack):
        self.nc = nc
        self.stack = stack
        self.ops = []
        self.tiles_by_space = {}
        self.n_alloc = 0
        self.dom_last = {}

    def arena(self, name, nbytes):
        assert nbytes % 4 == 0
        t = self.stack.enter_context(self.nc.sbuf_tensor(name, [128, nbytes // 4], F32))
        return {"name": name, "t": t, "nbytes": nbytes, "cur": 0}

    def carve(self, arena, name, shape, dtype, at=None):
        esz = 2 if dtype == BF16 else 4
        n = int(np.prod(shape))
        nb = n * esz
        nb4 = (nb + 3) // 4 * 4
        if at is None:
            at = arena["cur"]
            arena["cur"] = at + nb4
        assert at % 4 == 0 and at + nb4 <= arena["nbytes"], (name, at, nb4, arena["nbytes"])
        ap = arena["t"][:, at // 4:(at + nb4) // 4]
        if dtype != F32:
            ap = ap.bitcast(dtype)
        ap = ap[:, 0:n]
        if len(shape) == 2:
            ap = ap.rearrange("p (a b) -> p a b", a=shape[0])
        elif len(shape) == 3:
            ap = ap.rearrange("p (a b c) -> p a b c", a=shape[0], b=shape[1])
        return self._reg(Tile(name, ap, arena["name"], at, at + nb4))

    def psum(self, name, shape, dtype=F32):
        t = self.stack.enter_context(self.nc.psum_tensor(name, [128] + list(shape), dtype))
        self.n_alloc += 1
        return self._reg(Tile(name, t[:], "psum_" + name, 0, 1))

    def sbuf(self, name, shape, dtype=F32, parts=128):
        t = self.stack.enter_context(self.nc.sbuf_tensor(name, [parts] + list(shape), dtype))
        return self._reg(Tile(name, t[:], "sb_" + name, 0, 1))

    def view(self, name, ap, parents):
        t = Tile(name, ap, None, 0, 0)
        for p in parents:
            t.overlaps.append(p)
            p.overlaps.append(t)
        return t

    def _reg(self, t):
        lst = self.tiles_by_space.setdefault(t.space, [])
        for u in lst:
            if u.lo < t.hi and t.lo < u.hi:
                u.overlaps.append(t)
                t.overlaps.append(u)
        lst.append(t)
        return t

    def add(self, eng, fn, reads=(), writes=(), dom=None, inc=1, label=""):
        op = Op(eng, fn, list(reads), list(writes), dom or eng, inc, label)
        deps = {}

        def need(p):
            if p is None or p is op:
                return
            deps[id(p)] = p

        for t in op.reads:
            for u in [t] + t.overlaps:
                for p in u.writers.values():
                    need(p)
        for t in op.writes:
            for u in [t] + t.overlaps:
                for p in u.writers.values():
                    need(p)
                for p in u.readers.values():
                    need(p)
        out = []
        for p in list(deps.values()):
            if p.dom.startswith("dma:"):
                if p.dom == op.dom:
                    continue
                p = self.dom_last[p.dom]
                if any(q is p for q in out):
                    continue
            if p.dom == op.dom and op.dom in ENGINES:
                if op.eng == "pe":
                    continue
                raw = any((p in u.writers.values()) for t in op.reads for u in [t] + t.overlaps)
                if not raw:
                    continue
            out.append(p)
        op.deps = out
        for p in out:
            p.signal = True
        for t in op.reads:
            t.readers[op.dom] = op
        for t in op.writes:
            t.writers = {op.dom: op}
            t.readers = {}
            for u in t.overlaps:
                u.writers[op.dom] = op
        if op.dom.startswith("dma:"):
            self.dom_last[op.dom] = op
        self.ops.append(op)
        return op

    def mm(self, out_t, out_ap, lhsT_t, lhsT_ap, rhs_t, rhs_ap, start=True, stop=True, extra_reads=()):
        return self.add("pe", lambda e: e.matmul(out_ap, lhsT_ap, rhs_ap, start=start, stop=stop),
                        reads=[lhsT_t, rhs_t] + list(extra_reads), writes=[out_t], label="mm")

    def dma(self, queue, out_ap, in_ap, reads=(), writes=(), dom=None, label="dma"):
        assert dom is not None
        return self.add(queue, lambda e: e.dma_start(out=out_ap, in_=in_ap),
                        reads=reads, writes=writes, dom="dma:" + dom, inc=16, label=label)

    def emit(self, final_wait_doms=()):
        nc = self.nc
        counts = {}
        for op in self.ops:
            if op.dom.startswith("dma:"):
                op.signal = True
            if op.signal:
                counts[op.dom] = counts.get(op.dom, 0) + 1
                op.idx = counts[op.dom]
        stream_vc = {e: {} for e in ENGINES}
        for op in self.ops:
            svc = stream_vc[op.eng]
            waits = []
            for p in op.deps:
                if svc.get(p.dom, 0) >= p.idx:
                    continue
                waits.append((p.dom, p.idx))
                for d, k in p.vc.items():
                    if svc.get(d, 0) < k:
                        svc[d] = k
            wm = {}
            for d, k in waits:
                if wm.get(d, 0) < k:
                    wm[d] = k
            op.waits = [(d, k) for d, k in wm.items() if True]
            if op.signal:
                vc = dict(svc)
                vc[op.dom] = op.idx
                op.vc = vc
            else:
                op.vc = None
        finals = [(d, k) for d, k in counts.items() if d.startswith("dma:")]
        sems = {}
        for d in counts:
            sems[d] = self.stack.enter_context(nc.semaphore("s_" + d.replace(":", "_")))
        self.n_sems = len(sems)
        per_eng = {e: [op for op in self.ops if op.eng == e] for e in ENGINES}
        self.stats = {e: len(per_eng[e]) for e in ENGINES}
        self.stats["waits"] = sum(len(op.waits) for op in self.ops)

        def run(eng_obj, ename):
            for op in per_eng[ename]:
                for d, k in op.waits:
                    mult = 16 if d.startswith("dma:") else 1
                    eng_obj.wait_ge(sems[d], k * mult)
                ins = op.fn(eng_obj)
                if op.signal:
                    ins.then_inc(sems[op.dom], op.inc)
            if ename == "sp":
                for d, k in finals:
                    eng_obj.wait_ge(sems[d], k * 16)

        with nc.Block() as block:
            @block.tensor
            def _(e):
                run(e, "pe")

            @block.scalar
            def _(e):
                run(e, "act")

            @block.vector
            def _(e):
                run(e, "dve")

            @block.gpsimd
            def _(e):
                run(e, "pool")

            @block.sync
            def _(e):
                run(e, "sp")


D = 1024
NTOK = 2048
NBLK = 4
ALPHA = (2 * 2) ** 0.25
HID = 2816
NJ = HID // 128
GW = 3072
NFC = GW // 128
QS = 128 ** -0.5
SM_CC, SM_MB0, SM_MB1, SM_LNG, SM_LNB, SM_NG, SM_GG, SM_GB, SM_N = 0, 16, 64, 112, 144, 176, 178, 202, 226


def build_program(stop_after=None):
    nc = bass.Bass("TRN2", target_bir_lowering=False)
    dt_ = nc.dram_tensor
    xT = dt_("xT", [D, 2 * NTOK + 256], F32, kind="ExternalInput").ap()
    sm = dt_("sm", [128, SM_N], F32, kind="ExternalInput").ap()
    cm = dt_("cm", [128, 7, 128], F32, kind="ExternalInput").ap()
    mod_w = dt_("mod_w", [2, D, 6 * D], F32, kind="ExternalInput").ap()
    w_in = dt_("w_in", [D, 3104], F32, kind="ExternalInput").ap()
    wdec = dt_("wdec", [16, 2, 512], F32, kind="ExternalInput").ap()
    bdec = dt_("bdec", [1, 2, 512], F32, kind="ExternalInput").ap()
    w_out = dt_("w_out", [D, D], F32, kind="ExternalInput").ap()
    gm_in = dt_("gm_in", [D, 2 * GW], F32, kind="ExternalInput").ap()
    gm_ws = dt_("gm_ws", [128, 4, 128], F32, kind="ExternalInput").ap()
    gm_bs = dt_("gm_bs", [128, 4 * 128], F32, kind="ExternalInput").ap()
    gm_out = dt_("gm_out", [GW, D], F32, kind="ExternalInput").ap()
    f_g = dt_("f_g", [2, D, HID], F32, kind="ExternalInput").ap()
    f_u = dt_("f_u", [2, D, HID], F32, kind="ExternalInput").ap()
    f_d = dt_("f_d", [2, HID, D], F32, kind="ExternalInput").ap()
    outT = dt_("outT", [D, NTOK], F32, kind="ExternalOutput").ap()

    st = ExitStack()
    with st:
        S = Sched(nc, st)
        A = S.arena("A", 210944)
        KB = 1024
        HA = [S.carve(A, f"ha{b}", [8, 512], F32, at=b * 16 * KB) for b in range(4)]
        HM = [S.carve(A, f"hm{b}", [8, 512], BF16, at=64 * KB + b * 8 * KB) for b in range(4)]
        OB = [S.carve(A, f"ob{t}", [1024], BF16, at=64 * KB + t * 2 * KB) for t in range(16)]
        A["cur"] = 96 * KB
        smt = S.carve(A, "smt", [SM_N], F32)
        cmt = S.carve(A, "cmt", [7, 128], BF16)
        ones = S.carve(A, "ones", [128], BF16)
        scb = S.carve(A, "scb", [8, 2], BF16)
        modT = [S.carve(A, f"modT{i}", [48, 2], F32) for i in range(2)]
        der = S.carve(A, "der", [24, 8], F32)
        wdt = S.carve(A, "wdt", [2, 512], BF16)
        bhi = S.carve(A, "bhi", [2, 512], BF16)
        blo = S.carve(A, "blo", [2, 512], BF16)
        btmp = S.carve(A, "btmp", [2, 512], F32, at=32 * KB)
        btmp2 = S.carve(A, "btmp2", [2, 512], F32, at=36 * KB)
        assert A["cur"] <= 108 * KB, A["cur"]
        PH = 108 * KB
        P = [S.psum(f"P{i}", [512]) for i in range(7)]
        PT = S.psum("PT", [1024], BF16)

        DER = {n: i for i, n in enumerate([
            "A1x", "cA1x", "Ga00", "Ba00", "A00", "B00", "Ga01", "Ba01", "A01", "B01",
            "Ga10", "Ba10", "A10", "B10", "G11", "B11", "tmp"])}

        def dv(name, m=None):
            i = DER[name]
            return der[:, i, :] if m is None else der[:, i, m:m + 1]

        def smv(off, n):
            return smt[:, off:off + n]

        def ACT(out_t, out_ap, in_t, in_ap, func, scale=1.0, bias=0.0, rd=()):
            S.add("act", lambda e: e.activation(out_ap, in_ap, func, bias=bias, scale=scale),
                  reads=[in_t] + list(rd), writes=[out_t])

        def TT(eng, out_t, out_ap, a_t, a_ap, b_t, b_ap, op):
            S.add(eng, lambda e: e.tensor_tensor(out_ap, a_ap, b_ap, op), reads=[a_t, b_t], writes=[out_t])

        def TS(eng, out_t, out_ap, in_t, in_ap, s1, s2, op0, op1=None, rd=()):
            if op1 is None:
                S.add(eng, lambda e: e.tensor_scalar(out_ap, in_ap, s1, None, op0), reads=[in_t] + list(rd), writes=[out_t])
            else:
                S.add(eng, lambda e: e.tensor_scalar(out_ap, in_ap, s1, s2, op0, op1), reads=[in_t] + list(rd), writes=[out_t])

        def STT(out_t, out_ap, a_t, a_ap, scal, b_t, b_ap, op0, op1, rd=()):
            S.add("dve", lambda e: e.scalar_tensor_tensor(out_ap, a_ap, scal, b_ap, op0, op1),
                  reads=[a_t, b_t] + list(rd), writes=[out_t])

        def CP(eng, out_t, out_ap, in_t, in_ap):
            if eng == "act":
                S.add("act", lambda e: e.copy(out_ap, in_ap), reads=[in_t], writes=[out_t])
            else:
                S.add(eng, lambda e: e.tensor_copy(out_ap, in_ap), reads=[in_t], writes=[out_t])

        def MM(out_t, out_ap, l_t, l_ap, r_t, r_ap, start, stop):
            S.mm(out_t, out_ap, l_t, l_ap, r_t, r_ap, start=start, stop=stop)

        stages = []

        def stage(load, compute):
            stages.append((load, compute))

        def run_stream():
            if stages:
                if stages[0][0]:
                    stages[0][0]()
            for si, (ld, cp) in enumerate(stages):
                if si + 1 < len(stages) and stages[si + 1][0]:
                    stages[si + 1][0]()
                if cp:
                    cp()

        S.dma("sp", smt.ap, sm, writes=[smt], dom="smt")
        S.dma("pool", cmt.ap, cm, writes=[cmt], dom="cmt")
        S.dma("pool", wdt[0:16], wdec, writes=[wdt], dom="wdt")
        S.dma("sp", btmp[0:1], bdec, writes=[btmp], dom="btmp")
        S.add("dve", lambda e: e.memset(ones.ap, 1.0), writes=[ones])
        CP("dve", bhi, bhi[0:1], btmp, btmp[0:1])
        CP("dve", btmp2, btmp2[0:1], bhi, bhi[0:1])
        TT("dve", btmp2, btmp2[0:1], btmp, btmp[0:1], btmp2, btmp2[0:1], ALU.subtract)
        CP("dve", blo, blo[0:1], btmp2, btmp2[0:1])
        ACT(scb, scb.ap, smt, smt[:, SM_CC:SM_CC + 16].rearrange("p (k c) -> p k c", c=2), AF.Silu)

        mwb = [S.carve(A, f"mwb{i}", [8, 512], BF16, at=48 * KB + i * 8 * KB) for i in range(2)]
        mcnt = [0]

        def mod_layer(i, pieces):
            for j in pieces:
                w = mwb[mcnt[0] % 2]
                mcnt[0] += 1

                def ld(w=w, j=j):
                    S.dma("pool", w.ap, mod_w[i, :, j * 512:(j + 1) * 512].rearrange("(kc p) n -> p kc n", p=128),
                          writes=[w], dom=w.name)

                def cp(w=w, j=j):
                    for m in range(4):
                        col = (j * 4 + m) * 2
                        for kc in range(8):
                            MM(P[6], P[6][:, col:col + 2], w, w[:, kc, m * 128:(m + 1) * 128], scb, scb[:, kc, :],
                               kc == 0, kc == 7)
                    c0 = j * 4
                    mb = SM_MB0 if i == 0 else SM_MB1
                    for cc in range(2):
                        TT("dve", modT[i], modT[i][:, c0:c0 + 4, cc],
                           P[6], P[6][:, c0 * 2:(c0 + 4) * 2].rearrange("p (a b) -> p a b", b=2)[:, :, cc],
                           smt, smt[:, mb + c0:mb + c0 + 4], ALU.add)
                stage(ld, cp)

        def mslice(i, which, col=0):
            return modT[i][:, which * 8:(which + 1) * 8, col]

        def derive(name, fn):
            fn(dv(name))

        def ln_scalars(k, li, lj, nxt, alpha=True):
            g = smt[:, SM_LNG + (li * 2 + lj) * 8: SM_LNG + (li * 2 + lj) * 8 + 8]
            b = smt[:, SM_LNB + (li * 2 + lj) * 8: SM_LNB + (li * 2 + lj) * 8 + 8]
            a = ALPHA if alpha else 1.0
            gn, bn = ("Ga" + k, "Ba" + k) if alpha else ("G" + k, "B" + k)
            TS("dve", der, dv(gn), smt, g, a, None, ALU.mult)
            TS("dve", der, dv(bn), smt, b, a, None, ALU.mult)
            if nxt is not None:
                L, sci, shi = nxt
                TS("dve", der, dv("tmp"), modT[L], mslice(L, sci), 1.0, None, ALU.add)
                TT("dve", der, dv("A" + k), smt, g, der, dv("tmp"), ALU.mult)
                TT("dve", der, dv("B" + k), smt, b, der, dv("tmp"), ALU.mult)
                TT("dve", der, dv("B" + k), der, dv("B" + k), modT[L], mslice(L, shi), ALU.add)

        mod_layer(0, range(12))

        def _d0():
            TS("dve", der, dv("A1x"), modT[0], mslice(0, 1, 0), 1.0, None, ALU.add)
            TS("dve", der, dv("cA1x"), modT[0], mslice(0, 1, 1), 1.0, None, ALU.add)
            ln_scalars("00", 0, 0, (0, 4, 3))
        stage(None, _d0)

        winb = S.carve(A, "winb", [8, 3104], BF16, at=PH)
        g0 = PH + 49664
        A["cur"] = g0
        xbuf = [S.carve(A, f"xbuf{i}", [8, 128], F32) for i in range(2)]
        ogT = S.carve(A, "ogT", [8, 512], BF16, at=g0)
        aTb = [S.carve(A, f"aT{i}", [8, 128], BF16) for i in range(2)]
        alr = S.carve(A, "alr", [128], BF16)
        spb = S.carve(A, "spb", [512], BF16)
        e_t = S.carve(A, "e_t", [512], F32)
        Est = S.carve(A, "Est", [512], F32)
        sq = S.carve(A, "sq", [4, 256], F32, at=A["cur"] - 4 * KB)
        EqT = S.carve(A, "EqT", [4, 128], F32)
        EkT = S.carve(A, "EkT", [4, 128], F32)
        kst = S.carve(A, "kst", [512], BF16)
        vb = S.carve(A, "vb", [1024], BF16)
        qin = S.carve(A, "qin", [4, 128], BF16)
        kin = S.carve(A, "kin", [4, 128], BF16)
        attT = S.carve(A, "attT", [4, 128], BF16)
        Sst = {d: S.carve(A, "S_" + d, [4, 256], F32) for d in "fb"}
        Sb = S.carve(A, "Sb", [4, 256], BF16)
        Dt = S.carve(A, "Dt", [4], F32)
        ssq = S.carve(A, "ssq", [4], F32)
        osum = S.carve(A, "osum", [4, 256], F32)
        sg = S.carve(A, "sg", [4, 256], BF16)
        og = S.carve(A, "og", [1024], BF16)
        woc = [S.carve(A, f"woc{i}", [8, 128], BF16) for i in range(2)]
        assert A["cur"] <= 210944, A["cur"]

        for kc in range(8):
            S.dma("pool", winb[:, kc, :], w_in[kc * 128:(kc + 1) * 128, :], writes=[winb], dom="winb")
        for d in "fb":
            S.add("pool", lambda e, d=d: e.memset(Sst[d].ap, 0.0), writes=[Sst[d]])

        CK, CV, CAF, CAB, CQ, CR = 0, 512, 1536, 1552, 1568, 2080
        cnt = {"x": 0}

        def tile_stage(col0, Ascale, shift, d, need_out, fin=None):
            i = cnt["x"] % 2
            cnt["x"] += 1
            xb_, at_ = xbuf[i], aTb[i]

            def ld():
                S.dma("sp", xb_.ap, xT[:, col0:col0 + 128].rearrange("(kc p) t -> p kc t", p=128), writes=[xb_], dom=xb_.name)

            def cp():
                for kc in range(8):
                    ACT(at_, at_[:, kc, :], xb_, xb_[:, kc, :], AF.Identity, scale=Ascale[:, kc:kc + 1], bias=shift[:, kc:kc + 1], rd=[der, modT[0]])
                gla_tile(at_, d, need_out, fin)
            stage(ld, cp)

        def gla_tile(at_, d, need_out, fin=None):
            di = 0 if d == "f" else 1
            triI, triR, msk = cmt[:, 3 * di + 0, :], cmt[:, 3 * di + 1, :], cmt[:, 3 * di + 2, :]
            last = 127 if d == "f" else 0
            ca = CAF if d == "f" else CAB
            for kc in range(8):
                MM(P[0], P[0][0:16, 0:128], winb, winb[:, kc, ca:ca + 16], at_, at_[:, kc, :], kc == 0, kc == 7)
            CP("act", alr, alr[0:16, :], P[0], P[0][0:16, 0:128])
            MM(P[1], P[1].ap, alr, alr[0:16, :], wdt, wdt[0:16, di, :], True, False)
            MM(P[1], P[1].ap, ones, ones[0:1, :], bhi, bhi[0:1, di, :], False, False)
            MM(P[1], P[1].ap, ones, ones[0:1, :], blo, blo[0:1, di, :], False, True)
            ACT(e_t, e_t.ap, P[1], P[1].ap, AF.Exp, scale=-1.0)
            ACT(spb, spb.ap, e_t, e_t.ap, AF.Ln, bias=1.0)
            MM(P[2], P[2].ap, cmt, triR, spb, spb.ap, True, True)
            ACT(Est, Est.ap, P[2], P[2].ap, AF.Exp)
            for h in range(4):
                MM(P[3], P[3][:, h * 128:(h + 1) * 128], spb, spb[:, h * 128:(h + 1) * 128], cmt, triI, True, True)
            p3 = P[3].ap.rearrange("p (h t) -> p h t", h=4)
            ACT(Dt, Dt.ap, P[3], p3[:, :, last], AF.Exp)
            if need_out:
                ACT(EqT, EqT.ap, P[3], p3, AF.Exp)
                ACT(EkT, EkT.ap, P[3], p3, AF.Exp, scale=-1.0)
            for kc in range(8):
                MM(P[4], P[4].ap, at_, at_[:, kc, :], winb, winb[:, kc, CK:CK + 512], kc == 0, kc == 7)
            TT("dve", kst, kst.ap, P[4], P[4].ap, Est, Est.ap, ALU.mult)
            for hb in range(2):
                for kc in range(8):
                    MM(P[5 + hb], P[5 + hb].ap, at_, at_[:, kc, :], winb, winb[:, kc, CV + hb * 512:CV + (hb + 1) * 512], kc == 0, kc == 7)
                CP("act", vb, vb[:, hb * 512:(hb + 1) * 512], P[5 + hb], P[5 + hb].ap)
            if need_out:
                for h in range(4):
                    for kc in range(8):
                        MM(P[0], P[0][:, h * 128:(h + 1) * 128], winb, winb[:, kc, CQ + h * 128:CQ + (h + 1) * 128], at_, at_[:, kc, :], kc == 0, kc == 7)
                STT(qin, qin.ap, P[0], P[0].ap.rearrange("p (h t) -> p h t", h=4), QS, EqT, EqT.ap, ALU.mult, ALU.mult)
                for h in range(4):
                    for kc in range(8):
                        MM(P[1], P[1][:, h * 128:(h + 1) * 128], winb, winb[:, kc, CK + h * 128:CK + (h + 1) * 128], at_, at_[:, kc, :], kc == 0, kc == 7)
                TT("dve", kin, kin.ap, P[1], P[1].ap.rearrange("p (h t) -> p h t", h=4), EkT, EkT.ap, ALU.mult)
                for h in range(4):
                    MM(P[2], P[2][:, h * 128:(h + 1) * 128], kin, kin[:, h, :], qin, qin[:, h, :], True, True)
                TT("dve", attT, attT.ap, P[2], P[2].ap.rearrange("p (h t) -> p h t", h=4),
                   cmt, msk.unsqueeze(1).to_broadcast([128, 4, 128]), ALU.mult)
                for h in range(4):
                    pb = P[3 + h // 2]
                    oap = pb[:, (h % 2) * 256:(h % 2 + 1) * 256]
                    MM(pb, oap, attT, attT[:, h, :], vb, vb[:, h * 256:(h + 1) * 256], True, False)
                    MM(pb, oap, qin, qin[:, h, :], Sb, Sb[:, h, :], False, True)
                fin(P[3], P[4])
            for h in range(4):
                pb = P[5 + h // 2]
                MM(pb, pb[:, (h % 2) * 256:(h % 2 + 1) * 256], kst, kst[:, h * 128:(h + 1) * 128], vb, vb[:, h * 256:(h + 1) * 256], True, True)
            for h in range(4):
                pb = P[5 + h // 2]
                STT(Sst[d], Sst[d][:, h, :], Sst[d], Sst[d][:, h, :], Dt[:, h:h + 1], pb, pb[:, (h % 2) * 256:(h % 2 + 1) * 256],
                    ALU.mult, ALU.add, rd=[Dt])
            CP("pool", Sb, Sb.ap, Sst[d], Sst[d].ap)

        csh = mslice(0, 0, 1)
        for t in range(2):
            tile_stage(2 * NTOK + t * 128, dv("cA1x"), csh, "f", False)
        for t in (1, 0):
            tile_stage(2 * NTOK + t * 128, dv("cA1x"), csh, "b", False)
        sh1 = mslice(0, 0, 0)
        for t in range(15, -1, -1):
            tile_stage(NTOK + t * 128, dv("A1x"), sh1, "b", False)
        def fin_b(t):
            def f(pa, pb):
                CP("act", OB[t], OB[t][:, 0:512], pa, pa.ap)
                CP("act", OB[t], OB[t][:, 512:1024], pb, pb.ap)
            return f
        stage(None, lambda: CP("pool", Sb, Sb.ap, Sst["b"], Sst["b"].ap))
        for t in range(15, -1, -1):
            tile_stage(t * 128, dv("A1x"), sh1, "b", True, fin_b(t))

        mod_layer(1, range(12))

        def _d1():
            ln_scalars("01", 0, 1, (1, 1, 0))
            ln_scalars("10", 1, 0, (1, 4, 3))
            ln_scalars("11", 1, 1, None, alpha=False)
        stage(None, _d1)

        PHL = PH + 90 * KB

        def ln_block(b, k, final=False, ub2=None, nmt=None):
            h = HA[b]
            ubs, uqs, nm, tq, rs = ub2
            for m in range(8):
                ub, uq = ubs[m % 2], uqs[m % 2]
                CP("act", ub, ub.ap, h, h[:, m, :])
                ACT(uq, uq.ap, h, h[:, m, :], AF.Square)
                MM(P[0], P[0].ap, ones, ones.ap, ub, ub.ap, m == 0, m == 7)
                MM(P[1], P[1].ap, ones, ones.ap, uq, uq.ap, m == 0, m == 7)
            TS("dve", nm, nm.ap, P[0], P[0].ap, -1.0 / D, None, ALU.mult)
            TT("dve", tq, tq.ap, nm, nm.ap, nm, nm.ap, ALU.mult)
            STT(rs, rs.ap, P[1], P[1].ap, 1.0 / D, tq, tq.ap, ALU.mult, ALU.subtract)
            ACT(rs, rs.ap, rs, rs.ap, AF.Ln, bias=1e-5)
            ACT(rs, rs.ap, rs, rs.ap, AF.Exp, scale=-0.5)
            for m in range(8):
                eng = "pool" if m % 2 == 0 else "dve"
                TT(eng, h, h[:, m, :], h, h[:, m, :], nm, nm.ap, ALU.add)
                TT(eng, h, h[:, m, :], h, h[:, m, :], rs, rs.ap, ALU.mult)
            if final:
                for m in range(8):
                    TS("dve" if m % 2 else "pool", h, h[:, m, :], h, h[:, m, :], dv("G" + k, m), dv("B" + k, m), ALU.mult, ALU.add, rd=[der])
            else:
                for m in range(8):
                    ACT(HM[b], HM[b][:, m, :], h, h[:, m, :], AF.Identity, scale=dv("A" + k, m), bias=dv("B" + k, m), rd=[der])
                    TS("dve" if m % 2 else "pool", h, h[:, m, :], h, h[:, m, :], dv("Ga" + k, m), dv("Ba" + k, m), ALU.mult, ALU.add, rd=[der])

        def ln_temps(base):
            A["cur"] = base
            ubs = [S.carve(A, f"ub{i}_{base}", [512], BF16) for i in range(2)]
            uqs = [S.carve(A, f"uq{i}_{base}", [512], BF16) for i in range(2)]
            nm = S.carve(A, f"nm_{base}", [512], F32)
            tq = S.carve(A, f"tq_{base}", [512], F32)
            rs = S.carve(A, f"rs_{base}", [512], F32)
            return (ubs, uqs, nm, tq, rs)

        lnA = ln_temps(e_t.lo)

        stage(None, lambda: CP("pool", Sb, Sb.ap, Sst["f"], Sst["f"].ap))
        wc = [0]
        for b in range(4):
            for tt in range(4):
                t = b * 4 + tt
                hb_ = HA[b]
                at_ = aTb[cnt["x"] % 2]
                cnt["x"] += 1

                def ldc(hb_=hb_, t=t, tt=tt):
                    S.dma("sp", hb_[:, :, tt * 128:(tt + 1) * 128], xT[:, t * 128:(t + 1) * 128].rearrange("(kc p) t -> p kc t", p=128),
                          writes=[hb_], dom=f"hax{t}")

                def pre(hb_=hb_, tt=tt, at_=at_):
                    for kc in range(8):
                        ACT(at_, at_[:, kc, :], hb_, hb_[:, kc, tt * 128:(tt + 1) * 128], AF.Identity,
                            scale=dv("A1x", kc), bias=sh1[:, kc:kc + 1], rd=[der, modT[0]])
                    for kc in range(8):
                        TS("pool", hb_, hb_[:, kc, tt * 128:(tt + 1) * 128], hb_, hb_[:, kc, tt * 128:(tt + 1) * 128], ALPHA, None, ALU.mult)

                def fin_c(pa, pb, t=t, tt=tt, at_=at_):
                    o3 = osum.ap
                    TT("dve", osum, osum[:, 0:2, :], pa, pa.ap.rearrange("p (h v) -> p h v", h=2), OB[t], OB[t][:, 0:512].rearrange("p (h v) -> p h v", h=2), ALU.add)
                    TT("dve", osum, osum[:, 2:4, :], pb, pb.ap.rearrange("p (h v) -> p h v", h=2), OB[t], OB[t][:, 512:1024].rearrange("p (h v) -> p h v", h=2), ALU.add)
                    TT("pool", sq, sq.ap, osum, osum.ap, osum, osum.ap, ALU.mult)
                    S.add("dve", lambda e: e.reduce_sum(ssq.ap, sq.ap, mybir.AxisListType.X), reads=[sq], writes=[ssq])
                    TS("dve", ssq, ssq.ap, ssq, ssq.ap, 1.0 / 256, None, ALU.mult)
                    ACT(ssq, ssq.ap, ssq, ssq.ap, AF.Ln, bias=1e-6)
                    ACT(ssq, ssq.ap, ssq, ssq.ap, AF.Exp, scale=-0.5)
                    for hb in range(2):
                        for kc in range(8):
                            MM(P[0 + hb], P[0 + hb].ap, at_, at_[:, kc, :], winb, winb[:, kc, CR + hb * 512:CR + (hb + 1) * 512], kc == 0, kc == 7)
                        ACT(sg, sg[:, 2 * hb:2 * hb + 2, :], P[0 + hb], P[0 + hb].ap.rearrange("p (h v) -> p h v", h=2), AF.Silu)
                    for h in range(4):
                        STT(og, og[:, h * 256:(h + 1) * 256], osum, osum[:, h, :], ssq[:, h:h + 1], sg, sg[:, h, :], ALU.mult, ALU.mult, rd=[ssq])
                    for c in range(8):
                        S.add("pe", lambda e, c=c: e.transpose(PT[:, c * 128:(c + 1) * 128], og[:, c * 128:(c + 1) * 128], cmt[:, 6, :]),
                              reads=[og, cmt], writes=[PT])
                    for c in range(8):
                        ng = smt[:, SM_NG + (c % 2):SM_NG + (c % 2) + 1]
                        ACT(ogT, ogT[:, c, tt * 128:(tt + 1) * 128], PT, PT[:, c * 128:(c + 1) * 128], AF.Identity, scale=ng, rd=[smt])

                def cpc(pre=pre, at_=at_, fin_c=fin_c):
                    pre()
                    gla_tile(at_, "f", True, fin_c)
                stage(ldc, cpc)
            for m in range(8):
                w = woc[wc[0] % 2]
                wc[0] += 1

                def ldo(w=w, m=m):
                    S.dma("pool", w.ap, w_out[:, m * 128:(m + 1) * 128].rearrange("(kc p) n -> p kc n", p=128), writes=[w], dom=w.name)

                def cpo(w=w, m=m, b=b):
                    for kc in range(8):
                        MM(P[2], P[2].ap, w, w[:, kc, :], ogT, ogT[:, kc, :], kc == 0, kc == 7)
                    STT(HA[b], HA[b][:, m, :], P[2], P[2].ap, modT[0][:, 16 + m, 0:1], HA[b], HA[b][:, m, :], ALU.mult, ALU.add, rd=[modT[0]])
                stage(ldo, cpo)
            stage(None, lambda b=b: ln_block(b, "00", ub2=lnA))

        def dump_and_finish():
            run_stream()
            for b in range(4):
                S.dma("sp", outT[:, b * 512:(b + 1) * 512].rearrange("(m p) t -> p m t", p=128), HA[b].ap, reads=[HA[b]], dom=f"out{b}")
            S.emit()
            return nc

        if stop_after == "gla":
            return dump_and_finish(), S

        A["cur"] = PH
        actb = S.carve(A, "actb", [NJ, 1024], BF16)
        ut = S.carve(A, "ut", [NFC, 512], BF16, at=PH)
        vg = [S.carve(A, f"vg{t}", [GW], BF16, at=PH + 24576 + t * 6144) for t in range(4)]
        A["cur"] = PH + 49152
        wgu = [[S.carve(A, f"w{n}{i}", [8, 128], BF16) for i in range(2)] for n in "gu"]
        wdn = [S.carve(A, f"wdn{i}", [NJ, 128], BF16) for i in range(2)]
        sgl = [S.carve(A, f"sgl{i}", [512], F32) for i in range(2)]
        ffn_end = A["cur"]
        A["cur"] = PH + 49152
        wiu = [S.carve(A, f"wiu{i}", [8, 128], BF16) for i in range(2)]
        wiv = [S.carve(A, f"wiv{i}", [8, 512], BF16) for i in range(2)]
        wom = [S.carve(A, f"wom{i}", [NFC, 128], BF16) for i in range(2)]
        wsT = S.carve(A, "wsT", [4, 128], BF16)
        Rg = S.carve(A, "Rg", [4, 128], F32)
        Bsg = S.carve(A, "Bsg", [4, 128], F32)
        tr0 = max(A["cur"], ffn_end)
        A["cur"] = tr0
        Xc = [S.carve(A, f"Xc{i}", [128], F32) for i in range(2)]
        stmp = [S.carve(A, f"stmp{i}", [512], F32) for i in range(2)]
        bst = S.carve(A, "bst", [6, 6], F32)
        mvt = S.carve(A, "mvt", [2], F32)
        rst = S.carve(A, "rst", [1], F32)
        lnB = ln_temps(tr0)
        assert A["cur"] <= 210944, A["cur"]
        cn = {"g": 0, "d": 0, "iu": 0, "iv": 0, "om": 0, "s": 0, "x": 0, "st": 0}

        def ffn_group(li, g, k):
            blks = (2 * g, 2 * g + 1)
            for j in range(NJ):
                i = cn["g"] % 2
                cn["g"] += 1
                wg_, wu_ = wgu[0][i], wgu[1][i]

                def ld(wg_=wg_, wu_=wu_, j=j):
                    S.dma("pool", wg_.ap, f_g[li, :, j * 128:(j + 1) * 128].rearrange("(kc p) n -> p kc n", p=128), writes=[wg_], dom=wg_.name)
                    S.dma("pool", wu_.ap, f_u[li, :, j * 128:(j + 1) * 128].rearrange("(kc p) n -> p kc n", p=128), writes=[wu_], dom=wu_.name)

                def cp(wg_=wg_, wu_=wu_, j=j):
                    for bi, b in enumerate(blks):
                        pg, pu = P[2 * bi], P[2 * bi + 1]
                        for kc in range(8):
                            MM(pg, pg.ap, wg_, wg_[:, kc, :], HM[b], HM[b][:, kc, :], kc == 0, kc == 7)
                        for kc in range(8):
                            MM(pu, pu.ap, wu_, wu_[:, kc, :], HM[b], HM[b][:, kc, :], kc == 0, kc == 7)
                        s_ = sgl[cn["s"] % 2]
                        cn["s"] += 1
                        ACT(s_, s_.ap, pg, pg.ap, AF.Silu)
                        TT("dve", actb, actb[:, j, bi * 512:(bi + 1) * 512], s_, s_.ap, pu, pu.ap, ALU.mult)
                stage(ld, cp)
            for m in range(8):
                w = wdn[cn["d"] % 2]
                cn["d"] += 1

                def ld(w=w, m=m):
                    S.dma("pool", w.ap, f_d[li, :, m * 128:(m + 1) * 128].rearrange("(j p) n -> p j n", p=128), writes=[w], dom=w.name)

                def cp(w=w, m=m):
                    for bi, b in enumerate(blks):
                        py = P[4 + bi]
                        for j in range(NJ):
                            MM(py, py.ap, w, w[:, j, :], actb, actb[:, j, bi * 512:(bi + 1) * 512], j == 0, j == NJ - 1)
                        STT(HA[b], HA[b][:, m, :], py, py.ap, modT[li][:, 40 + m, 0:1], HA[b], HA[b][:, m, :], ALU.mult, ALU.add, rd=[modT[li]])
                stage(ld, cp)
            for b in blks:
                stage(None, lambda b=b: ln_block(b, k, final=(k == "11"), ub2=lnB))

        def gmlp_consts():
            def ld():
                S.dma("pool", wsT.ap, gm_ws, writes=[wsT], dom="wsT")
                S.dma("sp", Bsg.ap, gm_bs.rearrange("p (g q) -> p g q", g=4), writes=[Bsg], dom="Bsg")

            def cp():
                MM(P[6], P[6].ap, ones, ones.ap, wsT, wsT.ap.rearrange("p g q -> p (g q)"), True, True)
                CP("dve", Rg, Rg.ap, P[6], P[6].ap.rearrange("p (g q) -> p g q", g=4))
            stage(ld, cp)

        def gmlp_block(b):
            hm = HM[b]
            for fc in range(NFC):
                w = wiu[cn["iu"] % 2]
                cn["iu"] += 1

                def ld(w=w, fc=fc):
                    S.dma("pool", w.ap, gm_in[:, fc * 128:(fc + 1) * 128].rearrange("(kc p) n -> p kc n", p=128), writes=[w], dom=w.name)

                def cp(w=w, fc=fc):
                    pu = P[fc % 2]
                    for kc in range(8):
                        MM(pu, pu.ap, w, w[:, kc, :], hm, hm[:, kc, :], kc == 0, kc == 7)
                    ACT(ut, ut[:, fc, :], pu, pu.ap, AF.Gelu)
                stage(ld, cp)
            for cb in range(6):
                w = wiv[cn["iv"] % 2]
                cn["iv"] += 1

                def ld(w=w, cb=cb):
                    S.dma("pool", w.ap, gm_in[:, GW + cb * 512:GW + (cb + 1) * 512].rearrange("(kc p) n -> p kc n", p=128), writes=[w], dom=w.name)

                def cp(w=w, cb=cb):
                    for t in range(4):
                        pv = P[2 + t]
                        for kc in range(8):
                            MM(pv, pv.ap, hm, hm[:, kc, t * 128:(t + 1) * 128], w, w[:, kc, :], kc == 0, kc == 7)
                        ACT(vg[t], vg[t][:, cb * 512:(cb + 1) * 512], pv, pv.ap, AF.Gelu)
                stage(ld, cp)
            stage(None, lambda: gm_mid(b))
            gm_tail(b)

        def gm_mid(b):
            hm = HM[b]
            for t in range(4):
                for cb in range(6):
                    S.add("dve", lambda e, t=t, cb=cb: e.bn_stats(bst[:, cb, :], vg[t][:, cb * 512:(cb + 1) * 512]), reads=[vg[t]], writes=[bst])
                S.add("dve", lambda e: e.bn_aggr(mvt.ap, bst.ap.rearrange("p a b -> p (a b)")), reads=[bst], writes=[mvt])
                ACT(rst, rst.ap, mvt, mvt[:, 1:2], AF.Ln, bias=1e-5)
                ACT(rst, rst.ap, rst, rst.ap, AF.Exp, scale=-0.5)
                TS("dve", vg[t], vg[t].ap, vg[t], vg[t].ap, mvt[:, 0:1], rst[:, 0:1], ALU.subtract, ALU.mult, rd=[mvt, rst])
            for fc in range(NFC):
                g = fc // 6
                pm = P[fc % 2]
                for t in range(4):
                    MM(pm, pm[:, t * 128:(t + 1) * 128], vg[t], vg[t][:, fc * 128:(fc + 1) * 128], wsT, wsT[:, g, :], True, True)
                xc = Xc[cn["x"] % 2]
                cn["x"] += 1
                STT(xc, xc.ap, Rg, Rg[:, g, :], smt[:, SM_GB + fc:SM_GB + fc + 1], Bsg, Bsg[:, g, :], ALU.mult, ALU.add, rd=[smt])
                s_ = stmp[cn["st"] % 2]
                cn["st"] += 1
                STT(s_, s_.ap.rearrange("p (t q) -> p t q", t=4), pm, pm.ap.rearrange("p (t q) -> p t q", t=4), smt[:, SM_GG + fc:SM_GG + fc + 1],
                    xc, xc.ap.unsqueeze(1).to_broadcast([128, 4, 128]), ALU.mult, ALU.add, rd=[smt])
                TT("pool", ut, ut[:, fc, :], ut, ut[:, fc, :], s_, s_.ap, ALU.mult)

        def gm_tail(b):
            for m in range(8):
                w = wom[cn["om"] % 2]
                cn["om"] += 1

                def ld(w=w, m=m):
                    S.dma("pool", w.ap, gm_out[:, m * 128:(m + 1) * 128].rearrange("(j p) n -> p j n", p=128), writes=[w], dom=w.name)

                def cp(w=w, m=m):
                    py = P[6]
                    for fc in range(NFC):
                        MM(py, py.ap, w, w[:, fc, :], ut, ut[:, fc, :], fc == 0, fc == NFC - 1)
                    STT(HA[b], HA[b][:, m, :], py, py.ap, modT[1][:, 16 + m, 0:1], HA[b], HA[b][:, m, :], ALU.mult, ALU.add, rd=[modT[1]])
                stage(ld, cp)
            stage(None, lambda: ln_block(b, "10", ub2=lnB))

        if stop_after == "ffn0":
            for g in range(2):
                ffn_group(0, g, "01")
            return dump_and_finish(), S
        gmlp_consts()
        for g in range(2):
            ffn_group(0, g, "01")
            if stop_after == "gmlp":
                gmlp_block(2 * g)
                gmlp_block(2 * g + 1)
                continue
            gmlp_block(2 * g)
            gmlp_block(2 * g + 1)
            ffn_group(1, g, "11")
        return dump_and_finish(), S


def _consts():
    j = np.arange(128)[:, None]
    i = np.arange(128)[None, :]
    c = -1.0 / 16.0
    mats = [c * (j <= i), c * (j > i), 1.0 * (j <= i), c * (j >= i), c * (j < i), 1.0 * (j >= i), 1.0 * (j == i)]
    return np.ascontiguousarray(np.stack([m.astype(np.float32) for m in mats], axis=1))


def _fm(v, n):
    return np.asarray(v, np.float32).reshape(n, 128).T


def make_in_maps(x, c, ctx, c_ctx, mod_w, mod_b, ln_g, ln_b, gla_w_in, gla_w_decay, gla_b_decay,
                 gla_norm_g, gla_w_out, gm_w_in, gm_ln_g, gm_ln_b, gm_w_s, gm_b_s, gm_w_out,
                 ffn_w_gate, ffn_w_up, ffn_w_down):
    f = lambda a: np.ascontiguousarray(np.asarray(a, dtype=np.float32))
    x, c, ctx, c_ctx = f(x), f(c), f(ctx), f(c_ctx)
    cmv = _consts()
    w_in0 = f(gla_w_in[0])
    w_in1 = w_in0.copy()
    w_in1[:, 1536:1552] = w_in0[:, 1552:1568]
    w_in1[:, 1552:1568] = w_in0[:, 1536:1552]
    wd = f(gla_w_decay[0])
    bd = f(gla_b_decay[0])
    ws = f(gm_w_s[0])
    bs = f(gm_b_s[0])
    shared = {
        "cm": cmv, "mod_w": f(mod_w), "w_out": f(gla_w_out[0]), "gm_in": f(gm_w_in[0]), "gm_out": f(gm_w_out[0]),
        "f_g": f(ffn_w_gate), "f_u": f(ffn_w_up), "f_d": f(ffn_w_down),
    }
    maps = []
    for core in range(8):
        b, half = core // 2, core % 2
        xs, cs = x[b], ctx[b]
        if half:
            xs, cs = xs[::-1], cs[::-1]
        xall = np.ascontiguousarray(np.concatenate([xs, cs], axis=0).T)
        smv = np.zeros((128, SM_N), np.float32)
        smv[:, SM_CC:SM_CC + 16] = np.stack([_fm(c[b], 8), _fm(c_ctx, 8)], axis=-1).reshape(128, 16)
        smv[:, SM_MB0:SM_MB0 + 48] = _fm(mod_b[0], 48)
        smv[:, SM_MB1:SM_MB1 + 48] = _fm(mod_b[1], 48)
        for li in range(2):
            for lj in range(2):
                o = (li * 2 + lj) * 8
                smv[:, SM_LNG + o:SM_LNG + o + 8] = _fm(ln_g[li, lj], 8)
                smv[:, SM_LNB + o:SM_LNB + o + 8] = _fm(ln_b[li, lj], 8)
        smv[:, SM_NG:SM_NG + 2] = _fm(gla_norm_g[0], 2)
        smv[:, SM_GG:SM_GG + 24] = _fm(gm_ln_g[0], 24)
        smv[:, SM_GB:SM_GB + 24] = _fm(gm_ln_b[0], 24)
        wdd = wd[::-1] if half else wd
        bdd = bd[::-1] if half else bd
        wss = ws[:, ::-1, ::-1] if half else ws
        bss = bs[::-1] if half else bs
        m = dict(shared)
        m.update({
            "xT": xall, "sm": smv, "w_in": w_in1 if half else w_in0,
            "wdec": np.ascontiguousarray(wdd.transpose(1, 0, 2)),
            "bdec": np.ascontiguousarray(bdd.reshape(1, 2, 512)),
            "gm_ws": np.ascontiguousarray(wss.transpose(2, 0, 1)),
            "gm_bs": np.ascontiguousarray(np.broadcast_to(bss.T.reshape(1, 512), (128, 512))),
        })
        maps.append(m)
    return maps


def assemble(results):
    out = np.empty((4, 4096, D), np.float32)
    for core in range(8):
        b, half = core // 2, core % 2
        o = results[core]["outT"].T
        if half:
            out[b, 2048:] = o[::-1]
        else:
            out[b, :2048] = o
    return out


_NC = {}


def kernel(**inputs):
    if "nc" not in _NC:
        _NC["nc"] = build_program()[0]
    maps = make_in_maps(**inputs)
    res = run_bass_kernel_spmd(_NC["nc"], maps, core_ids=list(range(8)))
    return assemble(res.results)
```

```python
import numpy as np
import concourse.bass as bass
import concourse.mybir as mybir
from contextlib import ExitStack
from concourse.bass_utils import run_bass_kernel_spmd

F32 = mybir.dt.float32
BF16 = mybir.dt.bfloat16
AF = mybir.ActivationFunctionType
ALU = mybir.AluOpType

ENGINES = ("pe", "act", "dve", "pool", "sp")


class Tile:
    __slots__ = ("name", "ap", "space", "lo", "hi", "writers", "readers", "overlaps")

    def __init__(self, name, ap, space, lo, hi):
        self.name = name
        self.ap = ap
        self.space = space
        self.lo = lo
        self.hi = hi
        self.writers = {}
        self.readers = {}
        self.overlaps = []

    def __getitem__(self, k):
        return self.ap[k]


class Op:
    __slots__ = ("eng", "fn", "reads", "writes", "dom", "deps", "signal", "idx", "inc", "waits", "vc", "label")

    def __init__(self, eng, fn, reads, writes, dom, inc, label):
        self.eng = eng
        self.fn = fn
        self.reads = reads
        self.writes = writes
        self.dom = dom
        self.deps = []
        self.signal = False
        self.idx = 0
        self.inc = inc
        self.waits = []
        self.vc = None
        self.label = label


class Sched:
    def __init__(self, nc, stack):
        self.nc = nc
        self.stack = stack
        self.ops = []
        self.tiles_by_space = {}
        self.n_alloc = 0
        self.dom_last = {}

    def arena(self, name, nbytes):
        assert nbytes % 4 == 0
        t = self.stack.enter_context(self.nc.sbuf_tensor(name, [128, nbytes // 4], F32))
        return {"name": name, "t": t, "nbytes": nbytes, "cur": 0}

    def carve(self, arena, name, shape, dtype, at=None):
        esz = 2 if dtype == BF16 else 4
        n = int(np.prod(shape))
        nb = n * esz
        nb4 = (nb + 3) // 4 * 4
        if at is None:
            at = arena["cur"]
            arena["cur"] = at + nb4
        assert at % 4 == 0 and at + nb4 <= arena["nbytes"], (name, at, nb4, arena["nbytes"])
        ap = arena["t"][:, at // 4:(at + nb4) // 4]
        if dtype != F32:
            ap = ap.bitcast(dtype)
        ap = ap[:, 0:n]
        if len(shape) == 2:
            ap = ap.rearrange("p (a b) -> p a b", a=shape[0])
        elif len(shape) == 3:
            ap = ap.rearrange("p (a b c) -> p a b c", a=shape[0], b=shape[1])
        return self._reg(Tile(name, ap, arena["name"], at, at + nb4))

    def psum(self, name, shape, dtype=F32):
        t = self.stack.enter_context(self.nc.psum_tensor(name, [128] + list(shape), dtype))
        self.n_alloc += 1
        return self._reg(Tile(name, t[:], "psum_" + name, 0, 1))

    def sbuf(self, name, shape, dtype=F32, parts=128):
        t = self.stack.enter_context(self.nc.sbuf_tensor(name, [parts] + list(shape), dtype))
        return self._reg(Tile(name, t[:], "sb_" + name, 0, 1))

    def view(self, name, ap, parents):
        t = Tile(name, ap, None, 0, 0)
        for p in parents:
            t.overlaps.append(p)
            p.overlaps.append(t)
        return t

    def _reg(self, t):
        lst = self.tiles_by_space.setdefault(t.space, [])
        for u in lst:
            if u.lo < t.hi and t.lo < u.hi:
                u.overlaps.append(t)
                t.overlaps.append(u)
        lst.append(t)
        return t

    def add(self, eng, fn, reads=(), writes=(), dom=None, inc=1, label=""):
        op = Op(eng, fn, list(reads), list(writes), dom or eng, inc, label)
        deps = {}

        def need(p):
            if p is None or p is op:
                return
            deps[id(p)] = p

        for t in op.reads:
            for u in [t] + t.overlaps:
                for p in u.writers.values():
                    need(p)
        for t in op.writes:
            for u in [t] + t.overlaps:
                for p in u.writers.values():
                    need(p)
                for p in u.readers.values():
                    need(p)
        out = []
        for p in list(deps.values()):
            if p.dom.startswith("dma:"):
                if p.dom == op.dom:
                    continue
                p = self.dom_last[p.dom]
                if any(q is p for q in out):
                    continue
            if p.dom == op.dom and op.dom in ENGINES:
                if op.eng == "pe":
                    continue
                raw = any((p in u.writers.values()) for t in op.reads for u in [t] + t.overlaps)
                if not raw:
                    continue
            out.append(p)
        op.deps = out
        for p in out:
            p.signal = True
        for t in op.reads:
            t.readers[op.dom] = op
        for t in op.writes:
            t.writers = {op.dom: op}
            t.readers = {}
        if op.dom.startswith("dma:"):
            self.dom_last[op.dom] = op
        self.ops.append(op)
        return op

    def mm(self, out_t, out_ap, lhsT_t, lhsT_ap, rhs_t, rhs_ap, start=True, stop=True, extra_reads=()):
        return self.add("pe", lambda e: e.matmul(out_ap, lhsT_ap, rhs_ap, start=start, stop=stop),
                        reads=[lhsT_t, rhs_t] + list(extra_reads), writes=[out_t], label="mm")

    def dma(self, queue, out_ap, in_ap, reads=(), writes=(), dom=None, label="dma"):
        assert dom is not None
        return self.add(queue, lambda e: e.dma_start(out=out_ap, in_=in_ap),
                        reads=reads, writes=writes, dom="dma:" + dom, inc=16, label=label)

    def emit(self, final_wait_doms=()):
        nc = self.nc
        counts = {}
        for op in self.ops:
            if op.dom.startswith("dma:"):
                op.signal = True
            if op.signal:
                counts[op.dom] = counts.get(op.dom, 0) + 1
                op.idx = counts[op.dom]
        stream_vc = {e: {} for e in ENGINES}
        for op in self.ops:
            svc = stream_vc[op.eng]
            waits = []
            for p in op.deps:
                if svc.get(p.dom, 0) >= p.idx:
                    continue
                waits.append((p.dom, p.idx))
                for d, k in p.vc.items():
                    if svc.get(d, 0) < k:
                        svc[d] = k
            wm = {}
            for d, k in waits:
                if wm.get(d, 0) < k:
                    wm[d] = k
            op.waits = [(d, k) for d, k in wm.items() if True]
            if op.signal:
                vc = dict(svc)
                vc[op.dom] = op.idx
                op.vc = vc
            else:
                op.vc = None
        finals = [(d, k) for d, k in counts.items() if d.startswith("dma:")]
        sems = {}
        for d in counts:
            sems[d] = self.stack.enter_context(nc.semaphore("s_" + d.replace(":", "_")))
        self.n_sems = len(sems)
        per_eng = {e: [op for op in self.ops if op.eng == e] for e in ENGINES}
        self.stats = {e: len(per_eng[e]) for e in ENGINES}
        self.stats["waits"] = sum(len(op.waits) for op in self.ops)

        def run(eng_obj, ename):
            for op in per_eng[ename]:
                for d, k in op.waits:
                    mult = 16 if d.startswith("dma:") else 1
                    eng_obj.wait_ge(sems[d], k * mult)
                ins = op.fn(eng_obj)
                if op.signal:
                    ins.then_inc(sems[op.dom], op.inc)
            if ename == "sp":
                for d, k in finals:
                    eng_obj.wait_ge(sems[d], k * 16)

        with nc.Block() as block:
            @block.tensor
            def _(e):
                run(e, "pe")

            @block.scalar
            def _(e):
                run(e, "act")

            @block.vector
            def _(e):
                run(e, "dve")

            @block.gpsimd
            def _(e):
                run(e, "pool")

            @block.sync
            def _(e):
                run(e, "sp")


D = 1024
NTOK = 2048
NBLK = 4
ALPHA = (2 * 2) ** 0.25
HID = 2816
NJ = HID // 128
GW = 3072
NFC = GW // 128
QS = 128 ** -0.5
SM_CC, SM_MB0, SM_MB1, SM_LNG, SM_LNB, SM_NG, SM_GG, SM_GB, SM_N = 0, 16, 64, 112, 144, 176, 178, 202, 226


def build_program(stop_after=None):
    nc = bass.Bass("TRN2", target_bir_lowering=False)
    dt_ = nc.dram_tensor
    xT = dt_("xT", [34, 128, 8, 128], F32, kind="ExternalInput").ap()
    sm = dt_("sm", [128, SM_N], F32, kind="ExternalInput").ap()
    cm = dt_("cm", [128, 7, 128], F32, kind="ExternalInput").ap()
    mod_w = dt_("mod_w", [2, 12, 128, 8, 512], F32, kind="ExternalInput").ap()
    w_in = dt_("w_in", [128, 8, 3104], F32, kind="ExternalInput").ap()
    wdec = dt_("wdec", [16, 2, 512], F32, kind="ExternalInput").ap()
    bdec = dt_("bdec", [1, 2, 512], F32, kind="ExternalInput").ap()
    w_out = dt_("w_out", [8, 128, 8, 128], F32, kind="ExternalInput").ap()
    gm_iu = dt_("gm_iu", [12, 128, 2, 8, 128], F32, kind="ExternalInput").ap()
    gm_iv = dt_("gm_iv", [6, 128, 8, 512], F32, kind="ExternalInput").ap()
    gm_ws = dt_("gm_ws", [128, 4, 128], F32, kind="ExternalInput").ap()
    gm_bs = dt_("gm_bs", [128, 4 * 128], F32, kind="ExternalInput").ap()
    gm_out = dt_("gm_out", [8, 128, NFC, 128], F32, kind="ExternalInput").ap()
    f_gu = dt_("f_gu", [2, NJ, 128, 2, 8, 128], F32, kind="ExternalInput").ap()
    f_d = dt_("f_d", [2, 8, 128, NJ, 128], F32, kind="ExternalInput").ap()
    outT = dt_("outT", [4, 128, 8, 512], F32, kind="ExternalOutput").ap()

    st = ExitStack()
    with st:
        S = Sched(nc, st)
        A = S.arena("A", 210944)
        KB = 1024
        HA = [S.carve(A, f"ha{b}", [8, 512], F32, at=b * 16 * KB) for b in range(4)]
        HM = [S.carve(A, f"hm{b}", [8, 512], BF16, at=64 * KB + b * 8 * KB) for b in range(4)]
        OB = [S.carve(A, f"ob{t}", [1024], BF16, at=64 * KB + t * 2 * KB) for t in range(16)]
        A["cur"] = 96 * KB
        smt = S.carve(A, "smt", [SM_N], F32)
        cmt = S.carve(A, "cmt", [7, 128], BF16)
        ones = S.carve(A, "ones", [128], BF16)
        scb = S.carve(A, "scb", [8, 2], BF16)
        modT = [S.carve(A, f"modT{i}", [48, 2], F32) for i in range(2)]
        der = S.carve(A, "der", [24, 8], F32)
        wdt = S.carve(A, "wdt", [2, 512], BF16)
        bhi = S.carve(A, "bhi", [2, 512], BF16)
        blo = S.carve(A, "blo", [2, 512], BF16)
        btmp = S.carve(A, "btmp", [2, 512], F32, at=32 * KB)
        btmp2 = S.carve(A, "btmp2", [2, 512], F32, at=36 * KB)
        assert A["cur"] <= 108 * KB, A["cur"]
        PH = 108 * KB
        P = [S.psum(f"P{i}", [512]) for i in range(7)]
        PT = S.psum("PT", [1024], BF16)

        DER = {n: i for i, n in enumerate([
            "A1x", "cA1x", "Ga00", "Ba00", "A00", "B00", "Ga01", "Ba01", "A01", "B01",
            "Ga10", "Ba10", "A10", "B10", "G11", "B11", "tmp"])}

        def dv(name, m=None):
            i = DER[name]
            return der[:, i, :] if m is None else der[:, i, m:m + 1]

        def smv(off, n):
            return smt[:, off:off + n]

        def ACT(out_t, out_ap, in_t, in_ap, func, scale=1.0, bias=0.0, rd=()):
            S.add("act", lambda e: e.activation(out_ap, in_ap, func, bias=bias, scale=scale),
                  reads=[in_t] + list(rd), writes=[out_t])

        def TT(eng, out_t, out_ap, a_t, a_ap, b_t, b_ap, op):
            S.add(eng, lambda e: e.tensor_tensor(out_ap, a_ap, b_ap, op), reads=[a_t, b_t], writes=[out_t])

        def TS(eng, out_t, out_ap, in_t, in_ap, s1, s2, op0, op1=None, rd=()):
            if op1 is None:
                S.add(eng, lambda e: e.tensor_scalar(out_ap, in_ap, s1, None, op0), reads=[in_t] + list(rd), writes=[out_t])
            else:
                S.add(eng, lambda e: e.tensor_scalar(out_ap, in_ap, s1, s2, op0, op1), reads=[in_t] + list(rd), writes=[out_t])

        def STT(out_t, out_ap, a_t, a_ap, scal, b_t, b_ap, op0, op1, rd=()):
            S.add("dve", lambda e: e.scalar_tensor_tensor(out_ap, a_ap, scal, b_ap, op0, op1),
                  reads=[a_t, b_t] + list(rd), writes=[out_t])

        def CP(eng, out_t, out_ap, in_t, in_ap):
            if eng == "act":
                S.add("act", lambda e: e.copy(out_ap, in_ap), reads=[in_t], writes=[out_t])
            else:
                S.add(eng, lambda e: e.tensor_copy(out_ap, in_ap), reads=[in_t], writes=[out_t])

        def MM(out_t, out_ap, l_t, l_ap, r_t, r_ap, start, stop):
            S.mm(out_t, out_ap, l_t, l_ap, r_t, r_ap, start=start, stop=stop)

        stages = []

        def stage(load, compute):
            stages.append((load, compute))

        def run_stream():
            if stages:
                if stages[0][0]:
                    stages[0][0]()
            for si, (ld, cp) in enumerate(stages):
                if si + 1 < len(stages) and stages[si + 1][0]:
                    stages[si + 1][0]()
                if cp:
                    cp()

        S.dma("sp", smt.ap, sm, writes=[smt], dom="smt")
        S.dma("pool", cmt.ap, cm, writes=[cmt], dom="cmt")
        S.dma("pool", wdt[0:16], wdec, writes=[wdt], dom="wdt")
        S.dma("sp", btmp[0:1], bdec, writes=[btmp], dom="btmp")
        S.add("dve", lambda e: e.memset(ones.ap, 1.0), writes=[ones])
        CP("dve", bhi, bhi[0:1], btmp, btmp[0:1])
        CP("dve", btmp2, btmp2[0:1], bhi, bhi[0:1])
        TT("dve", btmp2, btmp2[0:1], btmp, btmp[0:1], btmp2, btmp2[0:1], ALU.subtract)
        CP("dve", blo, blo[0:1], btmp2, btmp2[0:1])
        ACT(scb, scb.ap, smt, smt[:, SM_CC:SM_CC + 16].rearrange("p (k c) -> p k c", c=2), AF.Silu)

        mwb = [S.carve(A, f"mwb{i}", [8, 512], BF16, at=48 * KB + i * 8 * KB) for i in range(2)]
        mcnt = [0]

        def mod_layer(i, pieces):
            for j in pieces:
                w = mwb[mcnt[0] % 2]
                mcnt[0] += 1

                def ld(w=w, j=j):
                    S.dma("pool", w.ap, mod_w[i, j],
                          writes=[w], dom=w.name)

                def cp(w=w, j=j):
                    for m in range(4):
                        col = (j * 4 + m) * 2
                        for kc in range(8):
                            MM(P[6], P[6][:, col:col + 2], w, w[:, kc, m * 128:(m + 1) * 128], scb, scb[:, kc, :],
                               kc == 0, kc == 7)
                    c0 = j * 4
                    mb = SM_MB0 if i == 0 else SM_MB1
                    for cc in range(2):
                        TT("dve", modT[i], modT[i][:, c0:c0 + 4, cc],
                           P[6], P[6][:, c0 * 2:(c0 + 4) * 2].rearrange("p (a b) -> p a b", b=2)[:, :, cc],
                           smt, smt[:, mb + c0:mb + c0 + 4], ALU.add)
                stage(ld, cp)

        def mslice(i, which, col=0):
            return modT[i][:, which * 8:(which + 1) * 8, col]

        def derive(name, fn):
            fn(dv(name))

        def ln_scalars(k, li, lj, nxt, alpha=True):
            g = smt[:, SM_LNG + (li * 2 + lj) * 8: SM_LNG + (li * 2 + lj) * 8 + 8]
            b = smt[:, SM_LNB + (li * 2 + lj) * 8: SM_LNB + (li * 2 + lj) * 8 + 8]
            a = ALPHA if alpha else 1.0
            gn, bn = ("Ga" + k, "Ba" + k) if alpha else ("G" + k, "B" + k)
            TS("dve", der, dv(gn), smt, g, a, None, ALU.mult)
            TS("dve", der, dv(bn), smt, b, a, None, ALU.mult)
            if nxt is not None:
                L, sci, shi = nxt
                TS("dve", der, dv("tmp"), modT[L], mslice(L, sci), 1.0, None, ALU.add)
                TT("dve", der, dv("A" + k), smt, g, der, dv("tmp"), ALU.mult)
                TT("dve", der, dv("B" + k), smt, b, der, dv("tmp"), ALU.mult)
                TT("dve", der, dv("B" + k), der, dv("B" + k), modT[L], mslice(L, shi), ALU.add)

        mod_layer(0, range(12))

        def _d0():
            TS("dve", der, dv("A1x"), modT[0], mslice(0, 1, 0), 1.0, None, ALU.add)
            TS("dve", der, dv("cA1x"), modT[0], mslice(0, 1, 1), 1.0, None, ALU.add)
            ln_scalars("00", 0, 0, (0, 4, 3))
        stage(None, _d0)

        winb = S.carve(A, "winb", [8, 3104], BF16, at=PH)
        g0 = PH + 49664
        A["cur"] = g0
        xbuf = [S.carve(A, f"xbuf{i}", [8, 128], F32) for i in range(2)]
        ogT = S.carve(A, "ogT", [8, 512], BF16, at=g0)
        aTb = [S.carve(A, f"aT{i}", [8, 128], BF16) for i in range(2)]
        alr = S.carve(A, "alr", [128], BF16)
        spb = S.carve(A, "spb", [512], BF16)
        e_t = S.carve(A, "e_t", [512], F32)
        Est = S.carve(A, "Est", [512], F32)
        sq = S.carve(A, "sq", [4, 256], F32, at=A["cur"] - 4 * KB)
        EqT = S.carve(A, "EqT", [4, 128], F32)
        EkT = S.carve(A, "EkT", [4, 128], F32)
        kst = S.carve(A, "kst", [512], BF16)
        vb = S.carve(A, "vb", [1024], BF16)
        qin = S.carve(A, "qin", [4, 128], BF16)
        kin = S.carve(A, "kin", [4, 128], BF16)
        attT = S.carve(A, "attT", [4, 128], BF16)
        Sst = {d: S.carve(A, "S_" + d, [4, 256], F32) for d in "fb"}
        Sb = S.carve(A, "Sb", [4, 256], BF16)
        Dt = S.carve(A, "Dt", [4], F32)
        ssq = S.carve(A, "ssq", [4], F32)
        osum = S.carve(A, "osum", [4, 256], F32)
        sg = S.carve(A, "sg", [4, 256], BF16)
        og = S.carve(A, "og", [1024], BF16)
        woc = [S.carve(A, f"woc{i}", [8, 128], BF16) for i in range(2)]
        assert A["cur"] <= 210944, A["cur"]

        for kc in range(8):
            S.dma("pool", winb[:, kc, :], w_in[:, kc, :], writes=[winb], dom="winb")
        for d in "fb":
            S.add("pool", lambda e, d=d: e.memset(Sst[d].ap, 0.0), writes=[Sst[d]])

        CK, CV, CAF, CAB, CQ, CR = 0, 512, 1536, 1552, 1568, 2080
        cnt = {"x": 0}

        def tile_stage(col0, Ascale, shift, d, need_out, fin=None):
            i = cnt["x"] % 2
            cnt["x"] += 1
            xb_, at_ = xbuf[i], aTb[i]

            def ld():
                S.dma("sp", xb_.ap, xT[col0 // 128], writes=[xb_], dom=xb_.name)

            def cp():
                for kc in range(8):
                    ACT(at_, at_[:, kc, :], xb_, xb_[:, kc, :], AF.Identity, scale=Ascale[:, kc:kc + 1], bias=shift[:, kc:kc + 1], rd=[der, modT[0]])
                gla_tile(at_, d, need_out, fin)
            stage(ld, cp)

        def gla_tile(at_, d, need_out, fin=None):
            di = 0 if d == "f" else 1
            triI, triR, msk = cmt[:, 3 * di + 0, :], cmt[:, 3 * di + 1, :], cmt[:, 3 * di + 2, :]
            last = 127 if d == "f" else 0
            ca = CAF if d == "f" else CAB
            for kc in range(8):
                MM(P[0], P[0][0:16, 0:128], winb, winb[:, kc, ca:ca + 16], at_, at_[:, kc, :], kc == 0, kc == 7)
            CP("act", alr, alr[0:16, :], P[0], P[0][0:16, 0:128])
            MM(P[1], P[1].ap, alr, alr[0:16, :], wdt, wdt[0:16, di, :], True, False)
            MM(P[1], P[1].ap, ones, ones[0:1, :], bhi, bhi[0:1, di, :], False, False)
            MM(P[1], P[1].ap, ones, ones[0:1, :], blo, blo[0:1, di, :], False, True)
            ACT(e_t, e_t.ap, P[1], P[1].ap, AF.Exp, scale=-1.0)
            ACT(spb, spb.ap, e_t, e_t.ap, AF.Ln, bias=1.0)
            MM(P[2], P[2].ap, cmt, triR, spb, spb.ap, True, True)
            ACT(Est, Est.ap, P[2], P[2].ap, AF.Exp)
            for h in range(4):
                MM(P[3], P[3][:, h * 128:(h + 1) * 128], spb, spb[:, h * 128:(h + 1) * 128], cmt, triI, True, True)
            p3 = P[3].ap.rearrange("p (h t) -> p h t", h=4)
            ACT(Dt, Dt.ap, P[3], p3[:, :, last], AF.Exp)
            if need_out:
                ACT(EqT, EqT.ap, P[3], p3, AF.Exp)
                ACT(EkT, EkT.ap, P[3], p3, AF.Exp, scale=-1.0)
            for kc in range(8):
                MM(P[4], P[4].ap, at_, at_[:, kc, :], winb, winb[:, kc, CK:CK + 512], kc == 0, kc == 7)
            TT("dve", kst, kst.ap, P[4], P[4].ap, Est, Est.ap, ALU.mult)
            for hb in range(2):
                for kc in range(8):
                    MM(P[5 + hb], P[5 + hb].ap, at_, at_[:, kc, :], winb, winb[:, kc, CV + hb * 512:CV + (hb + 1) * 512], kc == 0, kc == 7)
                CP("act", vb, vb[:, hb * 512:(hb + 1) * 512], P[5 + hb], P[5 + hb].ap)
            if need_out:
                for h in range(4):
                    for kc in range(8):
                        MM(P[0], P[0][:, h * 128:(h + 1) * 128], winb, winb[:, kc, CQ + h * 128:CQ + (h + 1) * 128], at_, at_[:, kc, :], kc == 0, kc == 7)
                STT(qin, qin.ap, P[0], P[0].ap.rearrange("p (h t) -> p h t", h=4), QS, EqT, EqT.ap, ALU.mult, ALU.mult)
                for h in range(4):
                    for kc in range(8):
                        MM(P[1], P[1][:, h * 128:(h + 1) * 128], winb, winb[:, kc, CK + h * 128:CK + (h + 1) * 128], at_, at_[:, kc, :], kc == 0, kc == 7)
                TT("dve", kin, kin.ap, P[1], P[1].ap.rearrange("p (h t) -> p h t", h=4), EkT, EkT.ap, ALU.mult)
                for h in range(4):
                    MM(P[2], P[2][:, h * 128:(h + 1) * 128], kin, kin[:, h, :], qin, qin[:, h, :], True, True)
                TT("dve", attT, attT.ap, P[2], P[2].ap.rearrange("p (h t) -> p h t", h=4),
                   cmt, msk.unsqueeze(1).to_broadcast([128, 4, 128]), ALU.mult)
                for h in range(4):
                    pb = P[3 + h // 2]
                    oap = pb[:, (h % 2) * 256:(h % 2 + 1) * 256]
                    MM(pb, oap, attT, attT[:, h, :], vb, vb[:, h * 256:(h + 1) * 256], True, False)
                    MM(pb, oap, qin, qin[:, h, :], Sb, Sb[:, h, :], False, True)
                fin(P[3], P[4])
            for h in range(4):
                pb = P[5 + h // 2]
                MM(pb, pb[:, (h % 2) * 256:(h % 2 + 1) * 256], kst, kst[:, h * 128:(h + 1) * 128], vb, vb[:, h * 256:(h + 1) * 256], True, True)
            for h in range(4):
                pb = P[5 + h // 2]
                STT(Sst[d], Sst[d][:, h, :], Sst[d], Sst[d][:, h, :], Dt[:, h:h + 1], pb, pb[:, (h % 2) * 256:(h % 2 + 1) * 256],
                    ALU.mult, ALU.add, rd=[Dt])
            CP("pool", Sb, Sb.ap, Sst[d], Sst[d].ap)

        csh = mslice(0, 0, 1)
        for t in range(2):
            tile_stage(2 * NTOK + t * 128, dv("cA1x"), csh, "f", False)
        for t in (1, 0):
            tile_stage(2 * NTOK + t * 128, dv("cA1x"), csh, "b", False)
        sh1 = mslice(0, 0, 0)
        for t in range(15, -1, -1):
            tile_stage(NTOK + t * 128, dv("A1x"), sh1, "b", False)
        def fin_b(t):
            def f(pa, pb):
                CP("act", OB[t], OB[t][:, 0:512], pa, pa.ap)
                CP("act", OB[t], OB[t][:, 512:1024], pb, pb.ap)
            return f
        stage(None, lambda: CP("pool", Sb, Sb.ap, Sst["b"], Sst["b"].ap))
        for t in range(15, -1, -1):
            tile_stage(t * 128, dv("A1x"), sh1, "b", True, fin_b(t))

        mod_layer(1, range(12))

        def _d1():
            ln_scalars("01", 0, 1, (1, 1, 0))
            ln_scalars("10", 1, 0, (1, 4, 3))
            ln_scalars("11", 1, 1, None, alpha=False)
        stage(None, _d1)

        PHL = PH + 90 * KB

        def ln_block(b, k, final=False, ub2=None, nmt=None):
            h = HA[b]
            ubs, uqs, nm, tq, rs = ub2
            for m in range(8):
                ub, uq = ubs[m % 2], uqs[m % 2]
                CP("act", ub, ub.ap, h, h[:, m, :])
                ACT(uq, uq.ap, h, h[:, m, :], AF.Square)
                MM(P[0], P[0].ap, ones, ones.ap, ub, ub.ap, m == 0, m == 7)
                MM(P[1], P[1].ap, ones, ones.ap, uq, uq.ap, m == 0, m == 7)
            TS("dve", nm, nm.ap, P[0], P[0].ap, -1.0 / D, None, ALU.mult)
            TT("dve", tq, tq.ap, nm, nm.ap, nm, nm.ap, ALU.mult)
            STT(rs, rs.ap, P[1], P[1].ap, 1.0 / D, tq, tq.ap, ALU.mult, ALU.subtract)
            ACT(rs, rs.ap, rs, rs.ap, AF.Ln, bias=1e-5)
            ACT(rs, rs.ap, rs, rs.ap, AF.Exp, scale=-0.5)
            for m in range(8):
                eng = "pool" if m % 2 == 0 else "dve"
                TT(eng, h, h[:, m, :], h, h[:, m, :], nm, nm.ap, ALU.add)
                TT(eng, h, h[:, m, :], h, h[:, m, :], rs, rs.ap, ALU.mult)
            if final:
                for m in range(8):
                    TS("dve" if m % 2 else "pool", h, h[:, m, :], h, h[:, m, :], dv("G" + k, m), dv("B" + k, m), ALU.mult, ALU.add, rd=[der])
            else:
                for m in range(8):
                    ACT(HM[b], HM[b][:, m, :], h, h[:, m, :], AF.Identity, scale=dv("A" + k, m), bias=dv("B" + k, m), rd=[der])
                    TS("dve" if m % 2 else "pool", h, h[:, m, :], h, h[:, m, :], dv("Ga" + k, m), dv("Ba" + k, m), ALU.mult, ALU.add, rd=[der])

        def ln_temps(base, base2=None):
            A["cur"] = base
            ubs = [S.carve(A, f"ub{i}_{base}", [512], BF16) for i in range(2)]
            uqs = [S.carve(A, f"uq{i}_{base}", [512], BF16) for i in range(2)]
            if base2 is not None:
                A["cur"] = base2
            nm = S.carve(A, f"nm_{base}", [512], F32)
            tq = S.carve(A, f"tq_{base}", [512], F32)
            rs = S.carve(A, f"rs_{base}", [512], F32)
            return (ubs, uqs, nm, tq, rs)

        lnA = ln_temps(e_t.lo)

        stage(None, lambda: CP("pool", Sb, Sb.ap, Sst["f"], Sst["f"].ap))
        wc = [0]
        for b in range(4):
            for tt in range(4):
                t = b * 4 + tt
                hb_ = HA[b]
                at_ = aTb[cnt["x"] % 2]
                cnt["x"] += 1

                def ldc(hb_=hb_, t=t, tt=tt):
                    S.dma("sp", hb_[:, :, tt * 128:(tt + 1) * 128], xT[t],
                          writes=[hb_], dom=f"hax{t}")

                def pre(hb_=hb_, tt=tt, at_=at_):
                    for kc in range(8):
                        ACT(at_, at_[:, kc, :], hb_, hb_[:, kc, tt * 128:(tt + 1) * 128], AF.Identity,
                            scale=dv("A1x", kc), bias=sh1[:, kc:kc + 1], rd=[der, modT[0]])
                    for kc in range(8):
                        TS("pool", hb_, hb_[:, kc, tt * 128:(tt + 1) * 128], hb_, hb_[:, kc, tt * 128:(tt + 1) * 128], ALPHA, None, ALU.mult)

                def fin_c(pa, pb, t=t, tt=tt, at_=at_):
                    o3 = osum.ap
                    TT("dve", osum, osum[:, 0:2, :], pa, pa.ap.rearrange("p (h v) -> p h v", h=2), OB[t], OB[t][:, 0:512].rearrange("p (h v) -> p h v", h=2), ALU.add)
                    TT("dve", osum, osum[:, 2:4, :], pb, pb.ap.rearrange("p (h v) -> p h v", h=2), OB[t], OB[t][:, 512:1024].rearrange("p (h v) -> p h v", h=2), ALU.add)
                    TT("pool", sq, sq.ap, osum, osum.ap, osum, osum.ap, ALU.mult)
                    S.add("dve", lambda e: e.reduce_sum(ssq.ap, sq.ap, mybir.AxisListType.X), reads=[sq], writes=[ssq])
                    TS("dve", ssq, ssq.ap, ssq, ssq.ap, 1.0 / 256, None, ALU.mult)
                    ACT(ssq, ssq.ap, ssq, ssq.ap, AF.Ln, bias=1e-6)
                    ACT(ssq, ssq.ap, ssq, ssq.ap, AF.Exp, scale=-0.5)
                    for hb in range(2):
                        for kc in range(8):
                            MM(P[0 + hb], P[0 + hb].ap, at_, at_[:, kc, :], winb, winb[:, kc, CR + hb * 512:CR + (hb + 1) * 512], kc == 0, kc == 7)
                        ACT(sg, sg[:, 2 * hb:2 * hb + 2, :], P[0 + hb], P[0 + hb].ap.rearrange("p (h v) -> p h v", h=2), AF.Silu)
                    for h in range(4):
                        STT(og, og[:, h * 256:(h + 1) * 256], osum, osum[:, h, :], ssq[:, h:h + 1], sg, sg[:, h, :], ALU.mult, ALU.mult, rd=[ssq])
                    for c in range(8):
                        S.add("pe", lambda e, c=c: e.transpose(PT[:, c * 128:(c + 1) * 128], og[:, c * 128:(c + 1) * 128], cmt[:, 6, :]),
                              reads=[og, cmt], writes=[PT])
                    for c in range(8):
                        ng = smt[:, SM_NG + (c % 2):SM_NG + (c % 2) + 1]
                        ACT(ogT, ogT[:, c, tt * 128:(tt + 1) * 128], PT, PT[:, c * 128:(c + 1) * 128], AF.Identity, scale=ng, rd=[smt])

                def cpc(pre=pre, at_=at_, fin_c=fin_c):
                    pre()
                    gla_tile(at_, "f", True, fin_c)
                stage(ldc, cpc)
            for m in range(8):
                w = woc[wc[0] % 2]
                wc[0] += 1

                def ldo(w=w, m=m):
                    S.dma("pool", w.ap, w_out[m], writes=[w], dom=w.name)

                def cpo(w=w, m=m, b=b):
                    for kc in range(8):
                        MM(P[2], P[2].ap, w, w[:, kc, :], ogT, ogT[:, kc, :], kc == 0, kc == 7)
                    STT(HA[b], HA[b][:, m, :], P[2], P[2].ap, modT[0][:, 16 + m, 0:1], HA[b], HA[b][:, m, :], ALU.mult, ALU.add, rd=[modT[0]])
                stage(ldo, cpo)
            stage(None, lambda b=b: ln_block(b, "00", ub2=lnA))

        def dump_and_finish():
            run_stream()
            for b in range(4):
                S.dma("sp", outT[b], HA[b].ap, reads=[HA[b]], dom=f"out{b}")
            S.emit()
            return nc

        if stop_after == "gla":
            return dump_and_finish(), S

        A["cur"] = PH
        actb = S.carve(A, "actb", [NJ, 1024], BF16)
        ut = S.carve(A, "ut", [NFC, 512], BF16, at=PH)
        vg = [S.carve(A, f"vg{t}", [GW], BF16, at=PH + 24576 + t * 6144) for t in range(4)]
        A["cur"] = PH + 49152
        wgu = [S.carve(A, f"wgu{i}", [2, 8, 128], BF16) for i in range(2)]
        wdn = [S.carve(A, f"wdn{i}", [NJ, 128], BF16) for i in range(2)]
        sgl = [S.carve(A, f"sgl{i}", [512], F32) for i in range(2)]
        ffn_end = A["cur"]
        A["cur"] = PH + 49152
        wiu = [S.carve(A, f"wiu{i}", [2, 8, 128], BF16) for i in range(2)]
        wiv = [S.carve(A, f"wiv{i}", [8, 512], BF16) for i in range(2)]
        wom = [S.carve(A, f"wom{i}", [NFC, 128], BF16) for i in range(2)]
        wsT = S.carve(A, "wsT", [4, 128], BF16)
        Rg = S.carve(A, "Rg", [4, 128], F32)
        Bsg = S.carve(A, "Bsg", [4, 128], F32)
        tr0 = max(A["cur"], ffn_end)
        A["cur"] = tr0
        Xc = [S.carve(A, f"Xc{i}", [128], F32) for i in range(2)]
        stmp = [S.carve(A, f"stmp{i}", [512], F32) for i in range(2)]
        bst = S.carve(A, "bst", [6, 6], F32)
        mvt = S.carve(A, "mvt", [2], F32)
        rst = S.carve(A, "rst", [1], F32)
        assert A["cur"] <= 210944, A["cur"]
        lnB = ln_temps(tr0, wdt.lo)
        assert A["cur"] <= 108 * KB
        cn = {"g": 0, "d": 0, "iu": 0, "iv": 0, "om": 0, "s": 0, "x": 0, "st": 0}

        def ffn_group(li, g, k):
            blks = (2 * g, 2 * g + 1)
            for j in range(NJ):
                i = cn["g"] % 2
                cn["g"] += 1
                wg_ = wgu[i]

                def ld(wg_=wg_, j=j):
                    S.dma("pool", wg_.ap, f_gu[li, j], writes=[wg_], dom=wg_.name)

                def cp(wg_=wg_, j=j):
                    for bi, b in enumerate(blks):
                        pg, pu = P[2 * bi], P[2 * bi + 1]
                        for kc in range(8):
                            MM(pg, pg.ap, wg_, wg_[:, 0, kc, :], HM[b], HM[b][:, kc, :], kc == 0, kc == 7)
                        for kc in range(8):
                            MM(pu, pu.ap, wg_, wg_[:, 1, kc, :], HM[b], HM[b][:, kc, :], kc == 0, kc == 7)
                        s_ = sgl[cn["s"] % 2]
                        cn["s"] += 1
                        ACT(s_, s_.ap, pg, pg.ap, AF.Silu)
                        TT("dve", actb, actb[:, j, bi * 512:(bi + 1) * 512], s_, s_.ap, pu, pu.ap, ALU.mult)
                stage(ld, cp)
            for m in range(8):
                w = wdn[cn["d"] % 2]
                cn["d"] += 1

                def ld(w=w, m=m):
                    S.dma("pool", w.ap, f_d[li, m], writes=[w], dom=w.name)

                def cp(w=w, m=m):
                    for bi, b in enumerate(blks):
                        py = P[4 + bi]
                        for j in range(NJ):
                            MM(py, py.ap, w, w[:, j, :], actb, actb[:, j, bi * 512:(bi + 1) * 512], j == 0, j == NJ - 1)
                        STT(HA[b], HA[b][:, m, :], py, py.ap, modT[li][:, 40 + m, 0:1], HA[b], HA[b][:, m, :], ALU.mult, ALU.add, rd=[modT[li]])
                stage(ld, cp)
            for b in blks:
                stage(None, lambda b=b: ln_block(b, k, final=(k == "11"), ub2=lnB))

        def gmlp_consts():
            def ld():
                S.dma("pool", wsT.ap, gm_ws, writes=[wsT], dom="wsT")
                S.dma("sp", Bsg.ap, gm_bs.rearrange("p (g q) -> p g q", g=4), writes=[Bsg], dom="Bsg")

            def cp():
                MM(P[6], P[6].ap, ones, ones.ap, wsT, wsT.ap.rearrange("p g q -> p (g q)"), True, True)
                CP("dve", Rg, Rg.ap, P[6], P[6].ap.rearrange("p (g q) -> p g q", g=4))
            stage(ld, cp)

        def gmlp_block(b):
            hm = HM[b]
            for f4 in range(12):
                w = wiu[cn["iu"] % 2]
                cn["iu"] += 1

                def ld(w=w, f4=f4):
                    S.dma("pool", w.ap, gm_iu[f4], writes=[w], dom=w.name)

                def cp(w=w, f4=f4):
                    for q in range(2):
                        fc = f4 * 2 + q
                        pu = P[fc % 2]
                        for kc in range(8):
                            MM(pu, pu.ap, w, w[:, q, kc, :], hm, hm[:, kc, :], kc == 0, kc == 7)
                        ACT(ut, ut[:, fc, :], pu, pu.ap, AF.Gelu)
                stage(ld, cp)
            for cb in range(6):
                w = wiv[cn["iv"] % 2]
                cn["iv"] += 1

                def ld(w=w, cb=cb):
                    S.dma("pool", w.ap, gm_iv[cb], writes=[w], dom=w.name)

                def cp(w=w, cb=cb):
                    for t in range(4):
                        pv = P[2 + t]
                        for kc in range(8):
                            MM(pv, pv.ap, hm, hm[:, kc, t * 128:(t + 1) * 128], w, w[:, kc, :], kc == 0, kc == 7)
                        ACT(vg[t], vg[t][:, cb * 512:(cb + 1) * 512], pv, pv.ap, AF.Gelu)
                stage(ld, cp)
            stage(None, lambda: gm_mid(b))
            gm_tail(b)

        def gm_mid(b):
            hm = HM[b]
            for t in range(4):
                for cb in range(6):
                    S.add("dve", lambda e, t=t, cb=cb: e.bn_stats(bst[:, cb, :], vg[t][:, cb * 512:(cb + 1) * 512]), reads=[vg[t]], writes=[bst])
                S.add("dve", lambda e: e.bn_aggr(mvt.ap, bst.ap.rearrange("p a b -> p (a b)")), reads=[bst], writes=[mvt])
                ACT(rst, rst.ap, mvt, mvt[:, 1:2], AF.Ln, bias=1e-5)
                ACT(rst, rst.ap, rst, rst.ap, AF.Exp, scale=-0.5)
                TS("dve", vg[t], vg[t].ap, vg[t], vg[t].ap, mvt[:, 0:1], rst[:, 0:1], ALU.subtract, ALU.mult, rd=[mvt, rst])
            for fc in range(NFC):
                g = fc // 6
                pm = P[fc % 2]
                for t in range(4):
                    MM(pm, pm[:, t * 128:(t + 1) * 128], vg[t], vg[t][:, fc * 128:(fc + 1) * 128], wsT, wsT[:, g, :], True, True)
                xc = Xc[cn["x"] % 2]
                cn["x"] += 1
                STT(xc, xc.ap, Rg, Rg[:, g, :], smt[:, SM_GB + fc:SM_GB + fc + 1], Bsg, Bsg[:, g, :], ALU.mult, ALU.add, rd=[smt])
                s_ = stmp[cn["st"] % 2]
                cn["st"] += 1
                STT(s_, s_.ap.rearrange("p (t q) -> p t q", t=4), pm, pm.ap.rearrange("p (t q) -> p t q", t=4), smt[:, SM_GG + fc:SM_GG + fc + 1],
                    xc, xc.ap.unsqueeze(1).to_broadcast([128, 4, 128]), ALU.mult, ALU.add, rd=[smt])
                TT("pool", ut, ut[:, fc, :], ut, ut[:, fc, :], s_, s_.ap, ALU.mult)

        def gm_tail(b):
            for m in range(8):
                w = wom[cn["om"] % 2]
                cn["om"] += 1

                def ld(w=w, m=m):
                    S.dma("pool", w.ap, gm_out[m], writes=[w], dom=w.name)

                def cp(w=w, m=m):
                    py = P[6]
                    for fc in range(NFC):
                        MM(py, py.ap, w, w[:, fc, :], ut, ut[:, fc, :], fc == 0, fc == NFC - 1)
                    STT(HA[b], HA[b][:, m, :], py, py.ap, modT[1][:, 16 + m, 0:1], HA[b], HA[b][:, m, :], ALU.mult, ALU.add, rd=[modT[1]])
                stage(ld, cp)
            stage(None, lambda: ln_block(b, "10", ub2=lnB))

        if stop_after == "ffn0":
            for g in range(2):
                ffn_group(0, g, "01")
            return dump_and_finish(), S
        gmlp_consts()
        for g in range(2):
            ffn_group(0, g, "01")
            if stop_after == "gmlp":
                gmlp_block(2 * g)
                gmlp_block(2 * g + 1)
                continue
            gmlp_block(2 * g)
            gmlp_block(2 * g + 1)
            ffn_group(1, g, "11")
        return dump_and_finish(), S


def _consts():
    j = np.arange(128)[:, None]
    i = np.arange(128)[None, :]
    c = -1.0 / 16.0
    mats = [c * (j <= i), c * (j > i), 1.0 * (j <= i), c * (j >= i), c * (j < i), 1.0 * (j >= i), 1.0 * (j == i)]
    return np.ascontiguousarray(np.stack([m.astype(np.float32) for m in mats], axis=1))


def _fm(v, n):
    return np.asarray(v, np.float32).reshape(n, 128).T


def make_in_maps(x, c, ctx, c_ctx, mod_w, mod_b, ln_g, ln_b, gla_w_in, gla_w_decay, gla_b_decay,
                 gla_norm_g, gla_w_out, gm_w_in, gm_ln_g, gm_ln_b, gm_w_s, gm_b_s, gm_w_out,
                 ffn_w_gate, ffn_w_up, ffn_w_down):
    f = lambda a: np.ascontiguousarray(np.asarray(a, dtype=np.float32))
    x, c, ctx, c_ctx = f(x), f(c), f(ctx), f(c_ctx)
    cmv = _consts()
    w_in0 = f(gla_w_in[0])
    w_in1 = w_in0.copy()
    w_in1[:, 1536:1552] = w_in0[:, 1552:1568]
    w_in1[:, 1552:1568] = w_in0[:, 1536:1552]
    wd = f(gla_w_decay[0])
    bd = f(gla_b_decay[0])
    ws = f(gm_w_s[0])
    bs = f(gm_b_s[0])
    C = np.ascontiguousarray
    gmi = f(gm_w_in[0])
    fg = f(ffn_w_gate).reshape(2, 8, 128, NJ, 128).transpose(0, 3, 2, 1, 4)
    fu = f(ffn_w_up).reshape(2, 8, 128, NJ, 128).transpose(0, 3, 2, 1, 4)
    shared = {
        "cm": cmv,
        "mod_w": C(f(mod_w).reshape(2, 8, 128, 12, 512).transpose(0, 3, 2, 1, 4)),
        "w_out": C(f(gla_w_out[0]).reshape(8, 128, 8, 128).transpose(2, 1, 0, 3)),
        "gm_iu": C(gmi[:, :GW].reshape(8, 128, 12, 2, 128).transpose(2, 1, 3, 0, 4)),
        "gm_iv": C(gmi[:, GW:].reshape(8, 128, 6, 512).transpose(2, 1, 0, 3)),
        "gm_out": C(f(gm_w_out[0]).reshape(NFC, 128, 8, 128).transpose(2, 1, 0, 3)),
        "f_gu": C(np.stack([fg, fu], axis=3)),
        "f_d": C(f(ffn_w_down).reshape(2, NJ, 128, 8, 128).transpose(0, 3, 2, 1, 4)),
    }
    maps = []
    for core in range(8):
        b, half = core // 2, core % 2
        xs, cs = x[b], ctx[b]
        if half:
            xs, cs = xs[::-1], cs[::-1]
        xall = np.concatenate([xs, cs], axis=0)
        xall = np.ascontiguousarray(xall.reshape(34, 128, 8, 128).transpose(0, 3, 2, 1))
        smv = np.zeros((128, SM_N), np.float32)
        smv[:, SM_CC:SM_CC + 16] = np.stack([_fm(c[b], 8), _fm(c_ctx, 8)], axis=-1).reshape(128, 16)
        smv[:, SM_MB0:SM_MB0 + 48] = _fm(mod_b[0], 48)
        smv[:, SM_MB1:SM_MB1 + 48] = _fm(mod_b[1], 48)
        for li in range(2):
            for lj in range(2):
                o = (li * 2 + lj) * 8
                smv[:, SM_LNG + o:SM_LNG + o + 8] = _fm(ln_g[li, lj], 8)
                smv[:, SM_LNB + o:SM_LNB + o + 8] = _fm(ln_b[li, lj], 8)
        smv[:, SM_NG:SM_NG + 2] = _fm(gla_norm_g[0], 2)
        smv[:, SM_GG:SM_GG + 24] = _fm(gm_ln_g[0], 24)
        smv[:, SM_GB:SM_GB + 24] = _fm(gm_ln_b[0], 24)
        wdd = wd[::-1] if half else wd
        bdd = bd[::-1] if half else bd
        wss = ws[:, ::-1, ::-1] if half else ws
        bss = bs[::-1] if half else bs
        m = dict(shared)
        m.update({
            "xT": xall, "sm": smv,
            "w_in": np.ascontiguousarray((w_in1 if half else w_in0).reshape(8, 128, 3104).transpose(1, 0, 2)),
            "wdec": np.ascontiguousarray(wdd.transpose(1, 0, 2)),
            "bdec": np.ascontiguousarray(bdd.reshape(1, 2, 512)),
            "gm_ws": np.ascontiguousarray(wss.transpose(2, 0, 1)),
            "gm_bs": np.ascontiguousarray(np.broadcast_to(bss.T.reshape(1, 512), (128, 512))),
        })
        maps.append(m)
    return maps


def assemble(results):
    out = np.empty((4, 4096, D), np.float32)
    for core in range(8):
        b, half = core // 2, core % 2
        o = results[core]["outT"].transpose(0, 3, 2, 1).reshape(NTOK, D)
        if half:
            out[b, 2048:] = o[::-1]
        else:
            out[b, :2048] = o
    return out


_NC = {}


def kernel(**inputs):
    if "nc" not in _NC:
        _NC["nc"] = build_program()[0]
    maps = make_in_maps(**inputs)
    res = run_bass_kernel_spmd(_NC["nc"], maps, core_ids=list(range(8)))
    return assemble(res.results)
```

```python
import numpy as np
import concourse.bass as bass
import concourse.mybir as mybir
from contextlib import ExitStack
from concourse.bass_utils import run_bass_kernel_spmd

F32 = mybir.dt.float32
BF16 = mybir.dt.bfloat16
AF = mybir.ActivationFunctionType
ALU = mybir.AluOpType

ENGINES = ("pe", "act", "dve", "pool", "sp")


class Tile:
    __slots__ = ("name", "ap", "space", "lo", "hi", "writers", "readers", "overlaps")

    def __init__(self, name, ap, space, lo, hi):
        self.name = name
        self.ap = ap
        self.space = space
        self.lo = lo
        self.hi = hi
        self.writers = {}
        self.readers = {}
        self.overlaps = []

    def __getitem__(self, k):
        return self.ap[k]


class Op:
    __slots__ = ("eng", "fn", "reads", "writes", "dom", "deps", "signal", "idx", "inc", "waits", "vc", "label")

    def __init__(self, eng, fn, reads, writes, dom, inc, label):
        self.eng = eng
        self.fn = fn
        self.reads = reads
        self.writes = writes
        self.dom = dom
        self.deps = []
        self.signal = False
        self.idx = 0
        self.inc = inc
        self.waits = []
        self.vc = None
        self.label = label


class Sched:
    def __init__(self, nc, stack):
        self.nc = nc
        self.stack = stack
        self.ops = []
        self.tiles_by_space = {}
        self.n_alloc = 0
        self.dom_last = {}

    def arena(self, name, nbytes):
        assert nbytes % 4 == 0
        t = self.stack.enter_context(self.nc.sbuf_tensor(name, [128, nbytes // 4], F32))
        return {"name": name, "t": t, "nbytes": nbytes, "cur": 0}

    def carve(self, arena, name, shape, dtype, at=None):
        esz = 2 if dtype == BF16 else 4
        n = int(np.prod(shape))
        nb = n * esz
        nb4 = (nb + 3) // 4 * 4
        if at is None:
            at = arena["cur"]
            arena["cur"] = at + nb4
        assert at % 4 == 0 and at + nb4 <= arena["nbytes"], (name, at, nb4, arena["nbytes"])
        ap = arena["t"][:, at // 4:(at + nb4) // 4]
        if dtype != F32:
            ap = ap.bitcast(dtype)
        ap = ap[:, 0:n]
        if len(shape) == 2:
            ap = ap.rearrange("p (a b) -> p a b", a=shape[0])
        elif len(shape) == 3:
            ap = ap.rearrange("p (a b c) -> p a b c", a=shape[0], b=shape[1])
        return self._reg(Tile(name, ap, arena["name"], at, at + nb4))

    def psum(self, name, shape, dtype=F32):
        t = self.stack.enter_context(self.nc.psum_tensor(name, [128] + list(shape), dtype))
        self.n_alloc += 1
        return self._reg(Tile(name, t[:], "psum_" + name, 0, 1))

    def sbuf(self, name, shape, dtype=F32, parts=128):
        t = self.stack.enter_context(self.nc.sbuf_tensor(name, [parts] + list(shape), dtype))
        return self._reg(Tile(name, t[:], "sb_" + name, 0, 1))

    def view(self, name, ap, parents):
        t = Tile(name, ap, None, 0, 0)
        for p in parents:
            t.overlaps.append(p)
            p.overlaps.append(t)
        return t

    def _reg(self, t):
        lst = self.tiles_by_space.setdefault(t.space, [])
        for u in lst:
            if u.lo < t.hi and t.lo < u.hi:
                u.overlaps.append(t)
                t.overlaps.append(u)
        lst.append(t)
        return t

    def add(self, eng, fn, reads=(), writes=(), dom=None, inc=1, label=""):
        op = Op(eng, fn, list(reads), list(writes), dom or eng, inc, label)
        deps = {}

        def need(p):
            if p is None or p is op:
                return
            deps[id(p)] = p

        for t in op.reads:
            for u in [t] + t.overlaps:
                for p in u.writers.values():
                    need(p)
        for t in op.writes:
            for u in [t] + t.overlaps:
                for p in u.writers.values():
                    need(p)
                for p in u.readers.values():
                    need(p)
        out = []
        for p in list(deps.values()):
            if p.dom.startswith("dma:"):
                if p.dom == op.dom:
                    continue
                p = self.dom_last[p.dom]
                if any(q is p for q in out):
                    continue
            if p.dom == op.dom and op.dom in ENGINES:
                if op.eng == "pe":
                    continue
                raw = any((p in u.writers.values()) for t in op.reads for u in [t] + t.overlaps)
                if not raw:
                    continue
            out.append(p)
        op.deps = out
        for p in out:
            p.signal = True
        for t in op.reads:
            t.readers[op.dom] = op
        for t in op.writes:
            t.writers = {op.dom: op}
            t.readers = {}
        if op.dom.startswith("dma:"):
            self.dom_last[op.dom] = op
        self.ops.append(op)
        return op

    def mm(self, out_t, out_ap, lhsT_t, lhsT_ap, rhs_t, rhs_ap, start=True, stop=True, extra_reads=()):
        return self.add("pe", lambda e: e.matmul(out_ap, lhsT_ap, rhs_ap, start=start, stop=stop),
                        reads=[lhsT_t, rhs_t] + list(extra_reads), writes=[out_t], label="mm")

    def dma(self, queue, out_ap, in_ap, reads=(), writes=(), dom=None, label="dma"):
        assert dom is not None
        return self.add(queue, lambda e: e.dma_start(out=out_ap, in_=in_ap),
                        reads=reads, writes=writes, dom="dma:" + dom, inc=16, label=label)

    def emit(self, final_wait_doms=()):
        nc = self.nc
        counts = {}
        for op in self.ops:
            if op.dom.startswith("dma:"):
                op.signal = True
            if op.signal:
                counts[op.dom] = counts.get(op.dom, 0) + 1
                op.idx = counts[op.dom]
        stream_vc = {e: {} for e in ENGINES}
        for op in self.ops:
            svc = stream_vc[op.eng]
            waits = []
            for p in op.deps:
                if svc.get(p.dom, 0) >= p.idx:
                    continue
                waits.append((p.dom, p.idx))
                for d, k in p.vc.items():
                    if svc.get(d, 0) < k:
                        svc[d] = k
            wm = {}
            for d, k in waits:
                if wm.get(d, 0) < k:
                    wm[d] = k
            op.waits = [(d, k) for d, k in wm.items() if True]
            if op.signal:
                vc = dict(svc)
                vc[op.dom] = op.idx
                op.vc = vc
            else:
                op.vc = None
        finals = [(d, k) for d, k in counts.items() if d.startswith("dma:")]
        sems = {}
        for d in counts:
            sems[d] = self.stack.enter_context(nc.semaphore("s_" + d.replace(":", "_")))
        self.n_sems = len(sems)
        per_eng = {e: [op for op in self.ops if op.eng == e] for e in ENGINES}
        self.stats = {e: len(per_eng[e]) for e in ENGINES}
        self.stats["waits"] = sum(len(op.waits) for op in self.ops)

        def run(eng_obj, ename):
            for op in per_eng[ename]:
                for d, k in op.waits:
                    mult = 16 if d.startswith("dma:") else 1
                    eng_obj.wait_ge(sems[d], k * mult)
                ins = op.fn(eng_obj)
                if op.signal:
                    ins.then_inc(sems[op.dom], op.inc)
            if ename == "sp":
                for d, k in finals:
                    eng_obj.wait_ge(sems[d], k * 16)

        with nc.Block() as block:
            @block.tensor
            def _(e):
                run(e, "pe")

            @block.scalar
            def _(e):
                run(e, "act")

            @block.vector
            def _(e):
                run(e, "dve")

            @block.gpsimd
            def _(e):
                run(e, "pool")

            @block.sync
            def _(e):
                run(e, "sp")


D = 1024
NTOK = 2048
NBLK = 4
ALPHA = (2 * 2) ** 0.25
HID = 2816
NJ = HID // 128
GW = 3072
NFC = GW // 128
QS = 128 ** -0.5
SM_CC, SM_MB0, SM_MB1, SM_LNG, SM_LNB, SM_NG, SM_GG, SM_GB, SM_N = 0, 16, 64, 112, 144, 176, 178, 202, 226


def build_program(stop_after=None):
    nc = bass.Bass("TRN2", target_bir_lowering=False)
    dt_ = nc.dram_tensor
    xT = dt_("xT", [34, 128, 8, 128], F32, kind="ExternalInput").ap()
    sm = dt_("sm", [128, SM_N], F32, kind="ExternalInput").ap()
    cm = dt_("cm", [128, 7, 128], F32, kind="ExternalInput").ap()
    mod_w = dt_("mod_w", [2, 12, 128, 8, 512], F32, kind="ExternalInput").ap()
    w_in = dt_("w_in", [128, 8, 3104], F32, kind="ExternalInput").ap()
    wdec = dt_("wdec", [16, 2, 512], F32, kind="ExternalInput").ap()
    bdec = dt_("bdec", [1, 2, 512], F32, kind="ExternalInput").ap()
    w_out = dt_("w_out", [8, 128, 8, 128], F32, kind="ExternalInput").ap()
    gm_iu = dt_("gm_iu", [12, 128, 2, 8, 128], F32, kind="ExternalInput").ap()
    gm_iv = dt_("gm_iv", [6, 128, 8, 512], F32, kind="ExternalInput").ap()
    gm_ws = dt_("gm_ws", [128, 4, 128], F32, kind="ExternalInput").ap()
    gm_bs = dt_("gm_bs", [128, 4 * 128], F32, kind="ExternalInput").ap()
    gm_out = dt_("gm_out", [8, 128, NFC, 128], F32, kind="ExternalInput").ap()
    f_gu = dt_("f_gu", [2, NJ, 128, 2, 8, 128], F32, kind="ExternalInput").ap()
    f_d = dt_("f_d", [2, 8, 128, NJ, 128], F32, kind="ExternalInput").ap()
    outT = dt_("outT", [4, 128, 8, 512], F32, kind="ExternalOutput").ap()

    st = ExitStack()
    with st:
        S = Sched(nc, st)
        A = S.arena("A", 210944)
        KB = 1024
        HA = [S.carve(A, f"ha{b}", [8, 512], F32, at=b * 16 * KB) for b in range(4)]
        HM = [S.carve(A, f"hm{b}", [8, 512], BF16, at=64 * KB + b * 8 * KB) for b in range(4)]
        OB = [S.carve(A, f"ob{t}", [1024], BF16, at=64 * KB + t * 2 * KB) for t in range(16)]
        A["cur"] = 96 * KB
        smt = S.carve(A, "smt", [SM_N], F32)
        cmt = S.carve(A, "cmt", [7, 128], BF16)
        ones = S.carve(A, "ones", [128], BF16)
        scb = S.carve(A, "scb", [8, 2], BF16)
        modT = [S.carve(A, f"modT{i}", [48, 2], F32) for i in range(2)]
        der = S.carve(A, "der", [24, 8], F32)
        wdt = S.carve(A, "wdt", [2, 512], BF16)
        bhi = S.carve(A, "bhi", [2, 512], BF16)
        blo = S.carve(A, "blo", [2, 512], BF16)
        btmp = S.carve(A, "btmp", [2, 512], F32, at=32 * KB)
        btmp2 = S.carve(A, "btmp2", [2, 512], F32, at=36 * KB)
        assert A["cur"] <= 108 * KB, A["cur"]
        PH = 108 * KB
        P = [S.psum(f"P{i}", [512]) for i in range(7)]
        PT = S.psum("PT", [1024], BF16)

        DER = {n: i for i, n in enumerate([
            "A1x", "cA1x", "Ga00", "Ba00", "A00", "B00", "Ga01", "Ba01", "A01", "B01",
            "Ga10", "Ba10", "A10", "B10", "G11", "B11", "tmp"])}

        def dv(name, m=None):
            i = DER[name]
            return der[:, i, :] if m is None else der[:, i, m:m + 1]

        def smv(off, n):
            return smt[:, off:off + n]

        def ACT(out_t, out_ap, in_t, in_ap, func, scale=1.0, bias=0.0, rd=()):
            S.add("act", lambda e: e.activation(out_ap, in_ap, func, bias=bias, scale=scale),
                  reads=[in_t] + list(rd), writes=[out_t])

        def TT(eng, out_t, out_ap, a_t, a_ap, b_t, b_ap, op):
            S.add(eng, lambda e: e.tensor_tensor(out_ap, a_ap, b_ap, op), reads=[a_t, b_t], writes=[out_t])

        def TS(eng, out_t, out_ap, in_t, in_ap, s1, s2, op0, op1=None, rd=()):
            if op1 is None:
                S.add(eng, lambda e: e.tensor_scalar(out_ap, in_ap, s1, None, op0), reads=[in_t] + list(rd), writes=[out_t])
            else:
                S.add(eng, lambda e: e.tensor_scalar(out_ap, in_ap, s1, s2, op0, op1), reads=[in_t] + list(rd), writes=[out_t])

        def STT(out_t, out_ap, a_t, a_ap, scal, b_t, b_ap, op0, op1, rd=()):
            S.add("dve", lambda e: e.scalar_tensor_tensor(out_ap, a_ap, scal, b_ap, op0, op1),
                  reads=[a_t, b_t] + list(rd), writes=[out_t])

        def CP(eng, out_t, out_ap, in_t, in_ap):
            if eng == "act":
                S.add("act", lambda e: e.copy(out_ap, in_ap), reads=[in_t], writes=[out_t])
            else:
                S.add(eng, lambda e: e.tensor_copy(out_ap, in_ap), reads=[in_t], writes=[out_t])

        def MM(out_t, out_ap, l_t, l_ap, r_t, r_ap, start, stop):
            S.mm(out_t, out_ap, l_t, l_ap, r_t, r_ap, start=start, stop=stop)

        stages = []

        def stage(load, compute):
            stages.append((load, compute))

        def run_stream():
            if stages:
                if stages[0][0]:
                    stages[0][0]()
            for si, (ld, cp) in enumerate(stages):
                if si + 1 < len(stages) and stages[si + 1][0]:
                    stages[si + 1][0]()
                if cp:
                    cp()

        S.dma("sp", smt.ap, sm, writes=[smt], dom="smt")
        S.dma("pool", cmt.ap, cm, writes=[cmt], dom="cmt")
        S.dma("pool", wdt[0:16], wdec, writes=[wdt], dom="wdt")
        S.dma("sp", btmp[0:1], bdec, writes=[btmp], dom="btmp")
        S.add("dve", lambda e: e.memset(ones.ap, 1.0), writes=[ones])
        CP("dve", bhi, bhi[0:1], btmp, btmp[0:1])
        CP("dve", btmp2, btmp2[0:1], bhi, bhi[0:1])
        TT("dve", btmp2, btmp2[0:1], btmp, btmp[0:1], btmp2, btmp2[0:1], ALU.subtract)
        CP("dve", blo, blo[0:1], btmp2, btmp2[0:1])
        ACT(scb, scb.ap, smt, smt[:, SM_CC:SM_CC + 16].rearrange("p (k c) -> p k c", c=2), AF.Silu)

        mwb = [S.carve(A, f"mwb{i}", [8, 512], BF16, at=48 * KB + i * 8 * KB) for i in range(2)]
        mcnt = [0]

        def mod_layer(i, pieces):
            for j in pieces:
                w = mwb[mcnt[0] % 2]
                mcnt[0] += 1

                def ld(w=w, j=j):
                    S.dma("pool", w.ap, mod_w[i, j],
                          writes=[w], dom=w.name)

                def cp(w=w, j=j):
                    for m in range(4):
                        col = (j * 4 + m) * 2
                        for kc in range(8):
                            MM(P[6], P[6][:, col:col + 2], w, w[:, kc, m * 128:(m + 1) * 128], scb, scb[:, kc, :],
                               kc == 0, kc == 7)
                    c0 = j * 4
                    mb = SM_MB0 if i == 0 else SM_MB1
                    for cc in range(2):
                        TT("dve", modT[i], modT[i][:, c0:c0 + 4, cc],
                           P[6], P[6][:, c0 * 2:(c0 + 4) * 2].rearrange("p (a b) -> p a b", b=2)[:, :, cc],
                           smt, smt[:, mb + c0:mb + c0 + 4], ALU.add)
                stage(ld, cp)

        def mslice(i, which, col=0):
            return modT[i][:, which * 8:(which + 1) * 8, col]

        def derive(name, fn):
            fn(dv(name))

        def ln_scalars(k, li, lj, nxt, alpha=True):
            g = smt[:, SM_LNG + (li * 2 + lj) * 8: SM_LNG + (li * 2 + lj) * 8 + 8]
            b = smt[:, SM_LNB + (li * 2 + lj) * 8: SM_LNB + (li * 2 + lj) * 8 + 8]
            a = ALPHA if alpha else 1.0
            gn, bn = ("Ga" + k, "Ba" + k) if alpha else ("G" + k, "B" + k)
            TS("dve", der, dv(gn), smt, g, a, None, ALU.mult)
            TS("dve", der, dv(bn), smt, b, a, None, ALU.mult)
            if nxt is not None:
                L, sci, shi = nxt
                TS("dve", der, dv("tmp"), modT[L], mslice(L, sci), 1.0, None, ALU.add)
                TT("dve", der, dv("A" + k), smt, g, der, dv("tmp"), ALU.mult)
                TT("dve", der, dv("B" + k), smt, b, der, dv("tmp"), ALU.mult)
                TT("dve", der, dv("B" + k), der, dv("B" + k), modT[L], mslice(L, shi), ALU.add)

        mod_layer(0, range(12))

        def _d0():
            TS("dve", der, dv("A1x"), modT[0], mslice(0, 1, 0), 1.0, None, ALU.add)
            TS("dve", der, dv("cA1x"), modT[0], mslice(0, 1, 1), 1.0, None, ALU.add)
            ln_scalars("00", 0, 0, (0, 4, 3))
        stage(None, _d0)

        winb = S.carve(A, "winb", [8, 3104], BF16, at=PH)
        g0 = PH + 49664
        A["cur"] = g0
        xbuf = [S.carve(A, f"xbuf{i}", [8, 128], F32) for i in range(2)]
        ogT = S.carve(A, "ogT", [8, 512], BF16, at=g0)
        aTb = [S.carve(A, f"aT{i}", [8, 128], BF16) for i in range(2)]
        alr = S.carve(A, "alr", [128], BF16)
        spb = S.carve(A, "spb", [512], BF16)
        e_t = S.carve(A, "e_t", [512], F32)
        Est = S.carve(A, "Est", [512], F32)
        sq = S.carve(A, "sq", [4, 256], F32, at=A["cur"] - 4 * KB)
        EqT = S.carve(A, "EqT", [4, 128], F32)
        EkT = S.carve(A, "EkT", [4, 128], F32)
        kst = S.carve(A, "kst", [512], BF16)
        vb = S.carve(A, "vb", [1024], BF16)
        qin = S.carve(A, "qin", [4, 128], BF16)
        kin = S.carve(A, "kin", [4, 128], BF16)
        attT = S.carve(A, "attT", [4, 128], BF16)
        Sst = {d: S.carve(A, "S_" + d, [4, 256], F32) for d in "fb"}
        Sb = S.carve(A, "Sb", [4, 256], BF16)
        Dt = S.carve(A, "Dt", [4], F32)
        ssq = S.carve(A, "ssq", [4], F32)
        osum = S.carve(A, "osum", [4, 256], F32)
        sg = S.carve(A, "sg", [4, 256], BF16)
        og = S.carve(A, "og", [1024], BF16)
        woc = [S.carve(A, f"woc{i}", [8, 128], BF16) for i in range(2)]
        assert A["cur"] <= 210944, A["cur"]

        for kc in range(8):
            S.dma("pool", winb[:, kc, :], w_in[:, kc, :], writes=[winb], dom="winb")
        for d in "fb":
            S.add("pool", lambda e, d=d: e.memset(Sst[d].ap, 0.0), writes=[Sst[d]])

        CK, CV, CAF, CAB, CQ, CR = 0, 512, 1536, 1552, 1568, 2080
        cnt = {"x": 0}

        def tile_stage(col0, Ascale, shift, d, need_out, fin=None):
            i = cnt["x"] % 2
            cnt["x"] += 1
            xb_, at_ = xbuf[i], aTb[i]

            def ld():
                S.dma("sp", xb_.ap, xT[col0 // 128], writes=[xb_], dom=xb_.name)

            def cp():
                for kc in range(8):
                    ACT(at_, at_[:, kc, :], xb_, xb_[:, kc, :], AF.Identity, scale=Ascale[:, kc:kc + 1], bias=shift[:, kc:kc + 1], rd=[der, modT[0]])
                gla_tile(at_, d, need_out, fin)
            stage(ld, cp)

        def gla_tile(at_, d, need_out, fin=None):
            di = 0 if d == "f" else 1
            triI, triR, msk = cmt[:, 3 * di + 0, :], cmt[:, 3 * di + 1, :], cmt[:, 3 * di + 2, :]
            last = 127 if d == "f" else 0
            ca = CAF if d == "f" else CAB
            for kc in range(8):
                MM(P[0], P[0][0:16, 0:128], winb, winb[:, kc, ca:ca + 16], at_, at_[:, kc, :], kc == 0, kc == 7)
            CP("act", alr, alr[0:16, :], P[0], P[0][0:16, 0:128])
            for kc in range(8):
                MM(P[4], P[4].ap, at_, at_[:, kc, :], winb, winb[:, kc, CK:CK + 512], kc == 0, kc == 7)
            for hb in range(2):
                for kc in range(8):
                    MM(P[5 + hb], P[5 + hb].ap, at_, at_[:, kc, :], winb, winb[:, kc, CV + hb * 512:CV + (hb + 1) * 512], kc == 0, kc == 7)
                CP("act", vb, vb[:, hb * 512:(hb + 1) * 512], P[5 + hb], P[5 + hb].ap)
            MM(P[1], P[1].ap, alr, alr[0:16, :], wdt, wdt[0:16, di, :], True, False)
            MM(P[1], P[1].ap, ones, ones[0:1, :], bhi, bhi[0:1, di, :], False, False)
            MM(P[1], P[1].ap, ones, ones[0:1, :], blo, blo[0:1, di, :], False, True)
            ACT(e_t, e_t.ap, P[1], P[1].ap, AF.Exp, scale=-1.0)
            ACT(spb, spb.ap, e_t, e_t.ap, AF.Ln, bias=1.0)
            if need_out:
                for h in range(4):
                    for kc in range(8):
                        MM(P[0], P[0][:, h * 128:(h + 1) * 128], winb, winb[:, kc, CQ + h * 128:CQ + (h + 1) * 128], at_, at_[:, kc, :], kc == 0, kc == 7)
            MM(P[2], P[2].ap, cmt, triR, spb, spb.ap, True, True)
            for h in range(4):
                MM(P[3], P[3][:, h * 128:(h + 1) * 128], spb, spb[:, h * 128:(h + 1) * 128], cmt, triI, True, True)
            p3 = P[3].ap.rearrange("p (h t) -> p h t", h=4)
            ACT(Est, Est.ap, P[2], P[2].ap, AF.Exp)
            ACT(Dt, Dt.ap, P[3], p3[:, :, last], AF.Exp)
            if need_out:
                ACT(EqT, EqT.ap, P[3], p3, AF.Exp)
                ACT(EkT, EkT.ap, P[3], p3, AF.Exp, scale=-1.0)
            TT("dve", kst, kst.ap, P[4], P[4].ap, Est, Est.ap, ALU.mult)
            if need_out:
                STT(qin, qin.ap, P[0], P[0].ap.rearrange("p (h t) -> p h t", h=4), QS, EqT, EqT.ap, ALU.mult, ALU.mult)
                for h in range(4):
                    for kc in range(8):
                        MM(P[1], P[1][:, h * 128:(h + 1) * 128], winb, winb[:, kc, CK + h * 128:CK + (h + 1) * 128], at_, at_[:, kc, :], kc == 0, kc == 7)
                TT("dve", kin, kin.ap, P[1], P[1].ap.rearrange("p (h t) -> p h t", h=4), EkT, EkT.ap, ALU.mult)
                for h in range(4):
                    MM(P[2], P[2][:, h * 128:(h + 1) * 128], kin, kin[:, h, :], qin, qin[:, h, :], True, True)
                TT("dve", attT, attT.ap, P[2], P[2].ap.rearrange("p (h t) -> p h t", h=4),
                   cmt, msk.unsqueeze(1).to_broadcast([128, 4, 128]), ALU.mult)
                for h in range(4):
                    pb = P[3 + h // 2]
                    oap = pb[:, (h % 2) * 256:(h % 2 + 1) * 256]
                    MM(pb, oap, attT, attT[:, h, :], vb, vb[:, h * 256:(h + 1) * 256], True, False)
                    MM(pb, oap, qin, qin[:, h, :], Sb, Sb[:, h, :], False, True)
                fin(P[3], P[4])
            for h in range(4):
                pb = P[5 + h // 2]
                MM(pb, pb[:, (h % 2) * 256:(h % 2 + 1) * 256], kst, kst[:, h * 128:(h + 1) * 128], vb, vb[:, h * 256:(h + 1) * 256], True, True)
            for h in range(4):
                pb = P[5 + h // 2]
                STT(Sst[d], Sst[d][:, h, :], Sst[d], Sst[d][:, h, :], Dt[:, h:h + 1], pb, pb[:, (h % 2) * 256:(h % 2 + 1) * 256],
                    ALU.mult, ALU.add, rd=[Dt])
            CP("pool", Sb, Sb.ap, Sst[d], Sst[d].ap)

        csh = mslice(0, 0, 1)
        for t in range(2):
            tile_stage(2 * NTOK + t * 128, dv("cA1x"), csh, "f", False)
        for t in (1, 0):
            tile_stage(2 * NTOK + t * 128, dv("cA1x"), csh, "b", False)
        sh1 = mslice(0, 0, 0)
        for t in range(15, -1, -1):
            tile_stage(NTOK + t * 128, dv("A1x"), sh1, "b", False)
        def fin_b(t):
            def f(pa, pb):
                CP("act", OB[t], OB[t][:, 0:512], pa, pa.ap)
                CP("act", OB[t], OB[t][:, 512:1024], pb, pb.ap)
            return f
        stage(None, lambda: CP("pool", Sb, Sb.ap, Sst["b"], Sst["b"].ap))
        for t in range(15, -1, -1):
            tile_stage(t * 128, dv("A1x"), sh1, "b", True, fin_b(t))

        mod_layer(1, range(12))

        def _d1():
            ln_scalars("01", 0, 1, (1, 1, 0))
            ln_scalars("10", 1, 0, (1, 4, 3))
            ln_scalars("11", 1, 1, None, alpha=False)
        stage(None, _d1)

        PHL = PH + 90 * KB

        def ln_block(b, k, final=False, ub2=None, nmt=None):
            h = HA[b]
            ubs, uqs, nm, tq, rs = ub2
            for m in range(8):
                ub, uq = ubs[m % 2], uqs[m % 2]
                CP("act", ub, ub.ap, h, h[:, m, :])
                ACT(uq, uq.ap, h, h[:, m, :], AF.Square)
                MM(P[0], P[0].ap, ones, ones.ap, ub, ub.ap, m == 0, m == 7)
                MM(P[1], P[1].ap, ones, ones.ap, uq, uq.ap, m == 0, m == 7)
            TS("dve", nm, nm.ap, P[0], P[0].ap, -1.0 / D, None, ALU.mult)
            TT("dve", tq, tq.ap, nm, nm.ap, nm, nm.ap, ALU.mult)
            STT(rs, rs.ap, P[1], P[1].ap, 1.0 / D, tq, tq.ap, ALU.mult, ALU.subtract)
            ACT(rs, rs.ap, rs, rs.ap, AF.Ln, bias=1e-5)
            ACT(rs, rs.ap, rs, rs.ap, AF.Exp, scale=-0.5)
            for m in range(8):
                eng = "pool" if m % 2 == 0 else "dve"
                TT(eng, h, h[:, m, :], h, h[:, m, :], nm, nm.ap, ALU.add)
                TT(eng, h, h[:, m, :], h, h[:, m, :], rs, rs.ap, ALU.mult)
            if final:
                for m in range(8):
                    TS("dve" if m % 2 else "pool", h, h[:, m, :], h, h[:, m, :], dv("G" + k, m), dv("B" + k, m), ALU.mult, ALU.add, rd=[der])
            else:
                for m in range(8):
                    ACT(HM[b], HM[b][:, m, :], h, h[:, m, :], AF.Identity, scale=dv("A" + k, m), bias=dv("B" + k, m), rd=[der])
                    TS("dve" if m % 2 else "pool", h, h[:, m, :], h, h[:, m, :], dv("Ga" + k, m), dv("Ba" + k, m), ALU.mult, ALU.add, rd=[der])

        def ln_temps(base, base2=None):
            A["cur"] = base
            ubs = [S.carve(A, f"ub{i}_{base}", [512], BF16) for i in range(2)]
            uqs = [S.carve(A, f"uq{i}_{base}", [512], BF16) for i in range(2)]
            if base2 is not None:
                A["cur"] = base2
            nm = S.carve(A, f"nm_{base}", [512], F32)
            tq = S.carve(A, f"tq_{base}", [512], F32)
            rs = S.carve(A, f"rs_{base}", [512], F32)
            return (ubs, uqs, nm, tq, rs)

        lnA = ln_temps(e_t.lo)

        stage(None, lambda: CP("pool", Sb, Sb.ap, Sst["f"], Sst["f"].ap))
        wc = [0]
        for b in range(4):
            for tt in range(4):
                t = b * 4 + tt
                hb_ = HA[b]
                at_ = aTb[cnt["x"] % 2]
                cnt["x"] += 1

                def ldc(hb_=hb_, t=t, tt=tt):
                    S.dma("sp", hb_[:, :, tt * 128:(tt + 1) * 128], xT[t],
                          writes=[hb_], dom=f"hax{t}")

                def pre(hb_=hb_, tt=tt, at_=at_):
                    for kc in range(8):
                        ACT(at_, at_[:, kc, :], hb_, hb_[:, kc, tt * 128:(tt + 1) * 128], AF.Identity,
                            scale=dv("A1x", kc), bias=sh1[:, kc:kc + 1], rd=[der, modT[0]])
                    for kc in range(8):
                        TS("pool", hb_, hb_[:, kc, tt * 128:(tt + 1) * 128], hb_, hb_[:, kc, tt * 128:(tt + 1) * 128], ALPHA, None, ALU.mult)

                def fin_c(pa, pb, t=t, tt=tt, at_=at_):
                    o3 = osum.ap
                    TT("dve", osum, osum[:, 0:2, :], pa, pa.ap.rearrange("p (h v) -> p h v", h=2), OB[t], OB[t][:, 0:512].rearrange("p (h v) -> p h v", h=2), ALU.add)
                    TT("dve", osum, osum[:, 2:4, :], pb, pb.ap.rearrange("p (h v) -> p h v", h=2), OB[t], OB[t][:, 512:1024].rearrange("p (h v) -> p h v", h=2), ALU.add)
                    TT("pool", sq, sq.ap, osum, osum.ap, osum, osum.ap, ALU.mult)
                    S.add("dve", lambda e: e.reduce_sum(ssq.ap, sq.ap, mybir.AxisListType.X), reads=[sq], writes=[ssq])
                    TS("dve", ssq, ssq.ap, ssq, ssq.ap, 1.0 / 256, None, ALU.mult)
                    ACT(ssq, ssq.ap, ssq, ssq.ap, AF.Ln, bias=1e-6)
                    ACT(ssq, ssq.ap, ssq, ssq.ap, AF.Exp, scale=-0.5)
                    for hb in range(2):
                        for kc in range(8):
                            MM(P[0 + hb], P[0 + hb].ap, at_, at_[:, kc, :], winb, winb[:, kc, CR + hb * 512:CR + (hb + 1) * 512], kc == 0, kc == 7)
                        ACT(sg, sg[:, 2 * hb:2 * hb + 2, :], P[0 + hb], P[0 + hb].ap.rearrange("p (h v) -> p h v", h=2), AF.Silu)
                    for h in range(4):
                        STT(og, og[:, h * 256:(h + 1) * 256], osum, osum[:, h, :], ssq[:, h:h + 1], sg, sg[:, h, :], ALU.mult, ALU.mult, rd=[ssq])
                    for c in range(8):
                        S.add("pe", lambda e, c=c: e.transpose(PT[:, c * 128:(c + 1) * 128], og[:, c * 128:(c + 1) * 128], cmt[:, 6, :]),
                              reads=[og, cmt], writes=[PT])
                    for c in range(8):
                        ng = smt[:, SM_NG + (c % 2):SM_NG + (c % 2) + 1]
                        ACT(ogT, ogT[:, c, tt * 128:(tt + 1) * 128], PT, PT[:, c * 128:(c + 1) * 128], AF.Identity, scale=ng, rd=[smt])

                def cpc(pre=pre, at_=at_, fin_c=fin_c):
                    pre()
                    gla_tile(at_, "f", True, fin_c)
                stage(ldc, cpc)
            for m in range(8):
                w = woc[wc[0] % 2]
                wc[0] += 1

                def ldo(w=w, m=m):
                    S.dma("pool", w.ap, w_out[m], writes=[w], dom=w.name)

                def cpo(w=w, m=m, b=b):
                    for kc in range(8):
                        MM(P[2], P[2].ap, w, w[:, kc, :], ogT, ogT[:, kc, :], kc == 0, kc == 7)
                    STT(HA[b], HA[b][:, m, :], P[2], P[2].ap, modT[0][:, 16 + m, 0:1], HA[b], HA[b][:, m, :], ALU.mult, ALU.add, rd=[modT[0]])
                stage(ldo, cpo)
            stage(None, lambda b=b: ln_block(b, "00", ub2=lnA))

        def dump_and_finish():
            run_stream()
            for b in range(4):
                S.dma("sp", outT[b], HA[b].ap, reads=[HA[b]], dom=f"out{b}")
            S.emit()
            return nc

        if stop_after == "gla":
            return dump_and_finish(), S

        A["cur"] = PH
        actb = S.carve(A, "actb", [NJ, 1024], BF16)
        ut = S.carve(A, "ut", [NFC, 512], BF16, at=PH)
        vg = [S.carve(A, f"vg{t}", [GW], BF16, at=PH + 24576 + t * 6144) for t in range(4)]
        A["cur"] = PH + 49152
        wgu = [S.carve(A, f"wgu{i}", [2, 8, 128], BF16) for i in range(2)]
        wdn = [S.carve(A, f"wdn{i}", [NJ, 128], BF16) for i in range(2)]
        sgl = [S.carve(A, f"sgl{i}", [512], F32) for i in range(2)]
        ffn_end = A["cur"]
        A["cur"] = PH + 49152
        wiu = [S.carve(A, f"wiu{i}", [2, 8, 128], BF16) for i in range(2)]
        wiv = [S.carve(A, f"wiv{i}", [8, 512], BF16) for i in range(2)]
        wom = [S.carve(A, f"wom{i}", [NFC, 128], BF16) for i in range(2)]
        wsT = S.carve(A, "wsT", [4, 128], BF16)
        Rg = S.carve(A, "Rg", [4, 128], F32)
        Bsg = S.carve(A, "Bsg", [4, 128], F32)
        tr0 = max(A["cur"], ffn_end)
        A["cur"] = tr0
        Xc = [S.carve(A, f"Xc{i}", [128], F32) for i in range(2)]
        stmp = [S.carve(A, f"stmp{i}", [512], F32) for i in range(2)]
        bst = S.carve(A, "bst", [6, 6], F32)
        mvt = S.carve(A, "mvt", [2], F32)
        rst = S.carve(A, "rst", [1], F32)
        assert A["cur"] <= 210944, A["cur"]
        lnB = ln_temps(tr0, wdt.lo)
        assert A["cur"] <= 108 * KB
        cn = {"g": 0, "d": 0, "iu": 0, "iv": 0, "om": 0, "s": 0, "x": 0, "st": 0}

        def ffn_group(li, g, k):
            blks = (2 * g, 2 * g + 1)
            for j in range(NJ):
                i = cn["g"] % 2
                cn["g"] += 1
                wg_ = wgu[i]

                def ld(wg_=wg_, j=j):
                    S.dma("pool", wg_.ap, f_gu[li, j], writes=[wg_], dom=wg_.name)

                def cp(wg_=wg_, j=j):
                    for bi, b in enumerate(blks):
                        pg, pu = P[2 * bi], P[2 * bi + 1]
                        for kc in range(8):
                            MM(pg, pg.ap, wg_, wg_[:, 0, kc, :], HM[b], HM[b][:, kc, :], kc == 0, kc == 7)
                        for kc in range(8):
                            MM(pu, pu.ap, wg_, wg_[:, 1, kc, :], HM[b], HM[b][:, kc, :], kc == 0, kc == 7)
                        s_ = sgl[cn["s"] % 2]
                        cn["s"] += 1
                        ACT(s_, s_.ap, pg, pg.ap, AF.Silu)
                        TT("dve", actb, actb[:, j, bi * 512:(bi + 1) * 512], s_, s_.ap, pu, pu.ap, ALU.mult)
                stage(ld, cp)
            for m in range(8):
                w = wdn[cn["d"] % 2]
                cn["d"] += 1

                def ld(w=w, m=m):
                    S.dma("pool", w.ap, f_d[li, m], writes=[w], dom=w.name)

                def cp(w=w, m=m):
                    for bi, b in enumerate(blks):
                        py = P[4 + bi]
                        for j in range(NJ):
                            MM(py, py.ap, w, w[:, j, :], actb, actb[:, j, bi * 512:(bi + 1) * 512], j == 0, j == NJ - 1)
                        STT(HA[b], HA[b][:, m, :], py, py.ap, modT[li][:, 40 + m, 0:1], HA[b], HA[b][:, m, :], ALU.mult, ALU.add, rd=[modT[li]])
                stage(ld, cp)
            for b in blks:
                stage(None, lambda b=b: ln_block(b, k, final=(k == "11"), ub2=lnB))

        def gmlp_consts():
            def ld():
                S.dma("pool", wsT.ap, gm_ws, writes=[wsT], dom="wsT")
                S.dma("sp", Bsg.ap, gm_bs.rearrange("p (g q) -> p g q", g=4), writes=[Bsg], dom="Bsg")

            def cp():
                MM(P[6], P[6].ap, ones, ones.ap, wsT, wsT.ap.rearrange("p g q -> p (g q)"), True, True)
                CP("dve", Rg, Rg.ap, P[6], P[6].ap.rearrange("p (g q) -> p g q", g=4))
            stage(ld, cp)

        def gmlp_block(b):
            hm = HM[b]
            for f4 in range(12):
                w = wiu[cn["iu"] % 2]
                cn["iu"] += 1

                def ld(w=w, f4=f4):
                    S.dma("pool", w.ap, gm_iu[f4], writes=[w], dom=w.name)

                def cp(w=w, f4=f4):
                    for q in range(2):
                        fc = f4 * 2 + q
                        pu = P[fc % 2]
                        for kc in range(8):
                            MM(pu, pu.ap, w, w[:, q, kc, :], hm, hm[:, kc, :], kc == 0, kc == 7)
                        ACT(ut, ut[:, fc, :], pu, pu.ap, AF.Gelu)
                stage(ld, cp)
            for cb in range(6):
                w = wiv[cn["iv"] % 2]
                cn["iv"] += 1

                def ld(w=w, cb=cb):
                    S.dma("pool", w.ap, gm_iv[cb], writes=[w], dom=w.name)

                def cp(w=w, cb=cb):
                    for t in range(4):
                        pv = P[2 + t]
                        for kc in range(8):
                            MM(pv, pv.ap, hm, hm[:, kc, t * 128:(t + 1) * 128], w, w[:, kc, :], kc == 0, kc == 7)
                        ACT(vg[t], vg[t][:, cb * 512:(cb + 1) * 512], pv, pv.ap, AF.Gelu)
                stage(ld, cp)
            stage(None, lambda: gm_mid(b))
            gm_tail(b)

        def gm_mid(b):
            hm = HM[b]
            for t in range(4):
                for cb in range(6):
                    S.add("dve", lambda e, t=t, cb=cb: e.bn_stats(bst[:, cb, :], vg[t][:, cb * 512:(cb + 1) * 512]), reads=[vg[t]], writes=[bst])
                S.add("dve", lambda e: e.bn_aggr(mvt.ap, bst.ap.rearrange("p a b -> p (a b)")), reads=[bst], writes=[mvt])
                ACT(rst, rst.ap, mvt, mvt[:, 1:2], AF.Ln, bias=1e-5)
                ACT(rst, rst.ap, rst, rst.ap, AF.Exp, scale=-0.5)
                TS("dve", vg[t], vg[t].ap, vg[t], vg[t].ap, mvt[:, 0:1], rst[:, 0:1], ALU.subtract, ALU.mult, rd=[mvt, rst])
            for fc in range(NFC):
                g = fc // 6
                pm = P[fc % 2]
                for t in range(4):
                    MM(pm, pm[:, t * 128:(t + 1) * 128], vg[t], vg[t][:, fc * 128:(fc + 1) * 128], wsT, wsT[:, g, :], True, True)
                xc = Xc[cn["x"] % 2]
                cn["x"] += 1
                STT(xc, xc.ap, Rg, Rg[:, g, :], smt[:, SM_GB + fc:SM_GB + fc + 1], Bsg, Bsg[:, g, :], ALU.mult, ALU.add, rd=[smt])
                s_ = stmp[cn["st"] % 2]
                cn["st"] += 1
                STT(s_, s_.ap.rearrange("p (t q) -> p t q", t=4), pm, pm.ap.rearrange("p (t q) -> p t q", t=4), smt[:, SM_GG + fc:SM_GG + fc + 1],
                    xc, xc.ap.unsqueeze(1).to_broadcast([128, 4, 128]), ALU.mult, ALU.add, rd=[smt])
                TT("pool", ut, ut[:, fc, :], ut, ut[:, fc, :], s_, s_.ap, ALU.mult)

        def gm_tail(b):
            for m in range(8):
                w = wom[cn["om"] % 2]
                cn["om"] += 1

                def ld(w=w, m=m):
                    S.dma("pool", w.ap, gm_out[m], writes=[w], dom=w.name)

                def cp(w=w, m=m):
                    py = P[6]
                    for fc in range(NFC):
                        MM(py, py.ap, w, w[:, fc, :], ut, ut[:, fc, :], fc == 0, fc == NFC - 1)
                    STT(HA[b], HA[b][:, m, :], py, py.ap, modT[1][:, 16 + m, 0:1], HA[b], HA[b][:, m, :], ALU.mult, ALU.add, rd=[modT[1]])
                stage(ld, cp)
            stage(None, lambda: ln_block(b, "10", ub2=lnB))

        if stop_after == "ffn0":
            for g in range(2):
                ffn_group(0, g, "01")
            return dump_and_finish(), S
        gmlp_consts()
        for g in range(2):
            ffn_group(0, g, "01")
            if stop_after == "gmlp":
                gmlp_block(2 * g)
                gmlp_block(2 * g + 1)
                continue
            gmlp_block(2 * g)
            gmlp_block(2 * g + 1)
            ffn_group(1, g, "11")
        return dump_and_finish(), S


def _consts():
    j = np.arange(128)[:, None]
    i = np.arange(128)[None, :]
    c = -1.0 / 16.0
    mats = [c * (j <= i), c * (j > i), 1.0 * (j <= i), c * (j >= i), c * (j < i), 1.0 * (j >= i), 1.0 * (j == i)]
    return np.ascontiguousarray(np.stack([m.astype(np.float32) for m in mats], axis=1))


def _fm(v, n):
    return np.asarray(v, np.float32).reshape(n, 128).T


def make_in_maps(x, c, ctx, c_ctx, mod_w, mod_b, ln_g, ln_b, gla_w_in, gla_w_decay, gla_b_decay,
                 gla_norm_g, gla_w_out, gm_w_in, gm_ln_g, gm_ln_b, gm_w_s, gm_b_s, gm_w_out,
                 ffn_w_gate, ffn_w_up, ffn_w_down):
    f = lambda a: np.ascontiguousarray(np.asarray(a, dtype=np.float32))
    x, c, ctx, c_ctx = f(x), f(c), f(ctx), f(c_ctx)
    cmv = _consts()
    w_in0 = f(gla_w_in[0])
    w_in1 = w_in0.copy()
    w_in1[:, 1536:1552] = w_in0[:, 1552:1568]
    w_in1[:, 1552:1568] = w_in0[:, 1536:1552]
    wd = f(gla_w_decay[0])
    bd = f(gla_b_decay[0])
    ws = f(gm_w_s[0])
    bs = f(gm_b_s[0])
    C = np.ascontiguousarray
    gmi = f(gm_w_in[0])
    fg = f(ffn_w_gate).reshape(2, 8, 128, NJ, 128).transpose(0, 3, 2, 1, 4)
    fu = f(ffn_w_up).reshape(2, 8, 128, NJ, 128).transpose(0, 3, 2, 1, 4)
    shared = {
        "cm": cmv,
        "mod_w": C(f(mod_w).reshape(2, 8, 128, 12, 512).transpose(0, 3, 2, 1, 4)),
        "w_out": C(f(gla_w_out[0]).reshape(8, 128, 8, 128).transpose(2, 1, 0, 3)),
        "gm_iu": C(gmi[:, :GW].reshape(8, 128, 12, 2, 128).transpose(2, 1, 3, 0, 4)),
        "gm_iv": C(gmi[:, GW:].reshape(8, 128, 6, 512).transpose(2, 1, 0, 3)),
        "gm_out": C(f(gm_w_out[0]).reshape(NFC, 128, 8, 128).transpose(2, 1, 0, 3)),
        "f_gu": C(np.stack([fg, fu], axis=3)),
        "f_d": C(f(ffn_w_down).reshape(2, NJ, 128, 8, 128).transpose(0, 3, 2, 1, 4)),
    }
    maps = []
    for core in range(8):
        b, half = core // 2, core % 2
        xs, cs = x[b], ctx[b]
        if half:
            xs, cs = xs[::-1], cs[::-1]
        xall = np.concatenate([xs, cs], axis=0)
        xall = np.ascontiguousarray(xall.reshape(34, 128, 8, 128).transpose(0, 3, 2, 1))
        smv = np.zeros((128, SM_N), np.float32)
        smv[:, SM_CC:SM_CC + 16] = np.stack([_fm(c[b], 8), _fm(c_ctx, 8)], axis=-1).reshape(128, 16)
        smv[:, SM_MB0:SM_MB0 + 48] = _fm(mod_b[0], 48)
        smv[:, SM_MB1:SM_MB1 + 48] = _fm(mod_b[1], 48)
        for li in range(2):
            for lj in range(2):
                o = (li * 2 + lj) * 8
                smv[:, SM_LNG + o:SM_LNG + o + 8] = _fm(ln_g[li, lj], 8)
                smv[:, SM_LNB + o:SM_LNB + o + 8] = _fm(ln_b[li, lj], 8)
        smv[:, SM_NG:SM_NG + 2] = _fm(gla_norm_g[0], 2)
        smv[:, SM_GG:SM_GG + 24] = _fm(gm_ln_g[0], 24)
        smv[:, SM_GB:SM_GB + 24] = _fm(gm_ln_b[0], 24)
        wdd = wd[::-1] if half else wd
        bdd = bd[::-1] if half else bd
        wss = ws[:, ::-1, ::-1] if half else ws
        bss = bs[::-1] if half else bs
        m = dict(shared)
        m.update({
            "xT": xall, "sm": smv,
            "w_in": np.ascontiguousarray((w_in1 if half else w_in0).reshape(8, 128, 3104).transpose(1, 0, 2)),
            "wdec": np.ascontiguousarray(wdd.transpose(1, 0, 2)),
            "bdec": np.ascontiguousarray(bdd.reshape(1, 2, 512)),
            "gm_ws": np.ascontiguousarray(wss.transpose(2, 0, 1)),
            "gm_bs": np.ascontiguousarray(np.broadcast_to(bss.T.reshape(1, 512), (128, 512))),
        })
        maps.append(m)
    return maps


def assemble(results):
    out = np.empty((4, 4096, D), np.float32)
    for core in range(8):
        b, half = core // 2, core % 2
        o = results[core]["outT"].transpose(0, 3, 2, 1).reshape(NTOK, D)
        if half:
            out[b, 2048:] = o[::-1]
        else:
            out[b, :2048] = o
    return out


_NC = {}


def kernel(**inputs):
    if "nc" not in _NC:
        _NC["nc"] = build_program()[0]
    maps = make_in_maps(**inputs)
    res = run_bass_kernel_spmd(_NC["nc"], maps, core_ids=list(range(8)))
    return assemble(res.results)
```

```python
import numpy as np
import concourse.bass as bass
import concourse.mybir as mybir
from contextlib import ExitStack
from concourse.bass_utils import run_bass_kernel_spmd

F32 = mybir.dt.float32
BF16 = mybir.dt.bfloat16
AF = mybir.ActivationFunctionType
ALU = mybir.AluOpType

ENGINES = ("pe", "act", "dve", "pool", "sp")


class Tile:
    __slots__ = ("name", "ap", "space", "lo", "hi", "writers", "readers", "overlaps")

    def __init__(self, name, ap, space, lo, hi):
        self.name = name
        self.ap = ap
        self.space = space
        self.lo = lo
        self.hi = hi
        self.writers = {}
        self.readers = {}
        self.overlaps = []

    def __getitem__(self, k):
        return self.ap[k]


class Op:
    __slots__ = ("eng", "fn", "reads", "writes", "dom", "deps", "signal", "idx", "inc", "waits", "vc", "label")

    def __init__(self, eng, fn, reads, writes, dom, inc, label):
        self.eng = eng
        self.fn = fn
        self.reads = reads
        self.writes = writes
        self.dom = dom
        self.deps = []
        self.signal = False
        self.idx = 0
        self.inc = inc
        self.waits = []
        self.vc = None
        self.label = label


class Sched:
    def __init__(self, nc, stack):
        self.nc = nc
        self.stack = stack
        self.ops = []
        self.tiles_by_space = {}
        self.n_alloc = 0
        self.dom_last = {}

    def arena(self, name, nbytes):
        assert nbytes % 4 == 0
        t = self.stack.enter_context(self.nc.sbuf_tensor(name, [128, nbytes // 4], F32))
        return {"name": name, "t": t, "nbytes": nbytes, "cur": 0}

    def carve(self, arena, name, shape, dtype, at=None):
        esz = 2 if dtype == BF16 else 4
        n = int(np.prod(shape))
        nb = n * esz
        nb4 = (nb + 3) // 4 * 4
        if at is None:
            at = arena["cur"]
            arena["cur"] = at + nb4
        assert at % 4 == 0 and at + nb4 <= arena["nbytes"], (name, at, nb4, arena["nbytes"])
        ap = arena["t"][:, at // 4:(at + nb4) // 4]
        if dtype != F32:
            ap = ap.bitcast(dtype)
        ap = ap[:, 0:n]
        if len(shape) == 2:
            ap = ap.rearrange("p (a b) -> p a b", a=shape[0])
        elif len(shape) == 3:
            ap = ap.rearrange("p (a b c) -> p a b c", a=shape[0], b=shape[1])
        return self._reg(Tile(name, ap, arena["name"], at, at + nb4))

    def psum(self, name, shape, dtype=F32):
        t = self.stack.enter_context(self.nc.psum_tensor(name, [128] + list(shape), dtype))
        self.n_alloc += 1
        return self._reg(Tile(name, t[:], "psum_" + name, 0, 1))

    def sbuf(self, name, shape, dtype=F32, parts=128):
        t = self.stack.enter_context(self.nc.sbuf_tensor(name, [parts] + list(shape), dtype))
        return self._reg(Tile(name, t[:], "sb_" + name, 0, 1))

    def view(self, name, ap, parents):
        t = Tile(name, ap, None, 0, 0)
        for p in parents:
            t.overlaps.append(p)
            p.overlaps.append(t)
        return t

    def _reg(self, t):
        lst = self.tiles_by_space.setdefault(t.space, [])
        for u in lst:
            if u.lo < t.hi and t.lo < u.hi:
                u.overlaps.append(t)
                t.overlaps.append(u)
        lst.append(t)
        return t

    def add(self, eng, fn, reads=(), writes=(), dom=None, inc=1, label=""):
        op = Op(eng, fn, list(reads), list(writes), dom or eng, inc, label)
        deps = {}

        def need(p):
            if p is None or p is op:
                return
            deps[id(p)] = p

        for t in op.reads:
            for u in [t] + t.overlaps:
                for p in u.writers.values():
                    need(p)
        for t in op.writes:
            for u in [t] + t.overlaps:
                for p in u.writers.values():
                    need(p)
                for p in u.readers.values():
                    need(p)
        out = []
        for p in list(deps.values()):
            if p.dom.startswith("dma:"):
                if p.dom == op.dom:
                    continue
                p = self.dom_last[p.dom]
                if any(q is p for q in out):
                    continue
            if p.dom == op.dom and op.dom in ENGINES:
                if op.eng == "pe":
                    continue
                raw = any((p in u.writers.values()) for t in op.reads for u in [t] + t.overlaps)
                if not raw:
                    continue
            out.append(p)
        op.deps = out
        for p in out:
            p.signal = True
        for t in op.reads:
            t.readers[op.dom] = op
        for t in op.writes:
            t.writers = {op.dom: op}
            t.readers = {}
        if op.dom.startswith("dma:"):
            self.dom_last[op.dom] = op
        self.ops.append(op)
        return op

    def mm(self, out_t, out_ap, lhsT_t, lhsT_ap, rhs_t, rhs_ap, start=True, stop=True, extra_reads=()):
        return self.add("pe", lambda e: e.matmul(out_ap, lhsT_ap, rhs_ap, start=start, stop=stop),
                        reads=[lhsT_t, rhs_t] + list(extra_reads), writes=[out_t], label="mm")

    def dma(self, queue, out_ap, in_ap, reads=(), writes=(), dom=None, label="dma"):
        assert dom is not None
        return self.add(queue, lambda e: e.dma_start(out=out_ap, in_=in_ap),
                        reads=reads, writes=writes, dom="dma:" + dom, inc=16, label=label)

    def emit(self, final_wait_doms=()):
        nc = self.nc
        counts = {}
        for op in self.ops:
            if op.dom.startswith("dma:"):
                op.signal = True
            if op.signal:
                counts[op.dom] = counts.get(op.dom, 0) + 1
                op.idx = counts[op.dom]
        stream_vc = {e: {} for e in ENGINES}
        for op in self.ops:
            svc = stream_vc[op.eng]
            waits = []
            for p in op.deps:
                if svc.get(p.dom, 0) >= p.idx:
                    continue
                waits.append((p.dom, p.idx))
                for d, k in p.vc.items():
                    if svc.get(d, 0) < k:
                        svc[d] = k
            wm = {}
            for d, k in waits:
                if wm.get(d, 0) < k:
                    wm[d] = k
            op.waits = [(d, k) for d, k in wm.items() if True]
            if op.signal:
                vc = dict(svc)
                vc[op.dom] = op.idx
                op.vc = vc
            else:
                op.vc = None
        finals = [(d, k) for d, k in counts.items() if d.startswith("dma:")]
        sems = {}
        for d in counts:
            sems[d] = self.stack.enter_context(nc.semaphore("s_" + d.replace(":", "_")))
        self.n_sems = len(sems)
        per_eng = {e: [op for op in self.ops if op.eng == e] for e in ENGINES}
        self.stats = {e: len(per_eng[e]) for e in ENGINES}
        self.stats["waits"] = sum(len(op.waits) for op in self.ops)

        def run(eng_obj, ename):
            for op in per_eng[ename]:
                for d, k in op.waits:
                    mult = 16 if d.startswith("dma:") else 1
                    eng_obj.wait_ge(sems[d], k * mult)
                ins = op.fn(eng_obj)
                if op.signal:
                    ins.then_inc(sems[op.dom], op.inc)
            if ename == "sp":
                for d, k in finals:
                    eng_obj.wait_ge(sems[d], k * 16)

        with nc.Block() as block:
            @block.tensor
            def _(e):
                run(e, "pe")

            @block.scalar
            def _(e):
                run(e, "act")

            @block.vector
            def _(e):
                run(e, "dve")

            @block.gpsimd
            def _(e):
                run(e, "pool")

            @block.sync
            def _(e):
                run(e, "sp")


D = 1024
NTOK = 2048
NBLK = 4
ALPHA = (2 * 2) ** 0.25
HID = 2816
NJ = HID // 128
GW = 3072
NFC = GW // 128
QS = 128 ** -0.5
SM_CC, SM_MB0, SM_MB1, SM_LNG, SM_LNB, SM_NG, SM_GG, SM_GB, SM_N = 0, 16, 64, 112, 144, 176, 178, 202, 226


def build_program(stop_after=None):
    nc = bass.Bass("TRN2", target_bir_lowering=False)
    dt_ = nc.dram_tensor
    xT = dt_("xT", [34, 128, 8, 128], F32, kind="ExternalInput").ap()
    sm = dt_("sm", [128, SM_N], F32, kind="ExternalInput").ap()
    cm = dt_("cm", [128, 7, 128], F32, kind="ExternalInput").ap()
    mod_w = dt_("mod_w", [2, 12, 128, 8, 512], F32, kind="ExternalInput").ap()
    w_in = dt_("w_in", [128, 8, 3104], F32, kind="ExternalInput").ap()
    wdec = dt_("wdec", [16, 2, 512], F32, kind="ExternalInput").ap()
    bdec = dt_("bdec", [1, 2, 512], F32, kind="ExternalInput").ap()
    w_out = dt_("w_out", [8, 128, 8, 128], F32, kind="ExternalInput").ap()
    gm_iu = dt_("gm_iu", [12, 128, 2, 8, 128], F32, kind="ExternalInput").ap()
    gm_iv = dt_("gm_iv", [6, 128, 8, 512], F32, kind="ExternalInput").ap()
    gm_ws = dt_("gm_ws", [128, 4, 128], F32, kind="ExternalInput").ap()
    gm_bs = dt_("gm_bs", [128, 4 * 128], F32, kind="ExternalInput").ap()
    gm_out = dt_("gm_out", [8, 128, NFC, 128], F32, kind="ExternalInput").ap()
    f_gu = dt_("f_gu", [2, NJ, 128, 2, 8, 128], F32, kind="ExternalInput").ap()
    f_d = dt_("f_d", [2, 8, 128, NJ, 128], F32, kind="ExternalInput").ap()
    outT = dt_("outT", [4, 128, 8, 512], F32, kind="ExternalOutput").ap()

    st = ExitStack()
    with st:
        S = Sched(nc, st)
        A = S.arena("A", 210944)
        KB = 1024
        HA = [S.carve(A, f"ha{b}", [8, 512], F32, at=b * 16 * KB) for b in range(4)]
        HM = [S.carve(A, f"hm{b}", [8, 512], BF16, at=64 * KB + b * 8 * KB) for b in range(4)]
        OB = [S.carve(A, f"ob{t}", [1024], BF16, at=64 * KB + t * 2 * KB) for t in range(16)]
        A["cur"] = 96 * KB
        smt = S.carve(A, "smt", [SM_N], F32)
        cmt = S.carve(A, "cmt", [7, 128], BF16)
        ones = S.carve(A, "ones", [128], BF16)
        scb = S.carve(A, "scb", [8, 2], BF16)
        modT = [S.carve(A, f"modT{i}", [48, 2], F32) for i in range(2)]
        der = S.carve(A, "der", [24, 8], F32)
        wdt = S.carve(A, "wdt", [2, 512], BF16)
        bhi = S.carve(A, "bhi", [2, 512], BF16)
        blo = S.carve(A, "blo", [2, 512], BF16)
        btmp = S.carve(A, "btmp", [2, 512], F32, at=32 * KB)
        btmp2 = S.carve(A, "btmp2", [2, 512], F32, at=36 * KB)
        assert A["cur"] <= 108 * KB, A["cur"]
        PH = 108 * KB
        P = [S.psum(f"P{i}", [512]) for i in range(7)]
        PT = S.psum("PT", [1024], BF16)

        DER = {n: i for i, n in enumerate([
            "A1x", "cA1x", "Ga00", "Ba00", "A00", "B00", "Ga01", "Ba01", "A01", "B01",
            "Ga10", "Ba10", "A10", "B10", "G11", "B11", "tmp"])}

        def dv(name, m=None):
            i = DER[name]
            return der[:, i, :] if m is None else der[:, i, m:m + 1]

        def smv(off, n):
            return smt[:, off:off + n]

        def ACT(out_t, out_ap, in_t, in_ap, func, scale=1.0, bias=0.0, rd=()):
            S.add("act", lambda e: e.activation(out_ap, in_ap, func, bias=bias, scale=scale),
                  reads=[in_t] + list(rd), writes=[out_t])

        def TT(eng, out_t, out_ap, a_t, a_ap, b_t, b_ap, op):
            S.add(eng, lambda e: e.tensor_tensor(out_ap, a_ap, b_ap, op), reads=[a_t, b_t], writes=[out_t])

        def TS(eng, out_t, out_ap, in_t, in_ap, s1, s2, op0, op1=None, rd=()):
            if op1 is None:
                S.add(eng, lambda e: e.tensor_scalar(out_ap, in_ap, s1, None, op0), reads=[in_t] + list(rd), writes=[out_t])
            else:
                S.add(eng, lambda e: e.tensor_scalar(out_ap, in_ap, s1, s2, op0, op1), reads=[in_t] + list(rd), writes=[out_t])

        def STT(out_t, out_ap, a_t, a_ap, scal, b_t, b_ap, op0, op1, rd=()):
            S.add("dve", lambda e: e.scalar_tensor_tensor(out_ap, a_ap, scal, b_ap, op0, op1),
                  reads=[a_t, b_t] + list(rd), writes=[out_t])

        def CP(eng, out_t, out_ap, in_t, in_ap):
            if eng == "act":
                S.add("act", lambda e: e.copy(out_ap, in_ap), reads=[in_t], writes=[out_t])
            else:
                S.add(eng, lambda e: e.tensor_copy(out_ap, in_ap), reads=[in_t], writes=[out_t])

        def MM(out_t, out_ap, l_t, l_ap, r_t, r_ap, start, stop):
            S.mm(out_t, out_ap, l_t, l_ap, r_t, r_ap, start=start, stop=stop)

        stages = []

        def stage(load, compute):
            stages.append((load, compute))

        def run_stream():
            if stages:
                if stages[0][0]:
                    stages[0][0]()
            for si, (ld, cp) in enumerate(stages):
                if si + 1 < len(stages) and stages[si + 1][0]:
                    stages[si + 1][0]()
                if cp:
                    cp()

        S.dma("sp", smt.ap, sm, writes=[smt], dom="smt")
        S.dma("pool", cmt.ap, cm, writes=[cmt], dom="cmt")
        S.dma("pool", wdt[0:16], wdec, writes=[wdt], dom="wdt")
        S.dma("sp", btmp[0:1], bdec, writes=[btmp], dom="btmp")
        S.add("dve", lambda e: e.memset(ones.ap, 1.0), writes=[ones])
        CP("dve", bhi, bhi[0:1], btmp, btmp[0:1])
        CP("dve", btmp2, btmp2[0:1], bhi, bhi[0:1])
        TT("dve", btmp2, btmp2[0:1], btmp, btmp[0:1], btmp2, btmp2[0:1], ALU.subtract)
        CP("dve", blo, blo[0:1], btmp2, btmp2[0:1])
        ACT(scb, scb.ap, smt, smt[:, SM_CC:SM_CC + 16].rearrange("p (k c) -> p k c", c=2), AF.Silu)

        mwb = [S.carve(A, f"mwb{i}", [8, 512], BF16, at=48 * KB + i * 8 * KB) for i in range(2)]
        mcnt = [0]

        def mod_layer(i, pieces):
            for j in pieces:
                w = mwb[mcnt[0] % 2]
                mcnt[0] += 1

                def ld(w=w, j=j):
                    S.dma("pool", w.ap, mod_w[i, j],
                          writes=[w], dom=w.name)

                def cp(w=w, j=j):
                    for m in range(4):
                        col = (j * 4 + m) * 2
                        for kc in range(8):
                            MM(P[6], P[6][:, col:col + 2], w, w[:, kc, m * 128:(m + 1) * 128], scb, scb[:, kc, :],
                               kc == 0, kc == 7)
                    c0 = j * 4
                    mb = SM_MB0 if i == 0 else SM_MB1
                    for cc in range(2):
                        TT("dve", modT[i], modT[i][:, c0:c0 + 4, cc],
                           P[6], P[6][:, c0 * 2:(c0 + 4) * 2].rearrange("p (a b) -> p a b", b=2)[:, :, cc],
                           smt, smt[:, mb + c0:mb + c0 + 4], ALU.add)
                stage(ld, cp)

        def mslice(i, which, col=0):
            return modT[i][:, which * 8:(which + 1) * 8, col]

        def derive(name, fn):
            fn(dv(name))

        def ln_scalars(k, li, lj, nxt, alpha=True):
            g = smt[:, SM_LNG + (li * 2 + lj) * 8: SM_LNG + (li * 2 + lj) * 8 + 8]
            b = smt[:, SM_LNB + (li * 2 + lj) * 8: SM_LNB + (li * 2 + lj) * 8 + 8]
            a = ALPHA if alpha else 1.0
            gn, bn = ("Ga" + k, "Ba" + k) if alpha else ("G" + k, "B" + k)
            TS("dve", der, dv(gn), smt, g, a, None, ALU.mult)
            TS("dve", der, dv(bn), smt, b, a, None, ALU.mult)
            if nxt is not None:
                L, sci, shi = nxt
                TS("dve", der, dv("tmp"), modT[L], mslice(L, sci), 1.0, None, ALU.add)
                TT("dve", der, dv("A" + k), smt, g, der, dv("tmp"), ALU.mult)
                TT("dve", der, dv("B" + k), smt, b, der, dv("tmp"), ALU.mult)
                TT("dve", der, dv("B" + k), der, dv("B" + k), modT[L], mslice(L, shi), ALU.add)

        mod_layer(0, range(4))

        def _d0():
            TS("dve", der, dv("A1x"), modT[0], mslice(0, 1, 0), 1.0, None, ALU.add)
            TS("dve", der, dv("cA1x"), modT[0], mslice(0, 1, 1), 1.0, None, ALU.add)
        stage(None, _d0)

        winb = S.carve(A, "winb", [8, 3104], BF16, at=PH)
        g0 = PH + 49664
        A["cur"] = g0
        xbuf = [S.carve(A, f"xbuf{i}", [8, 128], F32) for i in range(2)]
        ogT = S.carve(A, "ogT", [8, 512], BF16, at=g0)
        aTb = [S.carve(A, f"aT{i}", [8, 128], BF16) for i in range(2)]
        alr = S.carve(A, "alr", [128], BF16)
        spb = S.carve(A, "spb", [512], BF16)
        e_t = S.carve(A, "e_t", [512], F32)
        Est = S.carve(A, "Est", [512], F32)
        sq = S.carve(A, "sq", [4, 256], F32, at=A["cur"] - 4 * KB)
        EqT = S.carve(A, "EqT", [4, 128], F32)
        EkT = S.carve(A, "EkT", [4, 128], F32)
        kst = S.carve(A, "kst", [512], BF16)
        vb = S.carve(A, "vb", [1024], BF16)
        qin = S.carve(A, "qin", [4, 128], BF16)
        kin = S.carve(A, "kin", [4, 128], BF16)
        attT = S.carve(A, "attT", [4, 128], BF16)
        Sst = {d: S.carve(A, "S_" + d, [4, 256], F32) for d in "fb"}
        Sb = S.carve(A, "Sb", [4, 256], BF16)
        Dt = S.carve(A, "Dt", [4], F32)
        ssq = S.carve(A, "ssq", [4], F32)
        osum = S.carve(A, "osum", [4, 256], F32)
        sg = S.carve(A, "sg", [4, 256], BF16)
        og = S.carve(A, "og", [1024], BF16)
        woc = [S.carve(A, f"woc{i}", [8, 128], BF16) for i in range(2)]
        assert A["cur"] <= 210944, A["cur"]

        for kc in range(8):
            S.dma("pool", winb[:, kc, :], w_in[:, kc, :], writes=[winb], dom="winb")
        for d in "fb":
            S.add("pool", lambda e, d=d: e.memset(Sst[d].ap, 0.0), writes=[Sst[d]])

        CK, CV, CAF, CAB, CQ, CR = 0, 512, 1536, 1552, 1568, 2080
        cnt = {"x": 0}

        def tile_stage(col0, Ascale, shift, d, need_out, fin=None):
            i = cnt["x"] % 2
            cnt["x"] += 1
            xb_, at_ = xbuf[i], aTb[i]

            def ld():
                S.dma("sp", xb_.ap, xT[col0 // 128], writes=[xb_], dom=xb_.name)

            def cp():
                for kc in range(8):
                    ACT(at_, at_[:, kc, :], xb_, xb_[:, kc, :], AF.Identity, scale=Ascale[:, kc:kc + 1], bias=shift[:, kc:kc + 1], rd=[der, modT[0]])
                gla_tile(at_, d, need_out, fin)
            stage(ld, cp)

        def gla_tile(at_, d, need_out, fin=None):
            di = 0 if d == "f" else 1
            triI, triR, msk = cmt[:, 3 * di + 0, :], cmt[:, 3 * di + 1, :], cmt[:, 3 * di + 2, :]
            last = 127 if d == "f" else 0
            ca = CAF if d == "f" else CAB
            for kc in range(8):
                MM(P[0], P[0][0:16, 0:128], winb, winb[:, kc, ca:ca + 16], at_, at_[:, kc, :], kc == 0, kc == 7)
            CP("act", alr, alr[0:16, :], P[0], P[0][0:16, 0:128])
            for kc in range(8):
                MM(P[4], P[4].ap, at_, at_[:, kc, :], winb, winb[:, kc, CK:CK + 512], kc == 0, kc == 7)
            for hb in range(2):
                for kc in range(8):
                    MM(P[5 + hb], P[5 + hb].ap, at_, at_[:, kc, :], winb, winb[:, kc, CV + hb * 512:CV + (hb + 1) * 512], kc == 0, kc == 7)
                CP("act", vb, vb[:, hb * 512:(hb + 1) * 512], P[5 + hb], P[5 + hb].ap)
            MM(P[1], P[1].ap, alr, alr[0:16, :], wdt, wdt[0:16, di, :], True, False)
            MM(P[1], P[1].ap, ones, ones[0:1, :], bhi, bhi[0:1, di, :], False, False)
            MM(P[1], P[1].ap, ones, ones[0:1, :], blo, blo[0:1, di, :], False, True)
            ACT(e_t, e_t.ap, P[1], P[1].ap, AF.Exp, scale=-1.0)
            ACT(spb, spb.ap, e_t, e_t.ap, AF.Ln, bias=1.0)
            if need_out:
                for h in range(4):
                    for kc in range(8):
                        MM(P[0], P[0][:, h * 128:(h + 1) * 128], winb, winb[:, kc, CQ + h * 128:CQ + (h + 1) * 128], at_, at_[:, kc, :], kc == 0, kc == 7)
            MM(P[2], P[2].ap, cmt, triR, spb, spb.ap, True, True)
            for h in range(4):
                MM(P[3], P[3][:, h * 128:(h + 1) * 128], spb, spb[:, h * 128:(h + 1) * 128], cmt, triI, True, True)
            p3 = P[3].ap.rearrange("p (h t) -> p h t", h=4)
            ACT(Est, Est.ap, P[2], P[2].ap, AF.Exp)
            ACT(Dt, Dt.ap, P[3], p3[:, :, last], AF.Exp)
            if need_out:
                ACT(EqT, EqT.ap, P[3], p3, AF.Exp)
                ACT(EkT, EkT.ap, P[3], p3, AF.Exp, scale=-1.0)
            TT("dve", kst, kst.ap, P[4], P[4].ap, Est, Est.ap, ALU.mult)
            if need_out:
                STT(qin, qin.ap, P[0], P[0].ap.rearrange("p (h t) -> p h t", h=4), QS, EqT, EqT.ap, ALU.mult, ALU.mult)
                for h in range(4):
                    for kc in range(8):
                        MM(P[1], P[1][:, h * 128:(h + 1) * 128], winb, winb[:, kc, CK + h * 128:CK + (h + 1) * 128], at_, at_[:, kc, :], kc == 0, kc == 7)
                TT("dve", kin, kin.ap, P[1], P[1].ap.rearrange("p (h t) -> p h t", h=4), EkT, EkT.ap, ALU.mult)
                for h in range(4):
                    MM(P[2], P[2][:, h * 128:(h + 1) * 128], kin, kin[:, h, :], qin, qin[:, h, :], True, True)
                TT("dve", attT, attT.ap, P[2], P[2].ap.rearrange("p (h t) -> p h t", h=4),
                   cmt, msk.unsqueeze(1).to_broadcast([128, 4, 128]), ALU.mult)
                for h in range(4):
                    pb = P[3 + h // 2]
                    oap = pb[:, (h % 2) * 256:(h % 2 + 1) * 256]
                    MM(pb, oap, attT, attT[:, h, :], vb, vb[:, h * 256:(h + 1) * 256], True, False)
                    MM(pb, oap, qin, qin[:, h, :], Sb, Sb[:, h, :], False, True)
                fin(P[3], P[4])
            for h in range(4):
                pb = P[5 + h // 2]
                MM(pb, pb[:, (h % 2) * 256:(h % 2 + 1) * 256], kst, kst[:, h * 128:(h + 1) * 128], vb, vb[:, h * 256:(h + 1) * 256], True, True)
            for h in range(4):
                pb = P[5 + h // 2]
                STT(Sst[d], Sst[d][:, h, :], Sst[d], Sst[d][:, h, :], Dt[:, h:h + 1], pb, pb[:, (h % 2) * 256:(h % 2 + 1) * 256],
                    ALU.mult, ALU.add, rd=[Dt])
            CP("pool", Sb, Sb.ap, Sst[d], Sst[d].ap)

        csh = mslice(0, 0, 1)
        for t in range(2):
            tile_stage(2 * NTOK + t * 128, dv("cA1x"), csh, "f", False)
        for t in (1, 0):
            tile_stage(2 * NTOK + t * 128, dv("cA1x"), csh, "b", False)
        sh1 = mslice(0, 0, 0)
        for t in range(15, -1, -1):
            tile_stage(NTOK + t * 128, dv("A1x"), sh1, "b", False)
        def fin_b(t):
            def f(pa, pb):
                CP("act", OB[t], OB[t][:, 0:512], pa, pa.ap)
                CP("act", OB[t], OB[t][:, 512:1024], pb, pb.ap)
            return f
        stage(None, lambda: CP("pool", Sb, Sb.ap, Sst["b"], Sst["b"].ap))
        for t in range(15, -1, -1):
            tile_stage(t * 128, dv("A1x"), sh1, "b", True, fin_b(t))

        mod_layer(0, range(4, 12))
        mod_layer(1, range(12))

        def _d1():
            ln_scalars("00", 0, 0, (0, 4, 3))
            ln_scalars("01", 0, 1, (1, 1, 0))
            ln_scalars("10", 1, 0, (1, 4, 3))
            ln_scalars("11", 1, 1, None, alpha=False)
        stage(None, _d1)

        PHL = PH + 90 * KB

        pending = []

        def flush_pending():
            while pending:
                pending.pop(0)()

        def ln_stats_chunk(b, m, pa, pb, ub2, defer=False):
            h = HA[b]
            ubs, uqs = ub2[0], ub2[1]
            ub, uq = ubs[m % 2], uqs[m % 2]
            CP("act", ub, ub.ap, h, h[:, m, :])
            ACT(uq, uq.ap, h, h[:, m, :], AF.Square)

            def mms():
                MM(pa, pa.ap, ones, ones.ap, ub, ub.ap, m == 0, m == 7)
                MM(pb, pb.ap, ones, ones.ap, uq, uq.ap, m == 0, m == 7)
            if defer:
                pending.append(mms)
            else:
                mms()

        def ln_block(b, k, final=False, ub2=None, banks=None):
            h = HA[b]
            ubs, uqs, nm, tq, rs = ub2
            if banks is None:
                pa, pb = P[0], P[1]
                for m in range(8):
                    ln_stats_chunk(b, m, pa, pb, ub2)
            else:
                pa, pb = banks
                flush_pending()
            TS("dve", nm, nm.ap, pa, pa.ap, -1.0 / D, None, ALU.mult)
            TT("dve", tq, tq.ap, nm, nm.ap, nm, nm.ap, ALU.mult)
            STT(rs, rs.ap, pb, pb.ap, 1.0 / D, tq, tq.ap, ALU.mult, ALU.subtract)
            ACT(rs, rs.ap, rs, rs.ap, AF.Ln, bias=1e-5)
            ACT(rs, rs.ap, rs, rs.ap, AF.Exp, scale=-0.5)
            for m in range(8):
                eng = "pool" if m % 2 == 0 else "dve"
                TT(eng, h, h[:, m, :], h, h[:, m, :], nm, nm.ap, ALU.add)
                TT(eng, h, h[:, m, :], h, h[:, m, :], rs, rs.ap, ALU.mult)
            if final:
                for m in range(8):
                    TS("dve" if m % 2 else "pool", h, h[:, m, :], h, h[:, m, :], dv("G" + k, m), dv("B" + k, m), ALU.mult, ALU.add, rd=[der])
            else:
                for m in range(8):
                    ACT(HM[b], HM[b][:, m, :], h, h[:, m, :], AF.Identity, scale=dv("A" + k, m), bias=dv("B" + k, m), rd=[der])
                    TS("dve" if m % 2 else "pool", h, h[:, m, :], h, h[:, m, :], dv("Ga" + k, m), dv("Ba" + k, m), ALU.mult, ALU.add, rd=[der])

        def ln_temps(base, base2=None):
            A["cur"] = base
            ubs = [S.carve(A, f"ub{i}_{base}", [512], BF16) for i in range(2)]
            uqs = [S.carve(A, f"uq{i}_{base}", [512], BF16) for i in range(2)]
            if base2 is not None:
                A["cur"] = base2
            nm = S.carve(A, f"nm_{base}", [512], F32)
            tq = S.carve(A, f"tq_{base}", [512], F32)
            rs = S.carve(A, f"rs_{base}", [512], F32)
            return (ubs, uqs, nm, tq, rs)

        lnA = ln_temps(e_t.lo)

        stage(None, lambda: CP("pool", Sb, Sb.ap, Sst["f"], Sst["f"].ap))
        wc = [0]
        for b in range(4):
            for tt in range(4):
                t = b * 4 + tt
                hb_ = HA[b]
                at_ = aTb[cnt["x"] % 2]
                cnt["x"] += 1

                def ldc(hb_=hb_, t=t, tt=tt):
                    S.dma("sp", hb_[:, :, tt * 128:(tt + 1) * 128], xT[t],
                          writes=[hb_], dom=f"hax{t}")

                def pre(hb_=hb_, tt=tt, at_=at_):
                    for kc in range(8):
                        ACT(at_, at_[:, kc, :], hb_, hb_[:, kc, tt * 128:(tt + 1) * 128], AF.Identity,
                            scale=dv("A1x", kc), bias=sh1[:, kc:kc + 1], rd=[der, modT[0]])
                    for kc in range(8):
                        TS("pool", hb_, hb_[:, kc, tt * 128:(tt + 1) * 128], hb_, hb_[:, kc, tt * 128:(tt + 1) * 128], ALPHA, None, ALU.mult)

                def fin_c(pa, pb, t=t, tt=tt, at_=at_):
                    o3 = osum.ap
                    TT("dve", osum, osum[:, 0:2, :], pa, pa.ap.rearrange("p (h v) -> p h v", h=2), OB[t], OB[t][:, 0:512].rearrange("p (h v) -> p h v", h=2), ALU.add)
                    TT("dve", osum, osum[:, 2:4, :], pb, pb.ap.rearrange("p (h v) -> p h v", h=2), OB[t], OB[t][:, 512:1024].rearrange("p (h v) -> p h v", h=2), ALU.add)
                    TT("pool", sq, sq.ap, osum, osum.ap, osum, osum.ap, ALU.mult)
                    S.add("dve", lambda e: e.reduce_sum(ssq.ap, sq.ap, mybir.AxisListType.X), reads=[sq], writes=[ssq])
                    TS("dve", ssq, ssq.ap, ssq, ssq.ap, 1.0 / 256, None, ALU.mult)
                    ACT(ssq, ssq.ap, ssq, ssq.ap, AF.Ln, bias=1e-6)
                    ACT(ssq, ssq.ap, ssq, ssq.ap, AF.Exp, scale=-0.5)
                    for hb in range(2):
                        for kc in range(8):
                            MM(P[0 + hb], P[0 + hb].ap, at_, at_[:, kc, :], winb, winb[:, kc, CR + hb * 512:CR + (hb + 1) * 512], kc == 0, kc == 7)
                        ACT(sg, sg[:, 2 * hb:2 * hb + 2, :], P[0 + hb], P[0 + hb].ap.rearrange("p (h v) -> p h v", h=2), AF.Silu)
                    for h in range(4):
                        STT(og, og[:, h * 256:(h + 1) * 256], osum, osum[:, h, :], ssq[:, h:h + 1], sg, sg[:, h, :], ALU.mult, ALU.mult, rd=[ssq])
                    for c in range(8):
                        S.add("pe", lambda e, c=c: e.transpose(PT[:, c * 128:(c + 1) * 128], og[:, c * 128:(c + 1) * 128], cmt[:, 6, :]),
                              reads=[og, cmt], writes=[PT])
                    for c in range(8):
                        ng = smt[:, SM_NG + (c % 2):SM_NG + (c % 2) + 1]
                        ACT(ogT, ogT[:, c, tt * 128:(tt + 1) * 128], PT, PT[:, c * 128:(c + 1) * 128], AF.Identity, scale=ng, rd=[smt])

                def cpc(pre=pre, at_=at_, fin_c=fin_c):
                    pre()
                    gla_tile(at_, "f", True, fin_c)
                stage(ldc, cpc)
            for m in range(8):
                w = woc[wc[0] % 2]
                wc[0] += 1

                def ldo(w=w, m=m):
                    S.dma("pool", w.ap, w_out[m], writes=[w], dom=w.name)

                def cpo(w=w, m=m, b=b):
                    for kc in range(8):
                        MM(P[2], P[2].ap, w, w[:, kc, :], ogT, ogT[:, kc, :], kc == 0, kc == 7)
                    STT(HA[b], HA[b][:, m, :], P[2], P[2].ap, modT[0][:, 16 + m, 0:1], HA[b], HA[b][:, m, :], ALU.mult, ALU.add, rd=[modT[0]])
                stage(ldo, cpo)
            stage(None, lambda b=b: ln_block(b, "00", ub2=lnA))

        def dump_and_finish():
            run_stream()
            for b in range(4):
                S.dma("sp", outT[b], HA[b].ap, reads=[HA[b]], dom=f"out{b}")
            S.emit()
            return nc

        if stop_after == "gla":
            return dump_and_finish(), S

        A["cur"] = PH
        actb = S.carve(A, "actb", [NJ, 1024], BF16)
        ut = S.carve(A, "ut", [NFC, 512], BF16, at=PH)
        vg = [S.carve(A, f"vg{t}", [GW], BF16, at=PH + 24576 + t * 6144) for t in range(4)]
        A["cur"] = PH + 49152
        wgu = [S.carve(A, f"wgu{i}", [2, 8, 128], BF16) for i in range(2)]
        wdn = [S.carve(A, f"wdn{i}", [NJ, 128], BF16) for i in range(2)]
        sgl = [S.carve(A, f"sgl{i}", [512], F32) for i in range(2)]
        ffn_end = A["cur"]
        A["cur"] = PH + 49152
        wiu = [S.carve(A, f"wiu{i}", [2, 8, 128], BF16) for i in range(2)]
        wiv = [S.carve(A, f"wiv{i}", [8, 512], BF16) for i in range(2)]
        wom = [S.carve(A, f"wom{i}", [NFC, 128], BF16) for i in range(2)]
        wsT = S.carve(A, "wsT", [4, 128], BF16)
        Rg = S.carve(A, "Rg", [4, 128], F32)
        Bsg = S.carve(A, "Bsg", [4, 128], F32)
        tr0 = max(A["cur"], ffn_end)
        A["cur"] = tr0
        Xc = [S.carve(A, f"Xc{i}", [128], F32) for i in range(2)]
        stmp = [S.carve(A, f"stmp{i}", [512], F32) for i in range(2)]
        bst = S.carve(A, "bst", [6, 6], F32)
        mvt = S.carve(A, "mvt", [2], F32)
        rst = S.carve(A, "rst", [1], F32)
        assert A["cur"] <= 210944, A["cur"]
        lnB = ln_temps(tr0, wdt.lo)
        assert A["cur"] <= 108 * KB
        cn = {"g": 0, "d": 0, "iu": 0, "iv": 0, "om": 0, "s": 0, "x": 0, "st": 0}

        def ffn_group(li, g, k):
            blks = (2 * g, 2 * g + 1)
            for j in range(NJ):
                i = cn["g"] % 2
                cn["g"] += 1
                wg_ = wgu[i]

                def ld(wg_=wg_, j=j):
                    S.dma("pool", wg_.ap, f_gu[li, j], writes=[wg_], dom=wg_.name)

                def cp(wg_=wg_, j=j):
                    for bi, b in enumerate(blks):
                        pg, pu = P[2 * bi], P[2 * bi + 1]
                        for kc in range(8):
                            MM(pg, pg.ap, wg_, wg_[:, 0, kc, :], HM[b], HM[b][:, kc, :], kc == 0, kc == 7)
                        for kc in range(8):
                            MM(pu, pu.ap, wg_, wg_[:, 1, kc, :], HM[b], HM[b][:, kc, :], kc == 0, kc == 7)
                        s_ = sgl[cn["s"] % 2]
                        cn["s"] += 1
                        ACT(s_, s_.ap, pg, pg.ap, AF.Silu)
                        TT("dve", actb, actb[:, j, bi * 512:(bi + 1) * 512], s_, s_.ap, pu, pu.ap, ALU.mult)
                stage(ld, cp)
            for m in range(8):
                w = wdn[cn["d"] % 2]
                cn["d"] += 1

                def ld(w=w, m=m):
                    S.dma("pool", w.ap, f_d[li, m], writes=[w], dom=w.name)

                def cp(w=w, m=m):
                    for bi, b in enumerate(blks):
                        py = P[4 + bi]
                        for j in range(NJ):
                            MM(py, py.ap, w, w[:, j, :], actb, actb[:, j, bi * 512:(bi + 1) * 512], j == 0, j == NJ - 1)
                        flush_pending()
                        STT(HA[b], HA[b][:, m, :], py, py.ap, modT[li][:, 40 + m, 0:1], HA[b], HA[b][:, m, :], ALU.mult, ALU.add, rd=[modT[li]])
                        ln_stats_chunk(b, m, P[2 * bi], P[2 * bi + 1], lnB, defer=True)
                stage(ld, cp)
            for bi, b in enumerate(blks):
                stage(None, lambda b=b, bi=bi: ln_block(b, k, final=(k == "11"), ub2=lnB, banks=(P[2 * bi], P[2 * bi + 1])))

        def gmlp_consts():
            def ld():
                S.dma("pool", wsT.ap, gm_ws, writes=[wsT], dom="wsT")
                S.dma("sp", Bsg.ap, gm_bs.rearrange("p (g q) -> p g q", g=4), writes=[Bsg], dom="Bsg")

            def cp():
                MM(P[6], P[6].ap, ones, ones.ap, wsT, wsT.ap.rearrange("p g q -> p (g q)"), True, True)
                CP("dve", Rg, Rg.ap, P[6], P[6].ap.rearrange("p (g q) -> p g q", g=4))
            stage(ld, cp)

        def gmlp_block(b):
            hm = HM[b]
            for f4 in range(12):
                w = wiu[cn["iu"] % 2]
                cn["iu"] += 1

                def ld(w=w, f4=f4):
                    S.dma("pool", w.ap, gm_iu[f4], writes=[w], dom=w.name)

                def cp(w=w, f4=f4):
                    for q in range(2):
                        fc = f4 * 2 + q
                        pu = P[fc % 2]
                        for kc in range(8):
                            MM(pu, pu.ap, w, w[:, q, kc, :], hm, hm[:, kc, :], kc == 0, kc == 7)
                        ACT(ut, ut[:, fc, :], pu, pu.ap, AF.Gelu)
                stage(ld, cp)
            for cb in range(6):
                w = wiv[cn["iv"] % 2]
                cn["iv"] += 1

                def ld(w=w, cb=cb):
                    S.dma("pool", w.ap, gm_iv[cb], writes=[w], dom=w.name)

                def cp(w=w, cb=cb):
                    for t in range(4):
                        pv = P[2 + t]
                        for kc in range(8):
                            MM(pv, pv.ap, hm, hm[:, kc, t * 128:(t + 1) * 128], w, w[:, kc, :], kc == 0, kc == 7)
                        ACT(vg[t], vg[t][:, cb * 512:(cb + 1) * 512], pv, pv.ap, AF.Gelu)
                stage(ld, cp)
            stage(None, lambda: gm_mid(b))
            gm_tail(b)

        def gm_mid(b):
            hm = HM[b]
            for t in range(4):
                for cb in range(6):
                    S.add("dve", lambda e, t=t, cb=cb: e.bn_stats(bst[:, cb, :], vg[t][:, cb * 512:(cb + 1) * 512]), reads=[vg[t]], writes=[bst])
                S.add("dve", lambda e: e.bn_aggr(mvt.ap, bst.ap.rearrange("p a b -> p (a b)")), reads=[bst], writes=[mvt])
                ACT(rst, rst.ap, mvt, mvt[:, 1:2], AF.Ln, bias=1e-5)
                ACT(rst, rst.ap, rst, rst.ap, AF.Exp, scale=-0.5)
                TS("dve", vg[t], vg[t].ap, vg[t], vg[t].ap, mvt[:, 0:1], rst[:, 0:1], ALU.subtract, ALU.mult, rd=[mvt, rst])
            for fc in range(NFC):
                g = fc // 6
                pm = P[fc % 2]
                for t in range(4):
                    MM(pm, pm[:, t * 128:(t + 1) * 128], vg[t], vg[t][:, fc * 128:(fc + 1) * 128], wsT, wsT[:, g, :], True, True)
                xc = Xc[cn["x"] % 2]
                cn["x"] += 1
                STT(xc, xc.ap, Rg, Rg[:, g, :], smt[:, SM_GB + fc:SM_GB + fc + 1], Bsg, Bsg[:, g, :], ALU.mult, ALU.add, rd=[smt])
                s_ = stmp[cn["st"] % 2]
                cn["st"] += 1
                STT(s_, s_.ap.rearrange("p (t q) -> p t q", t=4), pm, pm.ap.rearrange("p (t q) -> p t q", t=4), smt[:, SM_GG + fc:SM_GG + fc + 1],
                    xc, xc.ap.unsqueeze(1).to_broadcast([128, 4, 128]), ALU.mult, ALU.add, rd=[smt])
                TT("pool", ut, ut[:, fc, :], ut, ut[:, fc, :], s_, s_.ap, ALU.mult)

        def gm_tail(b):
            for m in range(8):
                w = wom[cn["om"] % 2]
                cn["om"] += 1

                def ld(w=w, m=m):
                    S.dma("pool", w.ap, gm_out[m], writes=[w], dom=w.name)

                def cp(w=w, m=m):
                    py = P[6]
                    for fc in range(NFC):
                        MM(py, py.ap, w, w[:, fc, :], ut, ut[:, fc, :], fc == 0, fc == NFC - 1)
                    flush_pending()
                    STT(HA[b], HA[b][:, m, :], py, py.ap, modT[1][:, 16 + m, 0:1], HA[b], HA[b][:, m, :], ALU.mult, ALU.add, rd=[modT[1]])
                    ln_stats_chunk(b, m, P[0], P[1], lnB, defer=True)
                stage(ld, cp)
            stage(None, lambda: ln_block(b, "10", ub2=lnB, banks=(P[0], P[1])))

        if stop_after == "ffn0":
            for g in range(2):
                ffn_group(0, g, "01")
            return dump_and_finish(), S
        gmlp_consts()
        for g in range(2):
            ffn_group(0, g, "01")
            if stop_after == "gmlp":
                gmlp_block(2 * g)
                gmlp_block(2 * g + 1)
                continue
            gmlp_block(2 * g)
            gmlp_block(2 * g + 1)
            ffn_group(1, g, "11")
        return dump_and_finish(), S


def _consts():
    j = np.arange(128)[:, None]
    i = np.arange(128)[None, :]
    c = -1.0 / 16.0
    mats = [c * (j <= i), c * (j > i), 1.0 * (j <= i), c * (j >= i), c * (j < i), 1.0 * (j >= i), 1.0 * (j == i)]
    return np.ascontiguousarray(np.stack([m.astype(np.float32) for m in mats], axis=1))


def _fm(v, n):
    return np.asarray(v, np.float32).reshape(n, 128).T


def make_in_maps(x, c, ctx, c_ctx, mod_w, mod_b, ln_g, ln_b, gla_w_in, gla_w_decay, gla_b_decay,
                 gla_norm_g, gla_w_out, gm_w_in, gm_ln_g, gm_ln_b, gm_w_s, gm_b_s, gm_w_out,
                 ffn_w_gate, ffn_w_up, ffn_w_down):
    f = lambda a: np.ascontiguousarray(np.asarray(a, dtype=np.float32))
    x, c, ctx, c_ctx = f(x), f(c), f(ctx), f(c_ctx)
    cmv = _consts()
    w_in0 = f(gla_w_in[0])
    w_in1 = w_in0.copy()
    w_in1[:, 1536:1552] = w_in0[:, 1552:1568]
    w_in1[:, 1552:1568] = w_in0[:, 1536:1552]
    wd = f(gla_w_decay[0])
    bd = f(gla_b_decay[0])
    ws = f(gm_w_s[0])
    bs = f(gm_b_s[0])
    C = np.ascontiguousarray
    gmi = f(gm_w_in[0])
    fg = f(ffn_w_gate).reshape(2, 8, 128, NJ, 128).transpose(0, 3, 2, 1, 4)
    fu = f(ffn_w_up).reshape(2, 8, 128, NJ, 128).transpose(0, 3, 2, 1, 4)
    shared = {
        "cm": cmv,
        "mod_w": C(f(mod_w).reshape(2, 8, 128, 12, 512).transpose(0, 3, 2, 1, 4)),
        "w_out": C(f(gla_w_out[0]).reshape(8, 128, 8, 128).transpose(2, 1, 0, 3)),
        "gm_iu": C(gmi[:, :GW].reshape(8, 128, 12, 2, 128).transpose(2, 1, 3, 0, 4)),
        "gm_iv": C(gmi[:, GW:].reshape(8, 128, 6, 512).transpose(2, 1, 0, 3)),
        "gm_out": C(f(gm_w_out[0]).reshape(NFC, 128, 8, 128).transpose(2, 1, 0, 3)),
        "f_gu": C(np.stack([fg, fu], axis=3)),
        "f_d": C(f(ffn_w_down).reshape(2, NJ, 128, 8, 128).transpose(0, 3, 2, 1, 4)),
    }
    maps = []
    for core in range(8):
        b, half = core // 2, core % 2
        xs, cs = x[b], ctx[b]
        if half:
            xs, cs = xs[::-1], cs[::-1]
        xall = np.concatenate([xs, cs], axis=0)
        xall = np.ascontiguousarray(xall.reshape(34, 128, 8, 128).transpose(0, 3, 2, 1))
        smv = np.zeros((128, SM_N), np.float32)
        smv[:, SM_CC:SM_CC + 16] = np.stack([_fm(c[b], 8), _fm(c_ctx, 8)], axis=-1).reshape(128, 16)
        smv[:, SM_MB0:SM_MB0 + 48] = _fm(mod_b[0], 48)
        smv[:, SM_MB1:SM_MB1 + 48] = _fm(mod_b[1], 48)
        for li in range(2):
            for lj in range(2):
                o = (li * 2 + lj) * 8
                smv[:, SM_LNG + o:SM_LNG + o + 8] = _fm(ln_g[li, lj], 8)
                smv[:, SM_LNB + o:SM_LNB + o + 8] = _fm(ln_b[li, lj], 8)
        smv[:, SM_NG:SM_NG + 2] = _fm(gla_norm_g[0], 2)
        smv[:, SM_GG:SM_GG + 24] = _fm(gm_ln_g[0], 24)
        smv[:, SM_GB:SM_GB + 24] = _fm(gm_ln_b[0], 24)
        wdd = wd[::-1] if half else wd
        bdd = bd[::-1] if half else bd
        wss = ws[:, ::-1, ::-1] if half else ws
        bss = bs[::-1] if half else bs
        m = dict(shared)
        m.update({
            "xT": xall, "sm": smv,
            "w_in": np.ascontiguousarray((w_in1 if half else w_in0).reshape(8, 128, 3104).transpose(1, 0, 2)),
            "wdec": np.ascontiguousarray(wdd.transpose(1, 0, 2)),
            "bdec": np.ascontiguousarray(bdd.reshape(1, 2, 512)),
            "gm_ws": np.ascontiguousarray(wss.transpose(2, 0, 1)),
            "gm_bs": np.ascontiguousarray(np.broadcast_to(bss.T.reshape(1, 512), (128, 512))),
        })
        maps.append(m)
    return maps


def assemble(results):
    out = np.empty((4, 4096, D), np.float32)
    for core in range(8):
        b, half = core // 2, core % 2
        o = results[core]["outT"].transpose(0, 3, 2, 1).reshape(NTOK, D)
        if half:
            out[b, 2048:] = o[::-1]
        else:
            out[b, :2048] = o
    return out


_NC = {}


def kernel(**inputs):
    if "nc" not in _NC:
        _NC["nc"] = build_program()[0]
    maps = make_in_maps(**inputs)
    res = run_bass_kernel_spmd(_NC["nc"], maps, core_ids=list(range(8)))
    return assemble(res.results)
```
